# Optimizing a Trainium2 kernel written in Bass

```python
import math
import numpy as np
import jax
import jax.numpy as jnp
from jax import lax

D_MODEL = 2048
BATCH = 4
SEQ = 2048
DEPTH = 1
DEC_BATCH = 2
DEC_SEQ = 8192
PAST_LEN = 128

GRID_W = 64
WIN_H = 8
WIN_W = 16
H_A = 8
DH_A = 128
H_M = 4
DH_M = 256
W_A = H_A * DH_A
W_M = H_M * DH_M
D_MIX = W_A + W_M
CHUNK = 64
D_FF = 5632
PLE_DIM = 256
EPS = 1e-6
IN_SIZES = (W_A, W_A, W_A, W_M, W_M, W_M, W_M, 2 * H_M, 2 * H_M)
D_IN = sum(IN_SIZES)

kernel_name = "hymba_natten_mlstm_macaron_encoder"


def rms_norm(x, g):
    xf = x.astype(jnp.float32)
    y = xf * lax.rsqrt(jnp.mean(xf * xf, axis=-1, keepdims=True) + EPS)
    return (y * g).astype(x.dtype)


def swiglu(h, wg, wu, wd):
    return (jax.nn.silu(h @ wg) * (h @ wu)) @ wd


def neighbourhood_attention(q, k, v, rpb):
    B, S, H, Dh = q.shape
    R = S // GRID_W
    KH = min(WIN_H, R)
    r = jnp.arange(R)
    rs = jnp.clip(r - KH // 2, 0, R - KH)
    rows = rs[:, None] + jnp.arange(KH)[None, :]
    c = jnp.arange(GRID_W)
    cs = jnp.clip(c - WIN_W // 2, 0, GRID_W - WIN_W)
    colmask = (c[None, :] >= cs[:, None]) & (c[None, :] < cs[:, None] + WIN_W)

    qg = q.reshape(B, R, GRID_W, H, Dh)
    kb = k.reshape(B, R, GRID_W, H, Dh)[:, rows]
    vb = v.reshape(B, R, GRID_W, H, Dh)[:, rows]

    s = jnp.einsum('brqhd,brikhd->brhqik', qg, kb).astype(jnp.float32) * (Dh ** -0.5)
    dr = rows - r[:, None] + (WIN_H - 1)
    dc = jnp.clip(c[None, :] - c[:, None], -(WIN_W - 1), WIN_W - 1) + (WIN_W - 1)
    bias = rpb[:, dr[:, None, :, None], dc[None, :, None, :]]
    s = s + jnp.transpose(bias, (1, 0, 2, 3, 4))[None].astype(jnp.float32)
    s = jnp.where(colmask[:, None, :], s, -jnp.inf)
    p = jax.nn.softmax(s.reshape(B, R, H, GRID_W, KH * GRID_W), axis=-1)
    p = p.reshape(B, R, H, GRID_W, KH, GRID_W).astype(v.dtype)
    out = jnp.einsum('brhqik,brikhd->brqhd', p, vb)
    return out.reshape(B, S, H * Dh)


def mlstm_chunkwise(q, k, v, ig, lf):
    B, H, S, Dh = q.shape
    L = CHUNK
    N = S // L
    q = q.reshape(B, H, N, L, Dh)
    k = k.reshape(B, H, N, L, Dh)
    v = v.reshape(B, H, N, L, Dh)
    ig = ig.reshape(B, H, N, L)
    b = jnp.cumsum(lf.reshape(B, H, N, L), axis=-1)
    a = b[..., -1]

    g = a[..., None] - b + ig
    m_loc = jnp.max(g, axis=-1)
    w = jnp.exp(g - m_loc[..., None])
    C_loc = jnp.einsum('bhns,bhnsd,bhnse->bhnde', w, v, k)
    n_loc = jnp.einsum('bhns,bhnse->bhne', w, k)

    def step(carry, xs):
        C, n, m = carry
        a_c, m_l, C_l, n_l = xs
        m_new = jnp.maximum(a_c + m, m_l)
        s_prev = jnp.exp(a_c + m - m_new)
        s_loc = jnp.exp(m_l - m_new)
        C_new = s_prev[..., None, None] * C + s_loc[..., None, None] * C_l
        n_new = s_prev[..., None] * n + s_loc[..., None] * n_l
        return (C_new, n_new, m_new), (C, n, m)

    init = (jnp.zeros((B, H, Dh, Dh), jnp.float32), jnp.zeros((B, H, Dh), jnp.float32),
            jnp.zeros((B, H), jnp.float32))
    xs = (jnp.moveaxis(a, 2, 0), jnp.moveaxis(m_loc, 2, 0), jnp.moveaxis(C_loc, 2, 0), jnp.moveaxis(n_loc, 2, 0))
    _, (C_prev, n_prev, m_prev) = lax.scan(step, init, xs)
    C_prev = jnp.moveaxis(C_prev, 0, 2)
    n_prev = jnp.moveaxis(n_prev, 0, 2)
    m_prev = jnp.moveaxis(m_prev, 0, 2)

    dmat = b[..., :, None] - b[..., None, :] + ig[..., None, :]
    tri = jnp.tril(jnp.ones((L, L), dtype=bool))
    dmat = jnp.where(tri, dmat, -jnp.inf)
    inter = b + m_prev[..., None]
    m_t = jnp.maximum(inter, jnp.max(dmat, axis=-1))
    sqk = jnp.einsum('bhntd,bhnsd->bhnts', q, k) * jnp.exp(dmat - m_t[..., None])
    sc = jnp.exp(inter - m_t)
    num = sc[..., None] * jnp.einsum('bhnde,bhnte->bhntd', C_prev, q) + jnp.einsum('bhnts,bhnsd->bhntd', sqk, v)
    den = sc * jnp.einsum('bhne,bhnte->bhnt', n_prev, q) + jnp.sum(sqk, axis=-1)
    h = num / jnp.maximum(jnp.abs(den), jnp.exp(-m_t))[..., None]
    return h.reshape(B, H, S, Dh)


def mlstm_bidirectional(q, k, v, ig_f, lf_f, ig_b, lf_b):
    hf = mlstm_chunkwise(q, k, v, ig_f, lf_f)
    fl = lambda t: jnp.flip(t, axis=2)
    hb = fl(mlstm_chunkwise(fl(q), fl(k), fl(v), fl(ig_b), fl(lf_b)))
    return hf + hb


def encoder_layer(x, pe, g_ffn1, w_ffn1_gate, w_ffn1_up, w_ffn1_down, g_mix, w_in, b_igate, b_fgate,
                  g_qn, g_kn, rpb, g_mh, w_out, g_ffn2, w_ffn2_gate, w_ffn2_up, w_ffn2_down,
                  g_ple, w_ple_gate, w_ple_proj):
    B, S, _ = x.shape
    x = x + 0.5 * swiglu(rms_norm(x, g_ffn1), w_ffn1_gate, w_ffn1_up, w_ffn1_down)

    h = rms_norm(x, g_mix)
    u = h @ w_in
    offs = np.cumsum(IN_SIZES)[:-1].tolist()
    qa, ka, va, qm, km, vm, og, gi, gf = jnp.split(u, offs, axis=-1)

    qa = rms_norm(qa.reshape(B, S, H_A, DH_A), g_qn)
    ka = rms_norm(ka.reshape(B, S, H_A, DH_A), g_kn)
    va = va.reshape(B, S, H_A, DH_A)
    ya = neighbourhood_attention(qa, ka, va, rpb)

    to_bhsd = lambda t: jnp.transpose(t.reshape(B, S, H_M, DH_M), (0, 2, 1, 3)).astype(jnp.float32)
    qm_ = to_bhsd(qm)
    km_ = to_bhsd(km) * (DH_M ** -0.5)
    vm_ = to_bhsd(vm)
    gi = jnp.transpose((gi.reshape(B, S, 2, H_M) + b_igate).astype(jnp.float32), (2, 0, 3, 1))
    lf = jax.nn.log_sigmoid(jnp.transpose((gf.reshape(B, S, 2, H_M) + b_fgate).astype(jnp.float32), (2, 0, 3, 1)))
    hm = mlstm_bidirectional(qm_, km_, vm_, gi[0], lf[0], gi[1], lf[1])
    hm = jnp.transpose(hm, (0, 2, 1, 3))
    hm = rms_norm(hm, g_mh.reshape(H_M, DH_M)).reshape(B, S, W_M)
    ym = (jax.nn.sigmoid(og.astype(jnp.float32)) * hm).astype(x.dtype)

    x = x + jnp.concatenate([ya.astype(x.dtype), ym], axis=-1) @ w_out

    x = x + 0.5 * swiglu(rms_norm(x, g_ffn2), w_ffn2_gate, w_ffn2_up, w_ffn2_down)

    gate = jax.nn.sigmoid(rms_norm(x, g_ple) @ w_ple_gate)
    x = x + gate * (pe @ w_ple_proj)
    return x


def run_trunk(x, p, g_ffn1, w_ffn1_gate, w_ffn1_up, w_ffn1_down, g_mix, w_in, b_igate, b_fgate,
              g_qn, g_kn, rpb, g_mh, w_out, g_ffn2, w_ffn2_gate, w_ffn2_up, w_ffn2_down,
              g_ple, w_ple_gate, w_ple_proj):
    for i in range(DEPTH):
        x = encoder_layer(x, p[i], g_ffn1[i], w_ffn1_gate[i], w_ffn1_up[i], w_ffn1_down[i], g_mix[i], w_in[i],
                          b_igate[i], b_fgate[i], g_qn[i], g_kn[i], rpb[i], g_mh[i], w_out[i], g_ffn2[i],
                          w_ffn2_gate[i], w_ffn2_up[i], w_ffn2_down[i], g_ple[i], w_ple_gate[i], w_ple_proj[i])
    return x


def setup_inputs(seed: int = 0) -> dict:
    key = jax.random.key(seed)
    ks = jax.random.split(key, 24)
    nrm = lambda k, shape, scale: jax.random.normal(k, shape, jnp.float32) * scale
    gain = lambda k, shape: 1.0 + 0.1 * jax.random.normal(k, shape, jnp.float32)
    L = DEPTH
    return {
        "x_prompt": nrm(ks[0], (BATCH, SEQ, D_MODEL), 1.0),
        "x_sample": nrm(ks[1], (DEC_BATCH, DEC_SEQ, D_MODEL), 1.0),
        "p_prompt": nrm(ks[2], (DEPTH, BATCH, SEQ, PLE_DIM), 1.0),
        "p_sample": nrm(ks[3], (DEPTH, DEC_BATCH, DEC_SEQ, PLE_DIM), 1.0),
        "g_ffn1": gain(ks[4], (L, D_MODEL)),
        "w_ffn1_gate": nrm(ks[5], (L, D_MODEL, D_FF), D_MODEL ** -0.5),
        "w_ffn1_up": nrm(ks[6], (L, D_MODEL, D_FF), D_MODEL ** -0.5),
        "w_ffn1_down": nrm(ks[7], (L, D_FF, D_MODEL), D_FF ** -0.5),
        "g_mix": gain(ks[8], (L, D_MODEL)),
        "w_in": nrm(ks[9], (L, D_MODEL, D_IN), D_MODEL ** -0.5),
        "b_igate": nrm(ks[10], (L, 2, H_M), 0.1),
        "b_fgate": 3.0 + nrm(ks[11], (L, 2, H_M), 0.5),
        "g_qn": gain(ks[12], (L, DH_A)),
        "g_kn": gain(ks[13], (L, DH_A)),
        "rpb": nrm(ks[14], (L, H_A, 2 * WIN_H - 1, 2 * WIN_W - 1), 0.5),
        "g_mh": gain(ks[15], (L, W_M)),
        "w_out": nrm(ks[16], (L, D_MIX, D_MODEL), D_MIX ** -0.5),
        "g_ffn2": gain(ks[17], (L, D_MODEL)),
        "w_ffn2_gate": nrm(ks[18], (L, D_MODEL, D_FF), D_MODEL ** -0.5),
        "w_ffn2_up": nrm(ks[19], (L, D_MODEL, D_FF), D_MODEL ** -0.5),
        "w_ffn2_down": nrm(ks[20], (L, D_FF, D_MODEL), D_FF ** -0.5),
        "g_ple": gain(ks[21], (L, D_MODEL)),
        "w_ple_gate": nrm(ks[22], (L, D_MODEL, D_MODEL), D_MODEL ** -0.5),
        "w_ple_proj": nrm(ks[23], (L, PLE_DIM, D_MODEL), PLE_DIM ** -0.5),
    }


def reference(x_prompt, x_sample, p_prompt, p_sample, g_ffn1, w_ffn1_gate, w_ffn1_up, w_ffn1_down, g_mix, w_in,
              b_igate, b_fgate, g_qn, g_kn, rpb, g_mh, w_out, g_ffn2, w_ffn2_gate, w_ffn2_up, w_ffn2_down,
              g_ple, w_ple_gate, w_ple_proj):
    y_prompt = run_trunk(x_prompt, p_prompt, g_ffn1, w_ffn1_gate, w_ffn1_up, w_ffn1_down, g_mix, w_in,
                         b_igate, b_fgate, g_qn, g_kn, rpb, g_mh, w_out, g_ffn2, w_ffn2_gate, w_ffn2_up,
                         w_ffn2_down, g_ple, w_ple_gate, w_ple_proj)
    y_sample = run_trunk(x_sample, p_sample, g_ffn1, w_ffn1_gate, w_ffn1_up, w_ffn1_down, g_mix, w_in,
                         b_igate, b_fgate, g_qn, g_kn, rpb, g_mh, w_out, g_ffn2, w_ffn2_gate, w_ffn2_up,
                         w_ffn2_down, g_ple, w_ple_gate, w_ple_proj)
    return (y_prompt, y_sample)
```

```python
import os
from contextlib import ExitStack

import numpy as np
import concourse.bass as bass
import concourse.mybir as mybir
from concourse.bass_utils import run_bass_kernel_spmd

F32 = mybir.dt.float32
BF16 = mybir.dt.bfloat16
AF = mybir.ActivationFunctionType
OP = mybir.AluOpType

D = 2048
DC = 16
DFF = 5632
FC = 44
DIN = 7184
T = 512
EPS = 1e-6
NEG = -1000.0


class Buf:
    __slots__ = ("name", "w", "r", "dsem", "dcnt")

    def __init__(self, name):
        self.name = name
        self.w = {}
        self.r = {}
        self.dsem = None
        self.dcnt = 0


class Eng:
    def __init__(self, name, e, sem, inorder=False):
        self.name = name
        self.e = e
        self.sem = sem
        self.cnt = 0
        self.waited = {}
        self.inorder = inorder


def _merge(d, sem, val):
    k = id(sem)
    if k not in d or d[k][1] < val:
        d[k] = (sem, val)


class K:
    def __init__(self, nc, es):
        self.nc = nc
        self.es = es
        self.n_sem = 0
        self.pe = Eng("pe", nc.tensor, self.sem("pe"), inorder=True)
        self.act = Eng("act", nc.scalar, self.sem("act"))
        self.dve = Eng("dve", nc.vector, self.sem("dve"))
        self.pool = Eng("pool", nc.gpsimd, self.sem("pool"))
        self.sp = Eng("sp", nc.sync, self.sem("sp"), inorder=True)
        self.engs = [self.pe, self.act, self.dve, self.pool, self.sp]
        self.dbufs = []
        self.sem_pool = {"hw": [], "sw": []}
        self.n_inst = 0

    def sem(self, name):
        self.n_sem += 1
        return self.es.enter_context(self.nc.semaphore(name))

    def _wait(self, eng, reads, writes):
        deps = {}
        for b in reads:
            for s, v in b.w.values():
                _merge(deps, s, v)
        for b in writes:
            for s, v in b.w.values():
                _merge(deps, s, v)
            for s, v in b.r.values():
                _merge(deps, s, v)
        for s, v in deps.values():
            if s is eng.sem and eng.inorder:
                continue
            if eng.waited.get(id(s), 0) < v:
                eng.e.wait_ge(s, v)
                eng.waited[id(s)] = v
                self.n_inst += 1
                eng.ni = getattr(eng, 'ni', 0) + 1

    def op(self, eng, fns, reads=(), writes=()):
        self._wait(eng, reads, writes)
        if not isinstance(fns, (list, tuple)):
            fns = [fns]
        inst = None
        for f in fns:
            inst = f()
            self.n_inst += 1
            eng.ni = getattr(eng, 'ni', 0) + 1
        inst.then_inc(eng.sem, 1)
        eng.cnt += 1
        assert eng.cnt < 60000, eng.name
        for b in reads:
            _merge(b.r, eng.sem, eng.cnt)
        for b in writes:
            b.w = {id(eng.sem): (eng.sem, eng.cnt)}
            b.r = {}

    def dma(self, q, pairs, reads=(), writes=(), owner=None):
        kind = "sw" if q is self.pool else "hw"
        ent = owner.dsem.get(kind) if isinstance(owner.dsem, dict) else None
        if ent is None:
            if not isinstance(owner.dsem, dict):
                owner.dsem = {}
            if self.sem_pool[kind]:
                ent = list(self.sem_pool[kind].pop())
            else:
                ent = [self.sem(f"d{self.n_sem}_{kind}_" + owner.name), 0]
            owner.dsem[kind] = ent
            self.dbufs.append((owner, kind))
        self._wait(q, reads, writes)
        for out, in_ in pairs:
            q.e.dma_start(out=out, in_=in_).then_inc(ent[0], 16)
            ent[1] += 16
            self.n_inst += 1
            q.ni = getattr(q, 'ni', 0) + 1
            q.nd = getattr(q, 'nd', 0) + 1
        assert ent[1] < 60000, owner.name
        for b in reads:
            _merge(b.r, ent[0], ent[1])
        for b in writes:
            b.w = {id(ent[0]): (ent[0], ent[1])}
            b.r = {}

    def barrier(self):
        for e in self.engs:
            for o in self.engs:
                if o is e or o.cnt == 0:
                    continue
                if e.waited.get(id(o.sem), 0) < o.cnt:
                    e.e.wait_ge(o.sem, o.cnt)
                    e.waited[id(o.sem)] = o.cnt
            for b, kind in self.dbufs:
                s, c = b.dsem[kind]
                if c and e.waited.get(id(s), 0) < c:
                    e.e.wait_ge(s, c)
                    e.waited[id(s)] = c

    def end_phase(self, keep=()):
        self.barrier()
        kept = []
        for b, kind in self.dbufs:
            if b in keep:
                kept.append((b, kind))
            else:
                self.sem_pool[kind].append(tuple(b.dsem[kind]))
                del b.dsem[kind]
        self.dbufs = kept

    def sb(self, name, shape, dt):
        return self.es.enter_context(self.nc.sbuf_tensor(name, shape, dt))


def build(NTOK, debug=(), phases="0ABMC"):
    NT = NTOK // T
    NCH = NTOK // 128
    nc = bass.Bass("TRN2", target_bir_lowering=False)
    es = ExitStack()
    k = K(nc, es)

    def din(name, shape, dt=F32):
        return nc.dram_tensor(name, shape, dt, kind="ExternalInput").ap()

    def dscr(name, shape, dt):
        kind = "ExternalOutput" if name in debug else "Internal"
        return nc.dram_tensor(name, shape, dt, kind=kind).ap()

    xs = din("xs", [NTOK, D])
    pes = din("pes", [NTOK, 256])
    w1g = din("w1g", [D, DFF]); w1u = din("w1u", [D, DFF]); w1d = din("w1d", [DFF, D])
    w2g = din("w2g", [D, DFF]); w2u = din("w2u", [D, DFF]); w2d = din("w2d", [DFF, D])
    win = din("win", [D, DIN]); wout = din("wout", [D, D])
    wpg = din("wpg", [D, D]); wpp = din("wpp", [256, D])
    gv = din("gv", [128, 4 * DC])
    gsm = din("gsm", [128, 16])
    gbias = din("gbias", [128, 16])
    cst = din("cst", [128, 5 * 128])
    rpbg = din("rpbg", [8, 128, 9, 128])
    cmI = din("cmI", [128, 9, 128]); cmE = din("cmE", [128, 9, 128])
    NE = max(4, (NCH // 16) * 4)
    rb = din("rb", [128, NE, 9, 2])
    keep = din("keep", [128, 2, NCH])
    y = nc.dram_tensor("y", [NTOK, D], F32, kind="ExternalOutput").ap()

    WGU1 = dscr("WGU1", [22, 128, 2, 16, 256], BF16)
    WD1 = dscr("WD1", [16, 128, FC, 128], BF16)
    WGU2 = dscr("WGU2", [22, 128, 2, 16, 256], BF16)
    WD2 = dscr("WD2", [16, 128, FC, 128], BF16)
    WIN = dscr("WIN", [14, 128, 16, 512], BF16)
    WING = dscr("WING", [128, 16, 16], BF16)
    WOUT = dscr("WOUT", [4, 128, 16, 512], BF16)
    WPG = dscr("WPG", [4, 128, 16, 512], BF16)
    WPP = dscr("WPP", [128, 2, D], BF16)
    X1T = dscr("X1T", [128, DC, NTOK], F32)
    QAT = dscr("QAT", [8, 128, NTOK], BF16); KAT = dscr("KAT", [8, 128, NTOK], BF16)
    VA = dscr("VA", [NTOK, 1024], BF16)
    QMT = dscr("QMT", [8, 128, NTOK], BF16); KMT = dscr("KMT", [8, 128, NTOK], BF16)
    KM = dscr("KM", [NTOK, 1024], BF16); VM = dscr("VM", [NTOK, 1024], BF16)
    OGT = dscr("OGT", [8, 128, NTOK], BF16)
    GT = dscr("GT", [NTOK, 16], F32)
    HB = dscr("HB", [8, 128, NTOK], F32)
    YT = dscr("YT", [16, 128, NTOK], BF16)

    cst_sb = k.sb("cst_sb", [128, 5 * 128], F32)
    ident = cst_sb[:, 0:128]
    triF = cst_sb[:, 128:256]
    triB = cst_sb[:, 256:384]
    ones_f = cst_sb[:, 384:512]
    gv_sb = k.sb("gv_sb", [128, 4 * DC], F32)
    gsm_sb = k.sb("gsm_sb", [128, 16], F32)
    ones_bf = k.sb("ones_bf", [128, 128], BF16)
    b_cst = Buf("cst")
    k.dma(k.sp, [(cst_sb[:, :], cst[:, :]), (gv_sb[:, :], gv[:, :]), (gsm_sb[:, :], gsm[:, :])],
          writes=[b_cst], owner=b_cst)
    b_ones = Buf("ones")
    k.op(k.dve, lambda: nc.vector.tensor_copy(out=ones_bf[:, :], in_=cst_sb[:, 384:512]),
         reads=[b_cst], writes=[b_ones])

    psall = es.enter_context(nc.psum_tensor("psall", [128, 4096], F32))
    ps = [psall[:, i * 512:(i + 1) * 512] for i in range(8)]
    pb = [Buf(f"ps{i}") for i in range(8)]

    if "0" in phases:
      with ExitStack() as es0:
        def sb0(name, shape, dt):
            return es0.enter_context(nc.sbuf_tensor(name, shape, dt))
        NST = 3
        st_in = [sb0(f"c_in{i}", [128, DIN], F32) for i in range(NST)]
        st_out = [sb0(f"c_out{i}", [128, DIN], BF16) for i in range(NST)]
        bi = [Buf(f"c_in{i}") for i in range(NST)]
        bo = [Buf(f"c_out{i}") for i in range(NST)]
        state = {"i": 0}
        cast_engs = [k.act, k.dve, k.pool]

        def cast_rows(src_pairs_fn, ncols, dst_pairs_fn):
            i = state["i"] % NST
            e = cast_engs[state["i"] % 3]
            state["i"] += 1
            k.dma(k.sp, src_pairs_fn(st_in[i]), writes=[bi[i]], owner=bi[i])
            if e is k.act:
                fn = lambda: nc.scalar.copy(out=st_out[i][:, 0:ncols], in_=st_in[i][:, 0:ncols])
            elif e is k.dve:
                fn = lambda: nc.vector.tensor_copy(out=st_out[i][:, 0:ncols], in_=st_in[i][:, 0:ncols])
            else:
                fn = lambda: nc.gpsimd.tensor_copy(out=st_out[i][:, 0:ncols], in_=st_in[i][:, 0:ncols])
            k.op(e, fn, reads=[bi[i]], writes=[bo[i]])
            k.dma(k.pool, dst_pairs_fn(st_out[i]), reads=[bo[i]], owner=bo[i])

        def cast_gu(wg, wu, WGU):
            for gu, w in enumerate((wg, wu)):
                for kc in range(16):
                    cast_rows(lambda si, w=w, kc=kc: [(si[:, 0:DFF], w[kc * 128:(kc + 1) * 128, :])], DFF,
                              lambda so, gu=gu, kc=kc: [(
                                  WGU[a * 11:(a + 1) * 11, :, gu, kc, :].rearrange("n p f -> p n f"),
                                  so[:, a * 2816:(a + 1) * 2816].rearrange("p (n f) -> p n f", f=256)) for a in range(2)])

        def cast_d(wd, WD):
            for fc2 in range(FC // 2):
                def dst(so, fc2=fc2):
                    pairs = []
                    for a in range(2):
                        fc = fc2 * 2 + a
                        pairs.append((WD[:, :, fc, :].rearrange("c p d -> p c d"),
                                      so[:, a * 2048:(a + 1) * 2048].rearrange("p (c d) -> p c d", d=128)))
                    return pairs
                cast_rows(lambda si, fc2=fc2: [(si[:, 0:4096].rearrange("p (a d) -> p a d", a=2),
                                                wd[fc2 * 256:(fc2 + 1) * 256, :].rearrange("(a p) d -> p a d", p=128))],
                          4096, dst)

        def cast_panels512(w, WS, ncols_total):
            for kc in range(16):
                cast_rows(lambda si, kc=kc: [(si[:, 0:ncols_total], w[kc * 128:(kc + 1) * 128, 0:ncols_total])],
                          ncols_total,
                          lambda so, kc=kc: [(
                              WS[:, :, kc, :].rearrange("n p f -> p n f"),
                              so[:, 0:ncols_total].rearrange("p (n f) -> p n f", f=512))])

        cast_gu(w1g, w1u, WGU1)
        cast_d(w1d, WD1)
        cast_panels512(win, WIN, 7168)
        for kc in range(16):
            cast_rows(lambda si, kc=kc: [(si[:, 0:16], win[kc * 128:(kc + 1) * 128, 7168:7184])], 16,
                      lambda so, kc=kc: [(WING[:, kc, :], so[:, 0:16])])
        cast_panels512(wout, WOUT, 2048)
        cast_gu(w2g, w2u, WGU2)
        cast_d(w2d, WD2)
        cast_panels512(wpg, WPG, 2048)
        for k2 in range(2):
            cast_rows(lambda si, k2=k2: [(si[:, 0:2048], wpp[k2 * 128:(k2 + 1) * 128, :])], 2048,
                      lambda so, k2=k2: [(WPP[:, k2, :], so[:, 0:2048])])
        k.end_phase(keep=(b_cst,))

    class TB:
        pass

    def alloc_tiles(stk, pfx):
        tb = TB()
        sbt = lambda name, shape, dt: stk.enter_context(nc.sbuf_tensor(pfx + name, shape, dt))
        tb.xT = sbt("xT", [128, DC, T], F32)
        tb.xb = [Buf(f"xT{dc}") for dc in range(DC)]
        tb.hT = sbt("hT", [128, DC, T], BF16)
        tb.hb = Buf("hT")
        tb.act = sbt("act", [128, FC, T], BF16)
        tb.actb = [Buf(f"act{fc}") for fc in range(FC)]
        tb.sq = [sbt(f"sq{j}", [128, T], BF16) for j in range(2)]
        tb.sqb = [Buf(f"sq{j}") for j in range(2)]
        tb.sg = [sbt(f"sg{j}", [128, T], F32) for j in range(2)]
        tb.sgb = [Buf(f"sg{j}") for j in range(2)]
        tb.rstd = sbt("rstd", [128, T], F32)
        tb.rstdb = Buf("rstd")
        tb.xin = [sbt(f"xin{i}", [128, D], F32) for i in range(2)]
        tb.xinb = [Buf(f"xin{i}") for i in range(2)]
        tb.wslot = [sbt(f"wslot{i}", [128, 8192], BF16) for i in range(NSLOT)]
        tb.wsb = [Buf(f"wslot{i}") for i in range(NSLOT)]
        tb.wi = 0
        tb.stg = [sbt(f"stg{i}", [128, T], BF16) for i in range(4)]
        tb.stgb = [Buf(f"stg{i}") for i in range(4)]
        tb.stgf = [sbt(f"stgf{i}", [128, T], F32) for i in range(2)]
        tb.stgfb = [Buf(f"stgf{i}") for i in range(2)]
        tb.si = 0
        tb.fi = 0
        return tb

    NSLOT = 4

    def wload(tb, pairs_fn):
        i = tb.wi % NSLOT
        tb.wi += 1
        k.dma(k.sp, pairs_fn(tb.wslot[i]), writes=[tb.wsb[i]], owner=tb.wsb[i])
        return tb.wslot[i], tb.wsb[i]

    def rmsnorm_T(tb, gcol):
        xT, xb, hT, sq, sqb, rstd, rstdb = tb.xT, tb.xb, tb.hT, tb.sq, tb.sqb, tb.rstd, tb.rstdb
        for dc in range(DC):
            j = dc % 2
            k.op(k.act, lambda dc=dc, j=j: nc.scalar.activation(out=sq[j][:, :], in_=xT[:, dc, :], func=AF.Square),
                 reads=[xb[dc]], writes=[sqb[j]])
            k.op(k.pe, lambda dc=dc, j=j: nc.tensor.matmul(out=ps[0], lhsT=ones_bf[:, :], rhs=sq[j][:, :],
                                                          start=(dc == 0), stop=(dc == DC - 1)),
                 reads=[sqb[j], b_ones], writes=[pb[0]])
        k.op(k.act, lambda: nc.scalar.activation(out=rstd[:, :], in_=ps[0], func=AF.Sqrt,
                                                 scale=1.0 / D, bias=eps_sb[:, 0:1]),
             reads=[pb[0], b_eps], writes=[rstdb])
        k.op(k.dve, lambda: nc.vector.reciprocal(out=rstd[:, :], in_=rstd[:, :]), reads=[rstdb], writes=[rstdb])
        fns = []
        for dc in range(DC):
            fns.append(lambda dc=dc: nc.vector.scalar_tensor_tensor(
                out=hT[:, dc, :], in0=xT[:, dc, :], scalar=gv_sb[:, gcol + dc:gcol + dc + 1], in1=rstd[:, :],
                op0=OP.mult, op1=OP.mult))
        k.op(k.dve, fns, reads=list(xb) + [rstdb, b_cst], writes=[tb.hb])

    eps_sb = k.sb("eps_sb", [128, 4], F32)
    b_eps = Buf("eps")
    k.op(k.dve, [lambda: nc.vector.memset(eps_sb[:, 0:1], EPS), lambda: nc.vector.memset(eps_sb[:, 1:2], 128 * EPS),
                 lambda: nc.vector.memset(eps_sb[:, 2:3], 1.0)], writes=[b_eps])

    def ffn(tb, WGU, WD):
        hT, hb, act, actb, xT, xb, sg, sgb = tb.hT, tb.hb, tb.act, tb.actb, tb.xT, tb.xb, tb.sg, tb.sgb
        def load_gu(n):
            return wload(tb, lambda s, n=n: [(s[:, :], WGU[n].rearrange("p a c f -> p (a c f)"))])
        nxt = load_gu(0)
        for n in range(22):
            cur = nxt
            if n + 1 < 22:
                nxt = load_gu(n + 1)
            wt, wb = cur
            wv = wt[:, :].rearrange("p (a c f) -> p a c f", a=2, c=16)
            for f2 in range(2):
                fc = n * 2 + f2
                gbank, ubank = 2 + (fc % 2) * 2, 3 + (fc % 2) * 2
                for gu, bank in ((0, gbank), (1, ubank)):
                    fns = [lambda kc=kc, gu=gu, bank=bank, f2=f2: nc.tensor.matmul(
                        out=ps[bank], lhsT=wv[:, gu, kc, f2 * 128:(f2 + 1) * 128], rhs=hT[:, kc, :],
                        start=(kc == 0), stop=(kc == 15)) for kc in range(16)]
                    k.op(k.pe, fns, reads=[wb, hb], writes=[pb[bank]])
                j = fc % 2
                k.op(k.act, lambda j=j, gbank=gbank: nc.scalar.activation(out=sg[j][:, :], in_=ps[gbank], func=AF.Silu),
                     reads=[pb[gbank]], writes=[sgb[j]])
                k.op(k.dve, lambda j=j, ubank=ubank, fc=fc: nc.vector.tensor_tensor(
                    out=act[:, fc, :], in0=ps[ubank], in1=sg[j][:, :], op=OP.mult),
                     reads=[pb[ubank], sgb[j]], writes=[actb[fc]])
        def load_d(dc):
            return wload(tb, lambda s, dc=dc: [(s[:, 0:FC * 128], WD[dc].rearrange("p c d -> p (c d)"))])
        nxt = load_d(0)
        for dc in range(DC):
            cur = nxt
            if dc + 1 < DC:
                nxt = load_d(dc + 1)
            wt, wb = cur
            bank = 6 + dc % 2
            fns = [lambda fc=fc, bank=bank, wt=wt: nc.tensor.matmul(
                out=ps[bank], lhsT=wt[:, fc * 128:(fc + 1) * 128], rhs=act[:, fc, :],
                start=(fc == 0), stop=(fc == FC - 1)) for fc in range(FC)]
            k.op(k.pe, fns, reads=[wb] + actb, writes=[pb[bank]])
            k.op(k.dve, lambda dc=dc, bank=bank: nc.vector.scalar_tensor_tensor(
                out=xT[:, dc, :], in0=ps[bank], scalar=0.5, in1=xT[:, dc, :], op0=OP.mult, op1=OP.add),
                 reads=[pb[bank]], writes=[xb[dc]])

    def load_xT_from_tokmajor(tb, src, t):
        xin, xinb, xT, xb = tb.xin, tb.xinb, tb.xT, tb.xb
        for s in range(4):
            i = s % 2
            r0 = t * T + s * 128
            k.dma(k.sp, [(xin[i][:, :], src[r0:r0 + 128, :])], writes=[xinb[i]], owner=xinb[i])
            for q in range(4):
                bank = q % 2
                fns = [lambda dc=dc, j=j, i=i, bank=bank: nc.tensor.transpose(
                    out=ps[bank][:, j * 128:(j + 1) * 128], in_=xin[i][:, dc * 128:(dc + 1) * 128], identity=ident)
                    for j, dc in enumerate(range(q * 4, q * 4 + 4))]
                k.op(k.pe, fns, reads=[xinb[i], b_cst], writes=[pb[bank]])
                k.op(k.act, lambda q=q, s=s, bank=bank: nc.scalar.copy(
                    out=xT[:, q * 4:q * 4 + 4, s * 128:(s + 1) * 128],
                    in_=ps[bank].rearrange("p (j t) -> p j t", j=4)),
                     reads=[pb[bank]], writes=[xb[dc] for dc in range(q * 4, q * 4 + 4)])

    if "A" in phases:
      with ExitStack() as esA:
        tb = alloc_tiles(esA, "A_")
        hT, hb, sq, sqb, rstd, rstdb = tb.hT, tb.hb, tb.sq, tb.sqb, tb.rstd, tb.rstdb
        stg, stgb, stgf, stgfb = tb.stg, tb.stgb, tb.stgf, tb.stgfb
        for t in range(NT):
            tk = slice(t * T, (t + 1) * T)
            load_xT_from_tokmajor(tb, xs, t)
            rmsnorm_T(tb, 0)
            ffn(tb, WGU1, WD1)
            k.dma(k.pool, [(X1T[:, :, tk], tb.xT[:, :, :])], reads=tb.xb, owner=tb.xb[0])
            rmsnorm_T(tb, DC)
            FM = {0: "qa", 1: "qa", 2: "ka", 3: "ka", 6: "qm", 7: "qm", 8: "km", 9: "km", 12: "og", 13: "og"}
            TM = {4: "va", 5: "va", 8: "km", 9: "km", 10: "vm", 11: "vm"}
            for n in range(14):
                wt, wb = wload(tb, lambda s, n=n: [(s[:, :], WIN[n].rearrange("p c f -> p (c f)"))])
                wv = wt[:, :].rearrange("p (c f) -> p c f", c=16)
                if n in FM:
                    kind = FM[n]
                    for j in range(4):
                        oc = n * 4 + j
                        bank = 6 + oc % 2
                        fns = [lambda kc=kc, j=j, bank=bank: nc.tensor.matmul(
                            out=ps[bank], lhsT=wv[:, kc, j * 128:(j + 1) * 128], rhs=hT[:, kc, :],
                            start=(kc == 0), stop=(kc == 15)) for kc in range(16)]
                        k.op(k.pe, fns, reads=[wb, hb], writes=[pb[bank]])
                        si = tb.si % 4
                        tb.si += 1
                        if kind in ("qa", "ka"):
                            fi = tb.fi % 2
                            tb.fi += 1
                            k.op(k.act, lambda fi=fi, bank=bank: nc.scalar.copy(out=stgf[fi][:, :], in_=ps[bank]),
                                 reads=[pb[bank]], writes=[stgfb[fi]])
                            k.op(k.act, lambda bank=bank: nc.scalar.activation(out=sq[0][:, :], in_=ps[bank], func=AF.Square),
                                 reads=[pb[bank]], writes=[sqb[0]])
                            k.op(k.pe, lambda: nc.tensor.matmul(out=ps[1], lhsT=ones_bf[:, :], rhs=sq[0][:, :],
                                                                start=True, stop=True),
                                 reads=[sqb[0], b_ones], writes=[pb[1]])
                            if kind == "qa":
                                k.op(k.act, lambda: nc.scalar.activation(out=rstd[:, :], in_=ps[1], func=AF.Sqrt,
                                                                         scale=1.0, bias=eps_sb[:, 1:2]),
                                     reads=[pb[1], b_eps], writes=[rstdb])
                            else:
                                k.op(k.act, lambda: nc.scalar.activation(out=rstd[:, :], in_=ps[1], func=AF.Sqrt,
                                                                         scale=1.0 / 128, bias=eps_sb[:, 0:1]),
                                     reads=[pb[1], b_eps], writes=[rstdb])
                            k.op(k.dve, lambda: nc.vector.reciprocal(out=rstd[:, :], in_=rstd[:, :]),
                                 reads=[rstdb], writes=[rstdb])
                            gc = 0 if kind == "qa" else 1
                            k.op(k.dve, lambda fi=fi, si=si, gc=gc: nc.vector.scalar_tensor_tensor(
                                out=stg[si][:, :], in0=stgf[fi][:, :], scalar=gsm_sb[:, gc:gc + 1], in1=rstd[:, :],
                                op0=OP.mult, op1=OP.mult),
                                 reads=[stgfb[fi], rstdb, b_cst], writes=[stgb[si]])
                            dst = (QAT if kind == "qa" else KAT)[oc % 8, :, tk]
                        elif kind == "og":
                            k.op(k.act, lambda si=si, bank=bank: nc.scalar.activation(out=stg[si][:, :], in_=ps[bank], func=AF.Sigmoid),
                                 reads=[pb[bank]], writes=[stgb[si]])
                            dst = OGT[oc % 8, :, tk]
                        elif kind == "km":
                            k.op(k.act, lambda si=si, bank=bank: nc.scalar.mul(out=stg[si][:, :], in_=ps[bank], mul=0.0625),
                                 reads=[pb[bank]], writes=[stgb[si]])
                            dst = KMT[oc % 8, :, tk]
                        else:
                            k.op(k.act, lambda si=si, bank=bank: nc.scalar.copy(out=stg[si][:, :], in_=ps[bank]),
                                 reads=[pb[bank]], writes=[stgb[si]])
                            dst = QMT[oc % 8, :, tk]
                        k.dma(k.pool, [(dst, stg[si][:, :])], reads=[stgb[si]], owner=stgb[si])
                if n in TM:
                    kind = TM[n]
                    half = n % 2
                    dstT = {"va": VA, "km": KM, "vm": VM}[kind]
                    for s in range(4):
                        bank = 4 + s % 2
                        fns = [lambda kc=kc, s=s, bank=bank: nc.tensor.matmul(
                            out=ps[bank], lhsT=hT[:, kc, s * 128:(s + 1) * 128], rhs=wv[:, kc, :],
                            start=(kc == 0), stop=(kc == 15)) for kc in range(16)]
                        k.op(k.pe, fns, reads=[wb, hb], writes=[pb[bank]])
                        si = tb.si % 4
                        tb.si += 1
                        if kind == "km":
                            k.op(k.dve, lambda si=si, bank=bank: nc.vector.tensor_scalar(
                                out=stg[si][:, :], in0=ps[bank], scalar1=0.0625, scalar2=None, op0=OP.mult),
                                 reads=[pb[bank]], writes=[stgb[si]])
                        else:
                            k.op(k.dve, lambda si=si, bank=bank: nc.vector.tensor_copy(out=stg[si][:, :], in_=ps[bank]),
                                 reads=[pb[bank]], writes=[stgb[si]])
                        r0 = t * T + s * 128
                        k.dma(k.pool, [(dstT[r0:r0 + 128, half * 512:(half + 1) * 512], stg[si][:, :])],
                              reads=[stgb[si]], owner=stgb[si])
            wt, wb = wload(tb, lambda s: [(s[:, 0:256], WING.rearrange("p c g -> p (c g)"))])
            wv = wt[:, 0:256].rearrange("p (c g) -> p c g", c=16)
            for s in range(4):
                bank = 4 + s % 2
                fns = [lambda kc=kc, s=s, bank=bank: nc.tensor.matmul(
                    out=ps[bank][:, 0:16], lhsT=hT[:, kc, s * 128:(s + 1) * 128], rhs=wv[:, kc, :],
                    start=(kc == 0), stop=(kc == 15)) for kc in range(16)]
                k.op(k.pe, fns, reads=[wb, hb], writes=[pb[bank]])
                fi = tb.fi % 2
                tb.fi += 1
                k.op(k.dve, lambda fi=fi, bank=bank: nc.vector.tensor_copy(out=stgf[fi][:, 0:16], in_=ps[bank][:, 0:16]),
                     reads=[pb[bank]], writes=[stgfb[fi]])
                r0 = t * T + s * 128
                k.dma(k.pool, [(GT[r0:r0 + 128, :], stgf[fi][:, 0:16])], reads=[stgfb[fi]], owner=stgfb[fi])
        k.end_phase(keep=(b_cst,))


    if "B" in phases:
      with ExitStack() as esB:
        sbt = lambda name, shape, dt: esB.enter_context(nc.sbuf_tensor(name, shape, dt))
        QT = [sbt(f"naQ{i}", [128, NTOK], BF16) for i in range(2)]
        KT = [sbt(f"naK{i}", [128, NTOK], BF16) for i in range(2)]
        VV = [sbt(f"naV{i}", [128, NCH, 128], BF16) for i in range(2)]
        YA = [sbt(f"naY{i}", [128, NTOK], BF16) for i in range(2)]
        RPB = [sbt(f"naR{i}", [128, 9, 128], F32) for i in range(2)]
        BMI = [sbt(f"naBI{i}", [128, 9, 128], F32) for i in range(2)]
        BME = [sbt(f"naBE{i}", [128, 9, 128], F32) for i in range(2)]
        cmI_sb = sbt("cmI_sb", [128, 9, 128], F32)
        cmE_sb = sbt("cmE_sb", [128, 9, 128], F32)
        rb_sb = sbt("rb_sb", [128, NE, 9, 2], F32)
        S1 = [sbt(f"naS{i}", [128, 9, 128], F32) for i in range(2)]
        PT = [sbt(f"naP{i}", [128, 9, 128], BF16) for i in range(2)]
        REC = [sbt(f"naRec{i}", [128, 128], F32) for i in range(2)]
        b_in = [Buf(f"naIn{i}") for i in range(2)]
        b_ya = [Buf(f"naY{i}") for i in range(2)]
        b_rp = [Buf(f"naR{i}") for i in range(2)]
        b_bm = [Buf(f"naBM{i}") for i in range(2)]
        b_cm = Buf("naCM")
        b_s1 = [Buf(f"naS{i}") for i in range(2)]
        b_pt = [Buf(f"naP{i}") for i in range(2)]
        b_rec = [Buf(f"naRec{i}") for i in range(2)]
        k.dma(k.sp, [(cmI_sb[:, :, :], cmI[:, :, :]), (cmE_sb[:, :, :], cmE[:, :, :]), (rb_sb[:, :, :, :], rb[:, :, :, :])],
              writes=[b_cm], owner=b_cm)
        epos = {0: 0, 1: 1, 14: 2, 15: 3}

        def na_load(h):
            i = h % 2
            prs = [(QT[i][:, :], QAT[h]), (KT[i][:, :], KAT[h])]
            for n0 in range(0, NCH, 16):
                prs.append((VV[i][:, n0:n0 + 16, :],
                            VA[n0 * 128:(n0 + 16) * 128, h * 128:(h + 1) * 128].rearrange("(n p) d -> p n d", p=128)))
            k.dma(k.sp, prs, writes=[b_in[i]], owner=b_in[i])
            k.dma(k.sp, [(RPB[i][:, :, :], rpbg[h])], writes=[b_rp[i]], owner=b_rp[i])

        na_load(0)
        for h in range(8):
            i = h % 2
            if h + 1 < 8:
                na_load(h + 1)
            k.op(k.pool, [lambda i=i: nc.gpsimd.tensor_tensor(out=BMI[i][:, :, :], in0=RPB[i][:, :, :], in1=cmI_sb[:, :, :], op=OP.add),
                          lambda i=i: nc.gpsimd.tensor_tensor(out=BME[i][:, :, :], in0=RPB[i][:, :, :], in1=cmE_sb[:, :, :], op=OP.add)],
                 reads=[b_rp[i], b_cm], writes=[b_bm[i]])
            def rng_(c):
                return max(0, 4 - c), min(8, NCH - 1 - c + 4)

            def na_S(c, i=i):
                j = c % 2
                o_lo, o_hi = rng_(c)
                sbanks = [pb[3 * j], pb[3 * j + 1], pb[3 * j + 2]]
                Sps = psall[:, j * 1536:j * 1536 + 1152].rearrange("p (o q) -> p o q", o=9)
                fns = [lambda o=o: nc.tensor.matmul(
                    out=Sps[:, o, :], lhsT=KT[i][:, (c + o - 4) * 128:(c + o - 3) * 128], rhs=QT[i][:, c * 128:(c + 1) * 128],
                    start=True, stop=True) for o in range(o_lo, o_hi + 1)]
                k.op(k.pe, fns, reads=[b_in[i]], writes=sbanks)

            def na_soft(c, i=i):
                j = c % 2
                o_lo, o_hi = rng_(c)
                no = o_hi - o_lo + 1
                sbanks = [pb[3 * j], pb[3 * j + 1], pb[3 * j + 2]]
                Sps = psall[:, j * 1536:j * 1536 + 1152].rearrange("p (o q) -> p o q", o=9)
                edge = (c % 16) in epos
                BM = BME[i] if edge else BMI[i]
                k.op(k.dve, lambda: nc.vector.tensor_tensor(
                    out=S1[j][:, o_lo:o_hi + 1, :], in0=Sps[:, o_lo:o_hi + 1, :], in1=BM[:, o_lo:o_hi + 1, :], op=OP.add),
                     reads=sbanks + [b_bm[i]], writes=[b_s1[j]])
                if edge:
                    e = (c // 16) * 4 + epos[c % 16]
                    k.op(k.pool, lambda: nc.gpsimd.tensor_tensor(
                        out=S1[j][:, o_lo:o_hi + 1, :].rearrange("p o (b q) -> p o b q", b=2),
                        in0=S1[j][:, o_lo:o_hi + 1, :].rearrange("p o (b q) -> p o b q", b=2),
                        in1=rb_sb[:, e, o_lo:o_hi + 1, :].unsqueeze(3).to_broadcast([128, no, 2, 64]), op=OP.add),
                         reads=[b_cm], writes=[b_s1[j]])
                k.op(k.act, lambda: nc.scalar.activation(
                    out=PT[j][:, o_lo:o_hi + 1, :], in_=S1[j][:, o_lo:o_hi + 1, :], func=AF.Exp),
                     reads=[b_s1[j]], writes=[b_pt[j]])

            def na_PV(c, i=i):
                j = c % 2
                o_lo, o_hi = rng_(c)
                ob = 6 + j
                fns = []
                for o in range(o_lo, o_hi + 1):
                    fns.append(lambda o=o: nc.tensor.matmul(
                        out=ps[ob][:, 0:128], lhsT=VV[i][:, c + o - 4, :], rhs=PT[j][:, o, :],
                        start=(o == o_lo), stop=(o == o_hi)))
                for o in range(o_lo, o_hi + 1):
                    fns.append(lambda o=o: nc.tensor.matmul(
                        out=ps[ob][:, 128:256], lhsT=ones_bf[:, :], rhs=PT[j][:, o, :],
                        start=(o == o_lo), stop=(o == o_hi)))
                k.op(k.pe, fns, reads=[b_in[i], b_pt[j], b_ones], writes=[pb[ob]])

            def na_fin(c, i=i):
                j = c % 2
                ob = 6 + j
                k.op(k.dve, lambda: nc.vector.reciprocal(out=REC[j][:, :], in_=ps[ob][:, 128:256]),
                     reads=[pb[ob]], writes=[b_rec[j]])
                k.op(k.dve, lambda: nc.vector.tensor_tensor(
                    out=YA[i][:, c * 128:(c + 1) * 128], in0=ps[ob][:, 0:128], in1=REC[j][:, :], op=OP.mult),
                     reads=[pb[ob], b_rec[j]], writes=[b_ya[i]])

            na_S(0)
            for c in range(NCH):
                if c + 1 < NCH:
                    na_S(c + 1)
                na_soft(c)
                if c >= 1:
                    na_fin(c - 1)
                na_PV(c)
            na_fin(NCH - 1)
            k.dma(k.sp, [(YT[h], YA[i][:, :])], reads=[b_ya[i]], owner=b_ya[i])
        k.end_phase(keep=(b_cst,))


    if "M" in phases:
      with ExitStack() as esM:
        sbt = lambda name, shape, dt: esM.enter_context(nc.sbuf_tensor(name, shape, dt))
        NG = NCH * 8
        G_sb = sbt("G_sb", [128, NCH, 16], F32)
        gbias_sb = sbt("gbias_sb", [128, 16], F32)
        keep_sb = sbt("keep_sb", [128, 2, NCH], F32)
        LF = sbt("LF", [128, NCH, 8], F32)
        BS = sbt("BS", [128, NCH, 8], F32)
        IB = sbt("IB", [128, NCH, 8], F32)
        KS = sbt("KS", [128, NCH, 8], F32)
        DEC = sbt("DEC", [128, NCH, 8], F32)
        b_g = Buf("mG"); b_lf = Buf("mLF"); b_bs = Buf("mBS"); b_ib = Buf("mIB"); b_ks = Buf("mKS"); b_dec = Buf("mDEC")
        b_gb = Buf("mGB")
        k.dma(k.sp, [(G_sb[:, n0:n0 + 16, :], GT[n0 * 128:(n0 + 16) * 128, :].rearrange("(n p) g -> p n g", p=128))
                     for n0 in range(0, NCH, 16)], writes=[b_g], owner=b_g)
        k.dma(k.sp, [(gbias_sb[:, :], gbias[:, :]), (keep_sb[:, :, :], keep[:, :, :])], writes=[b_gb], owner=b_gb)
        k.op(k.dve, lambda: nc.vector.tensor_tensor(out=G_sb[:, :, :], in0=G_sb[:, :, :],
                                                    in1=gbias_sb[:, :].unsqueeze(1).to_broadcast([128, NCH, 16]), op=OP.add),
             reads=[b_gb], writes=[b_g])
        k.op(k.act, lambda: nc.scalar.activation(out=LF[:, :, :], in_=G_sb[:, :, 8:16], func=AF.Exp, scale=-1.0),
             reads=[b_g], writes=[b_lf])
        k.op(k.act, lambda: nc.scalar.activation(out=LF[:, :, :], in_=LF[:, :, :], func=AF.Ln, bias=eps_sb[:, 2:3]),
             reads=[b_lf, b_eps], writes=[b_lf])
        k.op(k.dve, lambda: nc.vector.tensor_scalar(out=LF[:, :, :], in0=LF[:, :, :], scalar1=-1.0, scalar2=None, op0=OP.mult),
             reads=[b_lf], writes=[b_lf])
        LF2 = LF[:, :, :].rearrange("p n g -> p (n g)")
        k.op(k.pe, lambda: nc.tensor.matmul(out=ps[0][:, 0:NG], lhsT=triF, rhs=LF2, start=True, stop=True),
             reads=[b_lf, b_cst], writes=[pb[0]])
        k.op(k.pe, lambda: nc.tensor.matmul(out=ps[1][:, 0:NG], lhsT=triB, rhs=LF2, start=True, stop=True),
             reads=[b_lf, b_cst], writes=[pb[1]])
        k.op(k.pe, lambda: nc.tensor.matmul(out=ps[2][:, 0:NG], lhsT=ones_f, rhs=LF2, start=True, stop=True),
             reads=[b_lf, b_cst], writes=[pb[2]])
        k.op(k.dve, [lambda: nc.vector.tensor_copy(out=BS[:, :, 0:4], in_=ps[0][:, 0:NG].rearrange("p (n g) -> p n g", g=8)[:, :, 0:4]),
                     lambda: nc.vector.tensor_copy(out=BS[:, :, 4:8], in_=ps[1][:, 0:NG].rearrange("p (n g) -> p n g", g=8)[:, :, 4:8])],
             reads=[pb[0], pb[1]], writes=[b_bs])
        k.op(k.dve, lambda: nc.vector.tensor_tensor(out=IB[:, :, :], in0=G_sb[:, :, 0:8], in1=BS[:, :, :], op=OP.subtract),
             reads=[b_g, b_bs], writes=[b_ib])
        k.op(k.act, lambda: nc.scalar.activation(out=KS[:, :, :], in_=IB[:, :, :], func=AF.Exp), reads=[b_ib], writes=[b_ks])
        k.op(k.act, lambda: nc.scalar.activation(out=DEC[:, :, :].rearrange("p n g -> p (n g)"), in_=ps[2][:, 0:NG], func=AF.Exp),
             reads=[pb[2]], writes=[b_dec])
        k.op(k.dve, [lambda d=d: nc.vector.tensor_tensor(
            out=DEC[:, :, d * 4:(d + 1) * 4], in0=DEC[:, :, d * 4:(d + 1) * 4],
            in1=keep_sb[:, d, :].unsqueeze(2).to_broadcast([128, NCH, 4]), op=OP.mult) for d in range(2)],
             reads=[b_gb], writes=[b_dec])

        QT2 = sbt("mQT", [128, 2, NTOK], BF16)
        KT2 = sbt("mKT", [128, 2, NTOK], BF16)
        Ktok = sbt("mK", [128, NCH, 256], BF16)
        Vaug = sbt("mV", [128, NCH, 264], BF16)
        b_hd = Buf("mHead")
        b_v1 = Buf("mVones")
        k.op(k.pool, lambda: nc.gpsimd.memset(Vaug[:, :, 256:264], 1.0), writes=[b_v1])

        class DS:
            pass
        dd = []
        for d in range(2):
            s_ = DS()
            nm = lambda x, d=d: f"m{x}{d}"
            def two(name, shape, dt, s_=s_, nm=nm):
                setattr(s_, name, [sbt(nm(name) + f"_{q}", shape, dt) for q in range(2)])
                setattr(s_, "b_" + name, [Buf(nm(name) + f"_{q}") for q in range(2)])
            def one(name, shape, dt, s_=s_, nm=nm):
                setattr(s_, name, sbt(nm(name), shape, dt))
                setattr(s_, "b_" + name, Buf(nm(name)))
            two("diag", [128, 128], F32); two("DT", [128, 128], F32); two("EB", [128, 128], F32)
            two("DTm", [128, 128], F32); two("SD", [128, 128], BF16); two("Qp", [128, 2, 128], BF16)
            two("Kp", [128, 256], BF16); two("dcl", [128, 128], F32); two("hbuf", [128, 2, 128], F32)
            two("hl", [128, 2, 128], F32); two("og", [128, 2, 128], BF16); two("sqh", [128, 2, 128], BF16)
            one("rs", [128, 128], F32); one("tmp", [128, 2, 128], F32); one("ym", [128, 2, 128], BF16)
            one("U32", [128, 2, 256], F32); two("Ubf", [128, 2, 256], BF16)
            one("n32", [128, 2], F32); two("nB", [128, 2, 128], BF16)
            s_.p_a = pb[4 * d]; s_.p_nd = pb[4 * d + 1]; s_.p_du = [pb[4 * d + 2], pb[4 * d + 3]]
            dd.append(s_)
        hbb = {}

        def geom(i, d):
            n = i if d == 0 else NCH - 1 - i
            nprev = n - 1 if d == 0 else n + 1
            first = (n < NCH // 2) if d == 0 else (n >= NCH // 2)
            return n, nprev, first

        def m_prep(h, i, d):
            s_ = dd[d]; q = i % 2
            n, nprev, first = geom(i, d)
            g = d * 4 + h
            ck = slice(n * 128, (n + 1) * 128)
            A = ps[4 * d]
            Bt = A[:, 0:128]
            Sp = A[:, 128:256]
            mask = triF if d == 0 else triB
            k.op(k.pool, lambda: nc.gpsimd.tensor_scalar(
                out=s_.diag[q][:, :], in0=ident, scalar1=BS[:, n, g:g + 1], scalar2=1.0, op0=OP.mult, op1=OP.mult),
                 reads=[b_bs, b_cst], writes=[s_.b_diag[q]])
            yield
            k.op(k.pe, lambda: nc.tensor.matmul(out=Bt, lhsT=ones_f, rhs=s_.diag[q][:, :], start=True, stop=True),
                 reads=[s_.b_diag[q], b_cst], writes=[s_.p_a])
            yield
            k.op(k.pe, [lambda dkc=dkc: nc.tensor.matmul(
                out=Sp, lhsT=KT2[:, dkc, ck], rhs=QT2[:, dkc, ck], start=(dkc == 0), stop=(dkc == 1))
                for dkc in range(2)], reads=[b_hd], writes=[s_.p_a])
            yield
            k.op(k.act, lambda: nc.scalar.activation(out=s_.DT[q][:, :], in_=Bt, func=AF.Exp, bias=IB[:, n, g:g + 1]),
                 reads=[s_.p_a, b_ib], writes=[s_.b_DT[q]])
            yield
            if i > 0:
                k.op(k.act, lambda: nc.scalar.activation(out=s_.EB[q][:, :], in_=Bt, func=AF.Exp),
                     reads=[s_.p_a], writes=[s_.b_EB[q]])
                yield
            k.op(k.pool, lambda: nc.gpsimd.tensor_tensor(out=s_.DTm[q][:, :], in0=s_.DT[q][:, :], in1=mask, op=OP.mult),
                 reads=[s_.b_DT[q], b_cst], writes=[s_.b_DTm[q]])
            yield
            k.op(k.dve, lambda: nc.vector.tensor_tensor(out=s_.SD[q][:, :], in0=Sp, in1=s_.DTm[q][:, :], op=OP.mult),
                 reads=[s_.p_a, s_.b_DTm[q]], writes=[s_.b_SD[q]])
            yield
            if i > 0:
                k.op(k.dve, lambda: nc.vector.scalar_tensor_tensor(
                    out=s_.Qp[q][:, :, :], in0=QT2[:, :, ck], scalar=DEC[:, nprev, g:g + 1],
                    in1=s_.EB[q][:, :].unsqueeze(1).to_broadcast([128, 2, 128]), op0=OP.mult, op1=OP.mult),
                     reads=[b_hd, b_dec, s_.b_EB[q]], writes=[s_.b_Qp[q]])
                yield
            if i < NCH - 1:
                k.op(k.pool, lambda: nc.gpsimd.tensor_scalar(
                    out=s_.Kp[q][:, :], in0=Ktok[:, n, :], scalar1=KS[:, n, g:g + 1], scalar2=1.0, op0=OP.mult, op1=OP.mult),
                     reads=[b_hd, b_ks], writes=[s_.b_Kp[q]])
                yield
                B2 = ps[4 * d + 2 + q]
                fns = [lambda dkc=dkc: nc.tensor.matmul(
                    out=B2[:, dkc * 256:(dkc + 1) * 256], lhsT=s_.Kp[q][:, dkc * 128:(dkc + 1) * 128], rhs=Vaug[:, n, 0:256],
                    start=True, stop=True) for dkc in range(2)]
                k.op(k.pe, fns, reads=[s_.b_Kp[q], b_hd], writes=[s_.p_du[q]])
                yield

        def m_chain(h, i, d):
            s_ = dd[d]; q = i % 2
            n, nprev, first = geom(i, d)
            g = d * 4 + h
            B1 = ps[4 * d + 1]
            fns = []
            for dvc in range(2):
                fns.append(lambda dvc=dvc: nc.tensor.matmul(
                    out=B1[:, dvc * 128:(dvc + 1) * 128], lhsT=Vaug[:, n, dvc * 128:(dvc + 1) * 128], rhs=s_.SD[q][:, :],
                    start=True, stop=(i == 0)))
                if i > 0:
                    for dkc in range(2):
                        fns.append(lambda dvc=dvc, dkc=dkc: nc.tensor.matmul(
                            out=B1[:, dvc * 128:(dvc + 1) * 128], lhsT=s_.Ubf[1 - q][:, dkc, dvc * 128:(dvc + 1) * 128],
                            rhs=s_.Qp[q][:, dkc, :], start=False, stop=(dkc == 1)))
            fns.append(lambda: nc.tensor.matmul(out=B1[:, 256:384], lhsT=ones_bf[:, :], rhs=s_.SD[q][:, :],
                                                start=True, stop=(i == 0)))
            if i > 0:
                for dkc in range(2):
                    fns.append(lambda dkc=dkc: nc.tensor.matmul(
                        out=B1[:, 256:384], lhsT=s_.nB[1 - q][:, dkc, :], rhs=s_.Qp[q][:, dkc, :], start=False, stop=(dkc == 1)))
            rd = [b_hd, b_v1, s_.b_SD[q], b_ones] + ([s_.b_Ubf[1 - q], s_.b_Qp[q], s_.b_nB[1 - q]] if i > 0 else [])
            k.op(k.pe, fns, reads=rd, writes=[s_.p_nd])
            yield
            if i < NCH - 1:
                B2 = ps[4 * d + 2 + q]
                A = ps[4 * d]
                k.op(k.pe, [lambda dkc=dkc: nc.tensor.matmul(
                    out=A[:, 384 + dkc:385 + dkc], lhsT=s_.Kp[q][:, dkc * 128:(dkc + 1) * 128], rhs=Vaug[:, n, 256:257],
                    start=True, stop=True) for dkc in range(2)], reads=[s_.b_Kp[q], b_v1], writes=[s_.p_a])
                yield
                U2 = s_.U32[:, :, :].rearrange("p c v -> p (c v)")
                if i == 0:
                    k.op(k.dve, lambda: nc.vector.tensor_copy(out=U2, in_=B2[:, :]), reads=[s_.p_du[q]], writes=[s_.b_U32])
                    yield
                    k.op(k.dve, lambda: nc.vector.tensor_copy(out=s_.n32[:, :], in_=A[:, 384:386]), reads=[s_.p_a], writes=[s_.b_n32])
                    yield
                else:
                    k.op(k.dve, lambda: nc.vector.scalar_tensor_tensor(
                        out=U2, in0=U2, scalar=DEC[:, nprev, g:g + 1], in1=B2[:, :], op0=OP.mult, op1=OP.add),
                         reads=[s_.p_du[q], b_dec], writes=[s_.b_U32])
                    yield
                    k.op(k.dve, lambda: nc.vector.scalar_tensor_tensor(
                        out=s_.n32[:, :], in0=s_.n32[:, :], scalar=DEC[:, nprev, g:g + 1], in1=A[:, 384:386], op0=OP.mult, op1=OP.add),
                         reads=[s_.p_a, b_dec], writes=[s_.b_n32])
                    yield
                k.op(k.act, lambda: nc.scalar.copy(out=s_.Ubf[q][:, :, :], in_=s_.U32[:, :, :]), reads=[s_.b_U32], writes=[s_.b_Ubf[q]])
                yield
                k.op(k.act, lambda: nc.scalar.copy(out=s_.nB[q][:, :, :], in_=s_.n32[:, :].unsqueeze(2).to_broadcast([128, 2, 128])),
                     reads=[s_.b_n32], writes=[s_.b_nB[q]])
                yield

        def m_h(h, i, d):
            s_ = dd[d]; q = i % 2
            n, nprev, first = geom(i, d)
            ck = slice(n * 128, (n + 1) * 128)
            B1 = ps[4 * d + 1]
            if not first:
                k.dma(k.sp, [(s_.hl[q][:, :, :], HB[2 * h:2 * h + 2, :, ck].rearrange("c p t -> p c t"))],
                      reads=[hbb[(h, n)]], writes=[s_.b_hl[q]], owner=s_.b_hl[q])
                yield
                k.dma(k.sp, [(s_.og[q][:, :, :], OGT[2 * h:2 * h + 2, :, ck].rearrange("c p t -> p c t"))],
                      writes=[s_.b_og[q]], owner=s_.b_og[q])
                yield
            k.op(k.act, lambda: nc.scalar.activation(out=s_.dcl[q][:, :], in_=B1[:, 256:384], func=AF.Abs),
                 reads=[s_.p_nd], writes=[s_.b_dcl[q]])
            yield
            k.op(k.dve, lambda: nc.vector.tensor_scalar(out=s_.dcl[q][:, :], in0=s_.dcl[q][:, :], scalar1=1.0, scalar2=None, op0=OP.max),
                 reads=[s_.b_dcl[q]], writes=[s_.b_dcl[q]])
            yield
            k.op(k.act, lambda: nc.scalar.activation(out=s_.dcl[q][:, :], in_=s_.dcl[q][:, :], func=AF.Ln),
                 reads=[s_.b_dcl[q]], writes=[s_.b_dcl[q]])
            yield
            k.op(k.act, lambda: nc.scalar.activation(out=s_.dcl[q][:, :], in_=s_.dcl[q][:, :], func=AF.Exp, scale=-1.0),
                 reads=[s_.b_dcl[q]], writes=[s_.b_dcl[q]])
            yield
            k.op(k.dve, lambda: nc.vector.tensor_tensor(
                out=s_.hbuf[q][:, :, :], in0=B1[:, 0:256].rearrange("p (c t) -> p c t", c=2),
                in1=s_.dcl[q][:, :].unsqueeze(1).to_broadcast([128, 2, 128]), op=OP.mult),
                 reads=[s_.p_nd, s_.b_dcl[q]], writes=[s_.b_hbuf[q]])
            yield
            if first:
                hbb[(h, n)] = Buf(f"hb{h}_{n}")
                k.dma(k.sp, [(HB[2 * h:2 * h + 2, :, ck].rearrange("c p t -> p c t"), s_.hbuf[q][:, :, :])],
                      reads=[s_.b_hbuf[q]], writes=[hbb[(h, n)]], owner=s_.b_hbuf[q])
                yield
            else:
                k.op(k.pool, lambda: nc.gpsimd.tensor_tensor(out=s_.hl[q][:, :, :], in0=s_.hl[q][:, :, :], in1=s_.hbuf[q][:, :, :], op=OP.add),
                     reads=[s_.b_hbuf[q]], writes=[s_.b_hl[q]])
                yield
                k.op(k.act, lambda: nc.scalar.activation(out=s_.sqh[q][:, :, :], in_=s_.hl[q][:, :, :], func=AF.Square),
                     reads=[s_.b_hl[q]], writes=[s_.b_sqh[q]])
                yield

        def m_fin2(h, i, d):
            s_ = dd[d]; q = i % 2
            n, nprev, first = geom(i, d)
            if first:
                return
            ck = slice(n * 128, (n + 1) * 128)
            A = ps[4 * d]
            k.op(k.pe, [lambda dvc=dvc: nc.tensor.matmul(
                out=A[:, 256:384], lhsT=ones_bf[:, :], rhs=s_.sqh[q][:, dvc, :], start=(dvc == 0), stop=(dvc == 1))
                for dvc in range(2)], reads=[s_.b_sqh[q], b_ones], writes=[s_.p_a])
            yield
            k.op(k.act, lambda: nc.scalar.activation(out=s_.rs[:, :], in_=A[:, 256:384], func=AF.Ln,
                                                     scale=1.0 / 256, bias=eps_sb[:, 0:1]),
                 reads=[s_.p_a, b_eps], writes=[s_.b_rs])
            yield
            k.op(k.act, lambda: nc.scalar.activation(out=s_.rs[:, :], in_=s_.rs[:, :], func=AF.Exp, scale=-0.5),
                 reads=[s_.b_rs], writes=[s_.b_rs])
            yield
            k.op(k.dve, [lambda dvc=dvc: nc.vector.scalar_tensor_tensor(
                out=s_.tmp[:, dvc, :], in0=s_.hl[q][:, dvc, :], scalar=gsm_sb[:, 2 + 2 * h + dvc:3 + 2 * h + dvc],
                in1=s_.rs[:, :], op0=OP.mult, op1=OP.mult) for dvc in range(2)],
                 reads=[s_.b_hl[q], s_.b_rs, b_cst], writes=[s_.b_tmp])
            yield
            k.op(k.pool, lambda: nc.gpsimd.tensor_tensor(out=s_.ym[:, :, :], in0=s_.tmp[:, :, :], in1=s_.og[q][:, :, :], op=OP.mult),
                 reads=[s_.b_tmp, s_.b_og[q]], writes=[s_.b_ym])
            yield
            k.dma(k.sp, [(YT[8 + 2 * h:10 + 2 * h, :, ck].rearrange("c p t -> p c t"), s_.ym[:, :, :])],
                  reads=[s_.b_ym], owner=s_.b_ym)
            yield

        for h in range(4):
            prs = [(QT2[:, :, :], QMT[2 * h:2 * h + 2].rearrange("c p t -> p c t")),
                   (KT2[:, :, :], KMT[2 * h:2 * h + 2].rearrange("c p t -> p c t"))]
            for n0 in range(0, NCH, 16):
                rows = slice(n0 * 128, (n0 + 16) * 128)
                prs.append((Ktok[:, n0:n0 + 16, :], KM[rows, h * 256:(h + 1) * 256].rearrange("(n p) d -> p n d", p=128)))
                prs.append((Vaug[:, n0:n0 + 16, 0:256], VM[rows, h * 256:(h + 1) * 256].rearrange("(n p) d -> p n d", p=128)))
            k.dma(k.sp, prs, writes=[b_hd], owner=b_hd)
            def rr(*gens):
                gens = list(gens)
                while gens:
                    for g_ in list(gens):
                        try:
                            next(g_)
                        except StopIteration:
                            gens.remove(g_)

            def seq(*gens):
                for g_ in gens:
                    for _ in g_:
                        pass

            seq(m_prep(h, 0, 0), m_prep(h, 0, 1))
            for i in range(NCH):
                if i + 1 < NCH:
                    seq(m_prep(h, i + 1, 0), m_prep(h, i + 1, 1))
                seq(m_chain(h, i, 0), m_chain(h, i, 1))
                if i >= 1:
                    seq(m_fin2(h, i - 1, 0), m_fin2(h, i - 1, 1))
                seq(m_h(h, i, 0), m_h(h, i, 1))
            seq(m_fin2(h, NCH - 1, 0), m_fin2(h, NCH - 1, 1))
        k.end_phase(keep=(b_cst,))


    if "C" in phases:
      with ExitStack() as esC:
        tb = alloc_tiles(esC, "C_")
        sbt = lambda name, shape, dt: esC.enter_context(nc.sbuf_tensor(name, shape, dt))
        hT, hb, xT, xb, sg, sgb = tb.hT, tb.hb, tb.xT, tb.xb, tb.sg, tb.sgb
        wpp_sb = sbt("wpp_sb", [128, 2, D], BF16)
        b_wpp = Buf("wpp")
        k.dma(k.sp, [(wpp_sb[:, :, :], WPP[:, :, :])], writes=[b_wpp], owner=b_wpp)
        pin = [sbt(f"pin{i}", [128, 256], F32) for i in range(2)]
        pinb = [Buf(f"pin{i}") for i in range(2)]
        peT = sbt("peT", [128, 2, T], BF16)
        b_peT = Buf("peT")
        sg2 = [sbt(f"sg2_{j}", [128, T], F32) for j in range(2)]
        sg2b = [Buf(f"sg2_{j}") for j in range(2)]
        for t in range(NT):
            tk = slice(t * T, (t + 1) * T)
            k.dma(k.sp, [(xT[:, :, :], X1T[:, :, tk])], writes=xb, owner=xb[0])
            k.dma(k.sp, [(hT[:, :, :], YT[:, :, tk].rearrange("c p t -> p c t"))], writes=[hb], owner=hb)
            for n in range(4):
                wt, wb = wload(tb, lambda s, n=n: [(s[:, :], WOUT[n].rearrange("p c f -> p (c f)"))])
                wv = wt[:, :].rearrange("p (c f) -> p c f", c=16)
                for j in range(4):
                    dc = n * 4 + j
                    bank = 6 + dc % 2
                    fns = [lambda kc=kc, j=j, bank=bank, wv=wv: nc.tensor.matmul(
                        out=ps[bank], lhsT=wv[:, kc, j * 128:(j + 1) * 128], rhs=hT[:, kc, :],
                        start=(kc == 0), stop=(kc == 15)) for kc in range(16)]
                    k.op(k.pe, fns, reads=[wb, hb], writes=[pb[bank]])
                    k.op(k.dve, lambda dc=dc, bank=bank: nc.vector.tensor_tensor(
                        out=xT[:, dc, :], in0=ps[bank], in1=xT[:, dc, :], op=OP.add),
                         reads=[pb[bank]], writes=[xb[dc]])
            rmsnorm_T(tb, 2 * DC)
            ffn(tb, WGU2, WD2)
            rmsnorm_T(tb, 3 * DC)
            for s in range(4):
                i = s % 2
                r0 = t * T + s * 128
                k.dma(k.sp, [(pin[i][:, :], pes[r0:r0 + 128, :])], writes=[pinb[i]], owner=pinb[i])
                k.op(k.pe, [lambda k2=k2, i=i: nc.tensor.transpose(
                    out=ps[i][:, k2 * 128:(k2 + 1) * 128], in_=pin[i][:, k2 * 128:(k2 + 1) * 128], identity=ident)
                    for k2 in range(2)], reads=[pinb[i], b_cst], writes=[pb[i]])
                k.op(k.act, lambda i=i, s=s: nc.scalar.copy(
                    out=peT[:, :, s * 128:(s + 1) * 128], in_=ps[i][:, 0:256].rearrange("p (c t) -> p c t", c=2)),
                     reads=[pb[i]], writes=[b_peT])
            for n in range(4):
                wt, wb = wload(tb, lambda s, n=n: [(s[:, :], WPG[n].rearrange("p c f -> p (c f)"))])
                wv = wt[:, :].rearrange("p (c f) -> p c f", c=16)
                for j in range(4):
                    dc = n * 4 + j
                    bank = 6 + dc % 2
                    pbank = 4 + dc % 2
                    jj = dc % 2
                    fns = [lambda kc=kc, j=j, bank=bank, wv=wv: nc.tensor.matmul(
                        out=ps[bank], lhsT=wv[:, kc, j * 128:(j + 1) * 128], rhs=hT[:, kc, :],
                        start=(kc == 0), stop=(kc == 15)) for kc in range(16)]
                    k.op(k.pe, fns, reads=[wb, hb], writes=[pb[bank]])
                    k.op(k.pe, [lambda k2=k2, dc=dc, pbank=pbank: nc.tensor.matmul(
                        out=ps[pbank], lhsT=wpp_sb[:, k2, dc * 128:(dc + 1) * 128], rhs=peT[:, k2, :],
                        start=(k2 == 0), stop=(k2 == 1)) for k2 in range(2)], reads=[b_wpp, b_peT], writes=[pb[pbank]])
                    k.op(k.act, lambda jj=jj, bank=bank: nc.scalar.activation(out=sg[jj][:, :], in_=ps[bank], func=AF.Sigmoid),
                         reads=[pb[bank]], writes=[sgb[jj]])
                    k.op(k.dve, lambda jj=jj, pbank=pbank: nc.vector.tensor_tensor(
                        out=sg2[jj][:, :], in0=ps[pbank], in1=sg[jj][:, :], op=OP.mult),
                         reads=[pb[pbank], sgb[jj]], writes=[sg2b[jj]])
                    k.op(k.pool, lambda jj=jj, dc=dc: nc.gpsimd.tensor_tensor(
                        out=xT[:, dc, :], in0=xT[:, dc, :], in1=sg2[jj][:, :], op=OP.add),
                         reads=[sg2b[jj]], writes=[xb[dc]])
            xin, xinb = tb.xin, tb.xinb
            for s in range(4):
                i = s % 2
                r0 = t * T + s * 128
                for q in range(4):
                    bank = q % 2
                    fns = [lambda dc=dc, j=j, bank=bank, s=s: nc.tensor.transpose(
                        out=ps[bank][:, j * 128:(j + 1) * 128], in_=xT[:, dc, s * 128:(s + 1) * 128], identity=ident)
                        for j, dc in enumerate(range(q * 4, q * 4 + 4))]
                    k.op(k.pe, fns, reads=[xb[dc] for dc in range(q * 4, q * 4 + 4)] + [b_cst], writes=[pb[bank]])
                    k.op(k.act, lambda q=q, i=i, bank=bank: nc.scalar.copy(out=xin[i][:, q * 512:(q + 1) * 512], in_=ps[bank]),
                         reads=[pb[bank]], writes=[xinb[i]])
                k.dma(k.pool, [(y[r0:r0 + 128, :], xin[i][:, :])], reads=[xinb[i]], owner=xinb[i])
        k.end_phase(keep=(b_cst,))


    k.barrier()
    es.close()
    return nc, k


def _consts():
    ident = np.eye(128, dtype=np.float32)
    s = np.arange(128)
    triF = (s[:, None] <= s[None, :]).astype(np.float32)
    triB = (s[:, None] >= s[None, :]).astype(np.float32)
    ones = np.ones((128, 128), np.float32)
    zeros = np.zeros((128, 128), np.float32)
    return np.ascontiguousarray(np.concatenate([ident, triF, triB, ones, zeros], axis=1))


def _na_tables(rpb, seq_rows):
    a = np.arange(2)[:, None, None, None, None]
    kc = np.arange(64)[None, :, None, None, None]
    o = np.arange(9)[None, None, :, None, None]
    b = np.arange(2)[None, None, None, :, None]
    qc = np.arange(64)[None, None, None, None, :]
    dr = 2 * (o - 4) + a - b + 0 * kc + 0 * qc
    dcol = np.clip(kc - qc, -15, 15) + 15 + 0 * dr
    cs = np.clip(qc - 8, 0, 48)
    colok = (kc >= cs) & (kc < cs + 16)
    inwin = np.abs(dr) <= 7
    dri = np.clip(dr + 7, 0, 14)
    g = rpb[:, dri, dcol]
    g = np.where(inwin[None], g, np.float32(0.0))
    rpbg = np.ascontiguousarray(g.reshape(8, 128, 9, 128).astype(np.float32))
    cmE = np.where(inwin & colok, 0.0, NEG).astype(np.float32).reshape(128, 9, 128)
    cmI = np.where(inwin & colok & (dr >= -4) & (dr <= 3), 0.0, NEG).astype(np.float32).reshape(128, 9, 128)
    R = sum(seq_rows)
    nch = R // 2
    seq_of_row = np.concatenate([np.full(r, i) for i, r in enumerate(seq_rows)])
    start = np.concatenate([[0], np.cumsum(seq_rows)[:-1]])
    ne = max(4, (nch // 16) * 4)
    rb = np.full((128, ne, 9, 2), NEG, np.float32)
    pos = {0: 0, 1: 1, 14: 2, 15: 3}
    for c in range(nch):
        if c % 16 not in pos:
            continue
        e = (c // 16) * 4 + pos[c % 16]
        for bb in range(2):
            qr = 2 * c + bb
            si = seq_of_row[qr]
            r0 = start[si]; Rs = seq_rows[si]
            rs = r0 + min(max(qr - r0 - 4, 0), Rs - 8)
            for oo in range(9):
                for aa in range(2):
                    kr = 2 * (c + oo - 4) + aa
                    if 0 <= kr < R and seq_of_row[kr] == si and rs <= kr <= rs + 7:
                        rb[aa * 64:(aa + 1) * 64, e, oo, bb] = 0.0
    keep = np.ones((128, 2, nch), np.float32)
    for s0 in start[1:]:
        cst = s0 // 2
        keep[:, 0, cst - 1] = 0.0
        keep[:, 1, cst] = 0.0
    return rpbg, cmE, cmI, rb, keep


def host_inputs(x_stream, p_stream, W, seq_rows):
    f = lambda a: np.ascontiguousarray(a, dtype=np.float32)
    gvec = lambda g: g.reshape(DC, 128).T
    rpbg, cmE, cmI, rb, keep = _na_tables(W["rpb"], seq_rows)
    gsm = np.zeros((128, 16), np.float32)
    gsm[:, 0] = W["g_qn"]; gsm[:, 1] = W["g_kn"]
    gsm[:, 2:10] = W["g_mh"].reshape(8, 128).T
    gb = np.concatenate([W["b_igate"].reshape(8), W["b_fgate"].reshape(8)])
    return {
        "xs": f(x_stream), "pes": f(p_stream),
        "w1g": f(W["w_ffn1_gate"]), "w1u": f(W["w_ffn1_up"]), "w1d": f(W["w_ffn1_down"]),
        "w2g": f(W["w_ffn2_gate"]), "w2u": f(W["w_ffn2_up"]), "w2d": f(W["w_ffn2_down"]),
        "win": f(W["w_in"]), "wout": f(W["w_out"]), "wpg": f(W["w_ple_gate"]), "wpp": f(W["w_ple_proj"]),
        "gv": f(np.concatenate([gvec(W["g_ffn1"]), gvec(W["g_mix"]), gvec(W["g_ffn2"]), gvec(W["g_ple"])], axis=1)),
        "gsm": f(gsm), "gbias": f(np.broadcast_to(gb[None, :], (128, 16))),
        "cst": _consts(), "rpbg": rpbg, "cmI": cmI, "cmE": cmE, "rb": rb, "keep": keep,
    }


_NC_CACHE = {}


def kernel(**inputs):
    NTOK = 8192
    f32 = lambda a: np.asarray(a, dtype=np.float32)
    W = {}
    for name in ("g_ffn1", "w_ffn1_gate", "w_ffn1_up", "w_ffn1_down", "g_mix", "w_in", "b_igate", "b_fgate",
                 "g_qn", "g_kn", "rpb", "g_mh", "w_out", "g_ffn2", "w_ffn2_gate", "w_ffn2_up", "w_ffn2_down",
                 "g_ple", "w_ple_gate", "w_ple_proj"):
        W[name] = f32(inputs[name])[0]
    xp = f32(inputs["x_prompt"]); xsm = f32(inputs["x_sample"])
    pp = f32(inputs["p_prompt"])[0]; psm = f32(inputs["p_sample"])[0]
    zx = np.zeros((NTOK, D), np.float32); zp = np.zeros((NTOK, 256), np.float32)
    streams = [(zx, zp, [128])] * 8
    streams[0] = (xsm[0], psm[0], [128])
    streams[2] = (xsm[1], psm[1], [128])
    streams[4] = (xp.reshape(NTOK, D), pp.reshape(NTOK, 256), [32, 32, 32, 32])
    in_maps = [host_inputs(x, p, W, rows) for (x, p, rows) in streams]
    if "nc" not in _NC_CACHE:
        _NC_CACHE["nc"] = build(NTOK)[0]
    res = run_bass_kernel_spmd(_NC_CACHE["nc"], in_maps, core_ids=list(range(8)))
    outs = [np.asarray(res.results[c]["y"], dtype=np.float32) for c in (0, 2, 4)]
    y_sample = np.stack([outs[0], outs[1]], axis=0)
    y_prompt = outs[2].reshape(4, 2048, D)
    return (y_prompt, y_sample)
```

```python
import os
from contextlib import ExitStack

import numpy as np
import concourse.bass as bass
import concourse.mybir as mybir
from concourse.bass_utils import run_bass_kernel_spmd

F32 = mybir.dt.float32
BF16 = mybir.dt.bfloat16
AF = mybir.ActivationFunctionType
OP = mybir.AluOpType

D = 2048
DC = 16
DFF = 5632
FC = 44
DIN = 7184
T = 512
EPS = 1e-6
NEG = -1000.0


class Buf:
    __slots__ = ("name", "w", "r", "dsem", "dcnt")

    def __init__(self, name):
        self.name = name
        self.w = {}
        self.r = {}
        self.dsem = None
        self.dcnt = 0


class Eng:
    def __init__(self, name, e, sem, inorder=False):
        self.name = name
        self.e = e
        self.sem = sem
        self.cnt = 0
        self.waited = {}
        self.inorder = inorder


def _merge(d, sem, val):
    k = id(sem)
    if k not in d or d[k][1] < val:
        d[k] = (sem, val)


class K:
    def __init__(self, nc, es):
        self.nc = nc
        self.es = es
        self.n_sem = 0
        self.pe = Eng("pe", nc.tensor, self.sem("pe"), inorder=True)
        self.act = Eng("act", nc.scalar, self.sem("act"))
        self.dve = Eng("dve", nc.vector, self.sem("dve"))
        self.pool = Eng("pool", nc.gpsimd, self.sem("pool"))
        self.sp = Eng("sp", nc.sync, self.sem("sp"), inorder=True)
        self.engs = [self.pe, self.act, self.dve, self.pool, self.sp]
        self.dbufs = []
        self.sem_pool = {"hw": [], "sw": []}
        self.n_inst = 0

    def sem(self, name):
        self.n_sem += 1
        return self.es.enter_context(self.nc.semaphore(name))

    def _wait(self, eng, reads, writes):
        deps = {}
        for b in reads:
            for s, v in b.w.values():
                _merge(deps, s, v)
        for b in writes:
            for s, v in b.w.values():
                _merge(deps, s, v)
            for s, v in b.r.values():
                _merge(deps, s, v)
        for s, v in deps.values():
            if s is eng.sem and eng.inorder:
                continue
            if eng.waited.get(id(s), 0) < v:
                eng.e.wait_ge(s, v)
                eng.waited[id(s)] = v
                self.n_inst += 1
                eng.ni = getattr(eng, 'ni', 0) + 1

    def op(self, eng, fns, reads=(), writes=()):
        self._wait(eng, reads, writes)
        if not isinstance(fns, (list, tuple)):
            fns = [fns]
        inst = None
        for f in fns:
            inst = f()
            self.n_inst += 1
            eng.ni = getattr(eng, 'ni', 0) + 1
        inst.then_inc(eng.sem, 1)
        eng.cnt += 1
        assert eng.cnt < 60000, eng.name
        for b in reads:
            _merge(b.r, eng.sem, eng.cnt)
        for b in writes:
            b.w = {id(eng.sem): (eng.sem, eng.cnt)}
            b.r = {}

    def dma(self, q, pairs, reads=(), writes=(), owner=None):
        kind = "sw" if q is self.pool else "hw"
        ent = owner.dsem.get(kind) if isinstance(owner.dsem, dict) else None
        if ent is None:
            if not isinstance(owner.dsem, dict):
                owner.dsem = {}
            if self.sem_pool[kind]:
                ent = list(self.sem_pool[kind].pop())
            else:
                ent = [self.sem(f"d{self.n_sem}_{kind}_" + owner.name), 0]
            owner.dsem[kind] = ent
            self.dbufs.append((owner, kind))
        self._wait(q, reads, writes)
        for out, in_ in pairs:
            q.e.dma_start(out=out, in_=in_).then_inc(ent[0], 16)
            ent[1] += 16
            self.n_inst += 1
            q.ni = getattr(q, 'ni', 0) + 1
            q.nd = getattr(q, 'nd', 0) + 1
        assert ent[1] < 60000, owner.name
        for b in reads:
            _merge(b.r, ent[0], ent[1])
        for b in writes:
            b.w = {id(ent[0]): (ent[0], ent[1])}
            b.r = {}

    def barrier(self):
        for e in self.engs:
            for o in self.engs:
                if o is e or o.cnt == 0:
                    continue
                if e.waited.get(id(o.sem), 0) < o.cnt:
                    e.e.wait_ge(o.sem, o.cnt)
                    e.waited[id(o.sem)] = o.cnt
            for b, kind in self.dbufs:
                s, c = b.dsem[kind]
                if c and e.waited.get(id(s), 0) < c:
                    e.e.wait_ge(s, c)
                    e.waited[id(s)] = c

    def end_phase(self, keep=()):
        self.barrier()
        kept = []
        for b, kind in self.dbufs:
            if b in keep:
                kept.append((b, kind))
            else:
                self.sem_pool[kind].append(tuple(b.dsem[kind]))
                del b.dsem[kind]
        self.dbufs = kept

    def sb(self, name, shape, dt):
        return self.es.enter_context(self.nc.sbuf_tensor(name, shape, dt))


def build(NTOK, debug=(), phases="0ABMC"):
    NT = NTOK // T
    NCH = NTOK // 128
    nc = bass.Bass("TRN2", target_bir_lowering=False)
    es = ExitStack()
    k = K(nc, es)

    def din(name, shape, dt=F32):
        return nc.dram_tensor(name, shape, dt, kind="ExternalInput").ap()

    def dscr(name, shape, dt):
        kind = "ExternalOutput" if name in debug else "Internal"
        return nc.dram_tensor(name, shape, dt, kind=kind).ap()

    xs = din("xs", [NTOK, D])
    pes = din("pes", [NTOK, 256])
    w1g = din("w1g", [D, DFF]); w1u = din("w1u", [D, DFF]); w1d = din("w1d", [DFF, D])
    w2g = din("w2g", [D, DFF]); w2u = din("w2u", [D, DFF]); w2d = din("w2d", [DFF, D])
    win = din("win", [D, DIN]); wout = din("wout", [D, D])
    wpg = din("wpg", [D, D]); wpp = din("wpp", [256, D])
    gv = din("gv", [128, 4 * DC])
    gsm = din("gsm", [128, 16])
    gbias = din("gbias", [128, 16])
    cst = din("cst", [128, 5 * 128])
    rpbg = din("rpbg", [8, 128, 9, 128])
    cmI = din("cmI", [128, 9, 128]); cmE = din("cmE", [128, 9, 128])
    NE = max(4, (NCH // 16) * 4)
    rb = din("rb", [128, NE, 9, 2])
    keep = din("keep", [128, 2, NCH])
    y = nc.dram_tensor("y", [NTOK, D], F32, kind="ExternalOutput").ap()

    WGU1 = dscr("WGU1", [22, 128, 2, 16, 256], BF16)
    WD1 = dscr("WD1", [16, 128, FC, 128], BF16)
    WGU2 = dscr("WGU2", [22, 128, 2, 16, 256], BF16)
    WD2 = dscr("WD2", [16, 128, FC, 128], BF16)
    WIN = dscr("WIN", [14, 128, 16, 512], BF16)
    WING = dscr("WING", [128, 16, 16], BF16)
    WOUT = dscr("WOUT", [4, 128, 16, 512], BF16)
    WPG = dscr("WPG", [4, 128, 16, 512], BF16)
    WPP = dscr("WPP", [128, 2, D], BF16)
    X1T = dscr("X1T", [128, DC, NTOK], F32)
    QAT = dscr("QAT", [8, 128, NTOK], BF16); KAT = dscr("KAT", [8, 128, NTOK], BF16)
    VA = dscr("VA", [NTOK, 1024], BF16)
    QMT = dscr("QMT", [8, 128, NTOK], BF16); KMT = dscr("KMT", [8, 128, NTOK], BF16)
    KM = dscr("KM", [NTOK, 1024], BF16); VM = dscr("VM", [NTOK, 1024], BF16)
    OGT = dscr("OGT", [8, 128, NTOK], BF16)
    GT = dscr("GT", [NTOK, 16], F32)
    HB = dscr("HB", [8, 128, NTOK], F32)
    YT = dscr("YT", [16, 128, NTOK], BF16)

    cst_sb = k.sb("cst_sb", [128, 5 * 128], F32)
    ident = cst_sb[:, 0:128]
    triF = cst_sb[:, 128:256]
    triB = cst_sb[:, 256:384]
    ones_f = cst_sb[:, 384:512]
    gv_sb = k.sb("gv_sb", [128, 4 * DC], F32)
    gsm_sb = k.sb("gsm_sb", [128, 16], F32)
    ones_bf = k.sb("ones_bf", [128, 128], BF16)
    b_cst = Buf("cst")
    k.dma(k.sp, [(cst_sb[:, :], cst[:, :]), (gv_sb[:, :], gv[:, :]), (gsm_sb[:, :], gsm[:, :])],
          writes=[b_cst], owner=b_cst)
    b_ones = Buf("ones")
    k.op(k.dve, lambda: nc.vector.tensor_copy(out=ones_bf[:, :], in_=cst_sb[:, 384:512]),
         reads=[b_cst], writes=[b_ones])

    psall = es.enter_context(nc.psum_tensor("psall", [128, 4096], F32))
    ps = [psall[:, i * 512:(i + 1) * 512] for i in range(8)]
    pb = [Buf(f"ps{i}") for i in range(8)]

    if "0" in phases:
      with ExitStack() as es0:
        def sb0(name, shape, dt):
            return es0.enter_context(nc.sbuf_tensor(name, shape, dt))
        NST = 3
        st_in = [sb0(f"c_in{i}", [128, DIN], F32) for i in range(NST)]
        st_out = [sb0(f"c_out{i}", [128, DIN], BF16) for i in range(NST)]
        bi = [Buf(f"c_in{i}") for i in range(NST)]
        bo = [Buf(f"c_out{i}") for i in range(NST)]
        state = {"i": 0}
        cast_engs = [k.act, k.dve, k.pool]

        def cast_rows(src_pairs_fn, ncols, dst_pairs_fn):
            i = state["i"] % NST
            e = cast_engs[state["i"] % 3]
            state["i"] += 1
            k.dma(k.sp, src_pairs_fn(st_in[i]), writes=[bi[i]], owner=bi[i])
            if e is k.act:
                fn = lambda: nc.scalar.copy(out=st_out[i][:, 0:ncols], in_=st_in[i][:, 0:ncols])
            elif e is k.dve:
                fn = lambda: nc.vector.tensor_copy(out=st_out[i][:, 0:ncols], in_=st_in[i][:, 0:ncols])
            else:
                fn = lambda: nc.gpsimd.tensor_copy(out=st_out[i][:, 0:ncols], in_=st_in[i][:, 0:ncols])
            k.op(e, fn, reads=[bi[i]], writes=[bo[i]])
            k.dma(k.pool, dst_pairs_fn(st_out[i]), reads=[bo[i]], owner=bo[i])

        def cast_gu(wg, wu, WGU):
            for gu, w in enumerate((wg, wu)):
                for kc in range(16):
                    cast_rows(lambda si, w=w, kc=kc: [(si[:, 0:DFF], w[kc * 128:(kc + 1) * 128, :])], DFF,
                              lambda so, gu=gu, kc=kc: [(
                                  WGU[a * 11:(a + 1) * 11, :, gu, kc, :].rearrange("n p f -> p n f"),
                                  so[:, a * 2816:(a + 1) * 2816].rearrange("p (n f) -> p n f", f=256)) for a in range(2)])

        def cast_d(wd, WD):
            for fc2 in range(FC // 2):
                def dst(so, fc2=fc2):
                    pairs = []
                    for a in range(2):
                        fc = fc2 * 2 + a
                        pairs.append((WD[:, :, fc, :].rearrange("c p d -> p c d"),
                                      so[:, a * 2048:(a + 1) * 2048].rearrange("p (c d) -> p c d", d=128)))
                    return pairs
                cast_rows(lambda si, fc2=fc2: [(si[:, 0:4096].rearrange("p (a d) -> p a d", a=2),
                                                wd[fc2 * 256:(fc2 + 1) * 256, :].rearrange("(a p) d -> p a d", p=128))],
                          4096, dst)

        def cast_panels512(w, WS, ncols_total):
            for kc in range(16):
                cast_rows(lambda si, kc=kc: [(si[:, 0:ncols_total], w[kc * 128:(kc + 1) * 128, 0:ncols_total])],
                          ncols_total,
                          lambda so, kc=kc: [(
                              WS[:, :, kc, :].rearrange("n p f -> p n f"),
                              so[:, 0:ncols_total].rearrange("p (n f) -> p n f", f=512))])

        cast_gu(w1g, w1u, WGU1)
        cast_d(w1d, WD1)
        cast_panels512(win, WIN, 7168)
        for kc in range(16):
            cast_rows(lambda si, kc=kc: [(si[:, 0:16], win[kc * 128:(kc + 1) * 128, 7168:7184])], 16,
                      lambda so, kc=kc: [(WING[:, kc, :], so[:, 0:16])])
        k.end_phase(keep=(b_cst,))

    class TB:
        pass

    def alloc_tiles(stk, pfx):
        tb = TB()
        sbt = lambda name, shape, dt: stk.enter_context(nc.sbuf_tensor(pfx + name, shape, dt))
        tb.xT = sbt("xT", [128, DC, T], F32)
        tb.xb = [Buf(f"xT{dc}") for dc in range(DC)]
        tb.hT = sbt("hT", [128, DC, T], BF16)
        tb.hb = Buf("hT")
        tb.act = sbt("act", [128, FC, T], BF16)
        tb.actb = [Buf(f"act{fc}") for fc in range(FC)]
        tb.sq = [sbt(f"sq{j}", [128, T], BF16) for j in range(2)]
        tb.sqb = [Buf(f"sq{j}") for j in range(2)]
        tb.sg = [sbt(f"sg{j}", [128, T], F32) for j in range(2)]
        tb.sgb = [Buf(f"sg{j}") for j in range(2)]
        tb.rstd = sbt("rstd", [128, T], F32)
        tb.rstdb = Buf("rstd")
        tb.xin = [sbt(f"xin{i}", [128, D], F32) for i in range(2)]
        tb.xinb = [Buf(f"xin{i}") for i in range(2)]
        tb.wslot = [sbt(f"wslot{i}", [128, 8192], BF16) for i in range(NSLOT)]
        tb.wsb = [Buf(f"wslot{i}") for i in range(NSLOT)]
        tb.wi = 0
        tb.stg = [sbt(f"stg{i}", [128, T], BF16) for i in range(4)]
        tb.stgb = [Buf(f"stg{i}") for i in range(4)]
        tb.stgf = [sbt(f"stgf{i}", [128, T], F32) for i in range(2)]
        tb.stgfb = [Buf(f"stgf{i}") for i in range(2)]
        tb.si = 0
        tb.fi = 0
        return tb

    NSLOT = 4

    def wload(tb, pairs_fn):
        i = tb.wi % NSLOT
        tb.wi += 1
        k.dma(k.sp, pairs_fn(tb.wslot[i]), writes=[tb.wsb[i]], owner=tb.wsb[i])
        return tb.wslot[i], tb.wsb[i]

    def rmsnorm_T(tb, gcol):
        xT, xb, hT, sq, sqb, rstd, rstdb = tb.xT, tb.xb, tb.hT, tb.sq, tb.sqb, tb.rstd, tb.rstdb
        act, actb = tb.act, tb.actb
        H = DC // 2
        k.op(k.act, lambda: nc.scalar.activation(out=act[:, 0:H, :], in_=xT[:, 0:H, :], func=AF.Square),
             reads=list(xb[0:H]), writes=list(actb[0:H]))
        k.op(k.dve, lambda: nc.vector.tensor_tensor(out=act[:, H:DC, :], in0=xT[:, H:DC, :], in1=xT[:, H:DC, :], op=OP.mult),
             reads=list(xb[H:DC]), writes=list(actb[H:DC]))
        k.op(k.pe, [lambda dc=dc: nc.tensor.matmul(out=ps[0], lhsT=ones_bf[:, :], rhs=act[:, dc, :],
                                                   start=(dc == 0), stop=(dc == DC - 1)) for dc in range(DC)],
             reads=list(actb[0:DC]) + [b_ones], writes=[pb[0]])
        k.op(k.act, lambda: nc.scalar.activation(out=rstd[:, :], in_=ps[0], func=AF.Sqrt,
                                                 scale=1.0 / D, bias=eps_sb[:, 0:1]),
             reads=[pb[0], b_eps], writes=[rstdb])
        k.op(k.dve, lambda: nc.vector.reciprocal(out=rstd[:, :], in_=rstd[:, :]), reads=[rstdb], writes=[rstdb])
        fns = []
        for dc in range(DC):
            fns.append(lambda dc=dc: nc.vector.scalar_tensor_tensor(
                out=hT[:, dc, :], in0=xT[:, dc, :], scalar=gv_sb[:, gcol + dc:gcol + dc + 1], in1=rstd[:, :],
                op0=OP.mult, op1=OP.mult))
        k.op(k.dve, fns, reads=list(xb) + [rstdb, b_cst], writes=[tb.hb])

    eps_sb = k.sb("eps_sb", [128, 4], F32)
    b_eps = Buf("eps")
    k.op(k.dve, [lambda: nc.vector.memset(eps_sb[:, 0:1], EPS), lambda: nc.vector.memset(eps_sb[:, 1:2], 128 * EPS),
                 lambda: nc.vector.memset(eps_sb[:, 2:3], 1.0)], writes=[b_eps])


    def bg_pieces():
        for kc in range(16):
            yield (lambda si, kc=kc: [(si[:, 0:2048], wout[kc * 128:(kc + 1) * 128, :])], 2048,
                   lambda so, kc=kc: [(WOUT[:, :, kc, :].rearrange("n p f -> p n f"), so[:, 0:2048].rearrange("p (n f) -> p n f", f=512))])
        for gu, w in enumerate((w2g, w2u)):
            for kc in range(16):
                for (n0, n1) in ((0, 8), (8, 16), (16, 22)):
                    nc_ = (n1 - n0) * 256
                    yield (lambda si, w=w, kc=kc, n0=n0, nc_=nc_: [(si[:, 0:nc_], w[kc * 128:(kc + 1) * 128, n0 * 256:n0 * 256 + nc_])], nc_,
                           lambda so, gu=gu, kc=kc, n0=n0, n1=n1, nc_=nc_: [(
                               WGU2[n0:n1, :, gu, kc, :].rearrange("n p f -> p n f"), so[:, 0:nc_].rearrange("p (n f) -> p n f", f=256))])
        for fc in range(FC):
            yield (lambda si, fc=fc: [(si[:, 0:2048], w2d[fc * 128:(fc + 1) * 128, :])], 2048,
                   lambda so, fc=fc: [(WD2[:, :, fc, :].rearrange("c p d -> p c d"), so[:, 0:2048].rearrange("p (c d) -> p c d", d=128))])
        for kc in range(16):
            yield (lambda si, kc=kc: [(si[:, 0:2048], wpg[kc * 128:(kc + 1) * 128, :])], 2048,
                   lambda so, kc=kc: [(WPG[:, :, kc, :].rearrange("n p f -> p n f"), so[:, 0:2048].rearrange("p (n f) -> p n f", f=512))])
        for k2 in range(2):
            yield (lambda si, k2=k2: [(si[:, 0:2048], wpp[k2 * 128:(k2 + 1) * 128, :])], 2048,
                   lambda so, k2=k2: [(WPP[:, k2, :], so[:, 0:2048])])

    class BG:
        pass
    bgs = BG()
    bgs.gen = None

    def bg_init(stk):
        bgs.cin = stk.enter_context(nc.sbuf_tensor("bg_in", [128, 2048], F32))
        bgs.cout = stk.enter_context(nc.sbuf_tensor("bg_out", [128, 2048], BF16))
        bgs.bi = Buf("bg_in"); bgs.bo = Buf("bg_out")
        bgs.gen = bg_pieces()
        bgs.pending = None
        bgs.n = 0

    def bg_step():
        if bgs.gen is None:
            return
        if bgs.pending is not None:
            ncols, dstf = bgs.pending
            e = (k.act, k.dve, k.pool)[bgs.n % 3]
            bgs.n += 1
            if e is k.act:
                fn = lambda: nc.scalar.copy(out=bgs.cout[:, 0:ncols], in_=bgs.cin[:, 0:ncols])
            elif e is k.dve:
                fn = lambda: nc.vector.tensor_copy(out=bgs.cout[:, 0:ncols], in_=bgs.cin[:, 0:ncols])
            else:
                fn = lambda: nc.gpsimd.tensor_copy(out=bgs.cout[:, 0:ncols], in_=bgs.cin[:, 0:ncols])
            k.op(e, fn, reads=[bgs.bi], writes=[bgs.bo])
            k.dma(k.pool, dstf(bgs.cout), reads=[bgs.bo], owner=bgs.bo)
            bgs.pending = None
        nxt = next(bgs.gen, None)
        if nxt is None:
            bgs.gen = None
            return
        srcf, ncols, dstf = nxt
        k.dma(k.sp, srcf(bgs.cin), writes=[bgs.bi], owner=bgs.bi)
        bgs.pending = (ncols, dstf)

    def ffn(tb, WGU, WD):
        hT, hb, act, actb, xT, xb, sg, sgb = tb.hT, tb.hb, tb.act, tb.actb, tb.xT, tb.xb, tb.sg, tb.sgb
        def load_gu(n):
            return wload(tb, lambda s, n=n: [(s[:, :], WGU[n].rearrange("p a c f -> p (a c f)"))])
        nxt = load_gu(0)
        for n in range(22):
            cur = nxt
            if n + 1 < 22:
                nxt = load_gu(n + 1)
            wt, wb = cur
            wv = wt[:, :].rearrange("p (a c f) -> p a c f", a=2, c=16)
            bg_step()
            for f2 in range(2):
                fc = n * 2 + f2
                gbank, ubank = 2 + (fc % 2) * 2, 3 + (fc % 2) * 2
                for gu, bank in ((0, gbank), (1, ubank)):
                    fns = [lambda kc=kc, gu=gu, bank=bank, f2=f2: nc.tensor.matmul(
                        out=ps[bank], lhsT=wv[:, gu, kc, f2 * 128:(f2 + 1) * 128], rhs=hT[:, kc, :],
                        start=(kc == 0), stop=(kc == 15)) for kc in range(16)]
                    k.op(k.pe, fns, reads=[wb, hb], writes=[pb[bank]])
                j = fc % 2
                k.op(k.act, lambda j=j, gbank=gbank: nc.scalar.activation(out=sg[j][:, :], in_=ps[gbank], func=AF.Silu),
                     reads=[pb[gbank]], writes=[sgb[j]])
                k.op(k.dve, lambda j=j, ubank=ubank, fc=fc: nc.vector.tensor_tensor(
                    out=act[:, fc, :], in0=ps[ubank], in1=sg[j][:, :], op=OP.mult),
                     reads=[pb[ubank], sgb[j]], writes=[actb[fc]])
        def load_d(dc):
            return wload(tb, lambda s, dc=dc: [(s[:, 0:FC * 128], WD[dc].rearrange("p c d -> p (c d)"))])
        nxt = load_d(0)
        for dc in range(DC):
            cur = nxt
            if dc + 1 < DC:
                nxt = load_d(dc + 1)
            wt, wb = cur
            bank = 6 + dc % 2
            fns = [lambda fc=fc, bank=bank, wt=wt: nc.tensor.matmul(
                out=ps[bank], lhsT=wt[:, fc * 128:(fc + 1) * 128], rhs=act[:, fc, :],
                start=(fc == 0), stop=(fc == FC - 1)) for fc in range(FC)]
            k.op(k.pe, fns, reads=[wb] + actb, writes=[pb[bank]])
            k.op(k.dve, lambda dc=dc, bank=bank: nc.vector.scalar_tensor_tensor(
                out=xT[:, dc, :], in0=ps[bank], scalar=0.5, in1=xT[:, dc, :], op0=OP.mult, op1=OP.add),
                 reads=[pb[bank]], writes=[xb[dc]])

    def load_xT_from_tokmajor(tb, src, t):
        xin, xinb, xT, xb = tb.xin, tb.xinb, tb.xT, tb.xb
        for s in range(4):
            i = s % 2
            r0 = t * T + s * 128
            k.dma(k.sp, [(xin[i][:, :], src[r0:r0 + 128, :])], writes=[xinb[i]], owner=xinb[i])
            for q in range(4):
                bank = q % 2
                fns = [lambda dc=dc, j=j, i=i, bank=bank: nc.tensor.transpose(
                    out=ps[bank][:, j * 128:(j + 1) * 128], in_=xin[i][:, dc * 128:(dc + 1) * 128], identity=ident)
                    for j, dc in enumerate(range(q * 4, q * 4 + 4))]
                k.op(k.pe, fns, reads=[xinb[i], b_cst], writes=[pb[bank]])
                k.op(k.act, lambda q=q, s=s, bank=bank: nc.scalar.copy(
                    out=xT[:, q * 4:q * 4 + 4, s * 128:(s + 1) * 128],
                    in_=ps[bank].rearrange("p (j t) -> p j t", j=4)),
                     reads=[pb[bank]], writes=[xb[dc] for dc in range(q * 4, q * 4 + 4)])

    if "A" in phases:
      with ExitStack() as esA:
        tb = alloc_tiles(esA, "A_")
        bg_init(esA)
        hT, hb, sq, sqb, rstd, rstdb = tb.hT, tb.hb, tb.sq, tb.sqb, tb.rstd, tb.rstdb
        stg, stgb, stgf, stgfb = tb.stg, tb.stgb, tb.stgf, tb.stgfb
        for t in range(NT):
            tk = slice(t * T, (t + 1) * T)
            load_xT_from_tokmajor(tb, xs, t)
            rmsnorm_T(tb, 0)
            ffn(tb, WGU1, WD1)
            k.dma(k.pool, [(X1T[:, :, tk], tb.xT[:, :, :])], reads=tb.xb, owner=tb.xb[0])
            rmsnorm_T(tb, DC)
            pend = []
            FM = {0: "qa", 1: "qa", 2: "ka", 3: "ka", 6: "qm", 7: "qm", 8: "km", 9: "km", 12: "og", 13: "og"}
            TM = {4: "va", 5: "va", 8: "km", 9: "km", 10: "vm", 11: "vm"}
            for n in range(14):
                wt, wb = wload(tb, lambda s, n=n: [(s[:, :], WIN[n].rearrange("p c f -> p (c f)"))])
                wv = wt[:, :].rearrange("p (c f) -> p c f", c=16)
                if n in FM:
                    kind = FM[n]
                    for j in range(4):
                        oc = n * 4 + j
                        bank = 6 + oc % 2
                        fns = [lambda kc=kc, j=j, bank=bank: nc.tensor.matmul(
                            out=ps[bank], lhsT=wv[:, kc, j * 128:(j + 1) * 128], rhs=hT[:, kc, :],
                            start=(kc == 0), stop=(kc == 15)) for kc in range(16)]
                        k.op(k.pe, fns, reads=[wb, hb], writes=[pb[bank]])
                        while pend:
                            pend.pop(0)()
                        si = tb.si % 4
                        tb.si += 1
                        if kind in ("qa", "ka"):
                            fi = tb.fi % 2
                            tb.fi += 1
                            sj = fi
                            k.op(k.act, lambda fi=fi, bank=bank: nc.scalar.copy(out=stgf[fi][:, :], in_=ps[bank]),
                                 reads=[pb[bank]], writes=[stgfb[fi]])
                            k.op(k.act, lambda bank=bank, sj=sj: nc.scalar.activation(out=sq[sj][:, :], in_=ps[bank], func=AF.Square),
                                 reads=[pb[bank]], writes=[sqb[sj]])

                            def fin(kind=kind, fi=fi, si=si, sj=sj, oc=oc):
                                k.op(k.pe, lambda: nc.tensor.matmul(out=ps[1], lhsT=ones_bf[:, :], rhs=sq[sj][:, :],
                                                                    start=True, stop=True),
                                     reads=[sqb[sj], b_ones], writes=[pb[1]])
                                if kind == "qa":
                                    k.op(k.act, lambda: nc.scalar.activation(out=rstd[:, :], in_=ps[1], func=AF.Sqrt,
                                                                             scale=1.0, bias=eps_sb[:, 1:2]),
                                         reads=[pb[1], b_eps], writes=[rstdb])
                                else:
                                    k.op(k.act, lambda: nc.scalar.activation(out=rstd[:, :], in_=ps[1], func=AF.Sqrt,
                                                                             scale=1.0 / 128, bias=eps_sb[:, 0:1]),
                                         reads=[pb[1], b_eps], writes=[rstdb])
                                k.op(k.dve, lambda: nc.vector.reciprocal(out=rstd[:, :], in_=rstd[:, :]),
                                     reads=[rstdb], writes=[rstdb])
                                gc = 0 if kind == "qa" else 1
                                k.op(k.dve, lambda: nc.vector.scalar_tensor_tensor(
                                    out=stg[si][:, :], in0=stgf[fi][:, :], scalar=gsm_sb[:, gc:gc + 1], in1=rstd[:, :],
                                    op0=OP.mult, op1=OP.mult),
                                     reads=[stgfb[fi], rstdb, b_cst], writes=[stgb[si]])
                                dst_ = (QAT if kind == "qa" else KAT)[oc % 8, :, tk]
                                k.dma(k.pool, [(dst_, stg[si][:, :])], reads=[stgb[si]], owner=stgb[si])
                            pend.append(fin)
                            continue
                        elif kind == "og":
                            k.op(k.act, lambda si=si, bank=bank: nc.scalar.activation(out=stg[si][:, :], in_=ps[bank], func=AF.Sigmoid),
                                 reads=[pb[bank]], writes=[stgb[si]])
                            dst = OGT[oc % 8, :, tk]
                        elif kind == "km":
                            k.op(k.act, lambda si=si, bank=bank: nc.scalar.mul(out=stg[si][:, :], in_=ps[bank], mul=0.0625),
                                 reads=[pb[bank]], writes=[stgb[si]])
                            dst = KMT[oc % 8, :, tk]
                        else:
                            k.op(k.act, lambda si=si, bank=bank: nc.scalar.copy(out=stg[si][:, :], in_=ps[bank]),
                                 reads=[pb[bank]], writes=[stgb[si]])
                            dst = QMT[oc % 8, :, tk]
                        k.dma(k.pool, [(dst, stg[si][:, :])], reads=[stgb[si]], owner=stgb[si])
                if n in TM:
                    while pend:
                        pend.pop(0)()
                    kind = TM[n]
                    half = n % 2
                    dstT = {"va": VA, "km": KM, "vm": VM}[kind]
                    for s in range(4):
                        bank = 4 + s % 2
                        fns = [lambda kc=kc, s=s, bank=bank: nc.tensor.matmul(
                            out=ps[bank], lhsT=hT[:, kc, s * 128:(s + 1) * 128], rhs=wv[:, kc, :],
                            start=(kc == 0), stop=(kc == 15)) for kc in range(16)]
                        k.op(k.pe, fns, reads=[wb, hb], writes=[pb[bank]])
                        si = tb.si % 4
                        tb.si += 1
                        if kind == "km":
                            k.op(k.dve, lambda si=si, bank=bank: nc.vector.tensor_scalar(
                                out=stg[si][:, :], in0=ps[bank], scalar1=0.0625, scalar2=None, op0=OP.mult),
                                 reads=[pb[bank]], writes=[stgb[si]])
                        else:
                            k.op(k.dve, lambda si=si, bank=bank: nc.vector.tensor_copy(out=stg[si][:, :], in_=ps[bank]),
                                 reads=[pb[bank]], writes=[stgb[si]])
                        r0 = t * T + s * 128
                        k.dma(k.pool, [(dstT[r0:r0 + 128, half * 512:(half + 1) * 512], stg[si][:, :])],
                              reads=[stgb[si]], owner=stgb[si])
            while pend:
                pend.pop(0)()
            wt, wb = wload(tb, lambda s: [(s[:, 0:256], WING.rearrange("p c g -> p (c g)"))])
            wv = wt[:, 0:256].rearrange("p (c g) -> p c g", c=16)
            for s in range(4):
                bank = 4 + s % 2
                fns = [lambda kc=kc, s=s, bank=bank: nc.tensor.matmul(
                    out=ps[bank][:, 0:16], lhsT=hT[:, kc, s * 128:(s + 1) * 128], rhs=wv[:, kc, :],
                    start=(kc == 0), stop=(kc == 15)) for kc in range(16)]
                k.op(k.pe, fns, reads=[wb, hb], writes=[pb[bank]])
                fi = tb.fi % 2
                tb.fi += 1
                k.op(k.dve, lambda fi=fi, bank=bank: nc.vector.tensor_copy(out=stgf[fi][:, 0:16], in_=ps[bank][:, 0:16]),
                     reads=[pb[bank]], writes=[stgfb[fi]])
                r0 = t * T + s * 128
                k.dma(k.pool, [(GT[r0:r0 + 128, :], stgf[fi][:, 0:16])], reads=[stgfb[fi]], owner=stgfb[fi])
        while bgs.gen is not None or bgs.pending is not None:
            bg_step()
        bgs.gen = None
        k.end_phase(keep=(b_cst,))


    if "B" in phases:
      with ExitStack() as esB:
        sbt = lambda name, shape, dt: esB.enter_context(nc.sbuf_tensor(name, shape, dt))
        QT = [sbt(f"naQ{i}", [128, NTOK], BF16) for i in range(2)]
        KT = [sbt(f"naK{i}", [128, NTOK], BF16) for i in range(2)]
        VV = [sbt(f"naV{i}", [128, NCH, 128], BF16) for i in range(2)]
        YA = [sbt(f"naY{i}", [128, NTOK], BF16) for i in range(2)]
        RPB = [sbt(f"naR{i}", [128, 9, 128], F32) for i in range(2)]
        BMI = [sbt(f"naBI{i}", [128, 9, 128], F32) for i in range(2)]
        BME = [sbt(f"naBE{i}", [128, 9, 128], F32) for i in range(2)]
        cmI_sb = sbt("cmI_sb", [128, 9, 128], F32)
        cmE_sb = sbt("cmE_sb", [128, 9, 128], F32)
        rb_sb = sbt("rb_sb", [128, NE, 9, 2], F32)
        S1 = [sbt(f"naS{i}", [128, 9, 128], F32) for i in range(2)]
        PT = [sbt(f"naP{i}", [128, 9, 128], BF16) for i in range(2)]
        REC = [sbt(f"naRec{i}", [128, 128], F32) for i in range(2)]
        b_in = [Buf(f"naIn{i}") for i in range(2)]
        b_ya = [Buf(f"naY{i}") for i in range(2)]
        b_rp = [Buf(f"naR{i}") for i in range(2)]
        b_bm = [Buf(f"naBM{i}") for i in range(2)]
        b_cm = Buf("naCM")
        b_s1 = [Buf(f"naS{i}") for i in range(2)]
        b_pt = [Buf(f"naP{i}") for i in range(2)]
        b_rec = [Buf(f"naRec{i}") for i in range(2)]
        k.dma(k.sp, [(cmI_sb[:, :, :], cmI[:, :, :]), (cmE_sb[:, :, :], cmE[:, :, :]), (rb_sb[:, :, :, :], rb[:, :, :, :])],
              writes=[b_cm], owner=b_cm)
        epos = {0: 0, 1: 1, 14: 2, 15: 3}

        def na_load(h):
            i = h % 2
            prs = [(QT[i][:, :], QAT[h]), (KT[i][:, :], KAT[h])]
            for n0 in range(0, NCH, 16):
                prs.append((VV[i][:, n0:n0 + 16, :],
                            VA[n0 * 128:(n0 + 16) * 128, h * 128:(h + 1) * 128].rearrange("(n p) d -> p n d", p=128)))
            k.dma(k.sp, prs, writes=[b_in[i]], owner=b_in[i])
            k.dma(k.sp, [(RPB[i][:, :, :], rpbg[h])], writes=[b_rp[i]], owner=b_rp[i])

        na_load(0)
        for h in range(8):
            i = h % 2
            if h + 1 < 8:
                na_load(h + 1)
            k.op(k.pool, [lambda i=i: nc.gpsimd.tensor_tensor(out=BMI[i][:, :, :], in0=RPB[i][:, :, :], in1=cmI_sb[:, :, :], op=OP.add),
                          lambda i=i: nc.gpsimd.tensor_tensor(out=BME[i][:, :, :], in0=RPB[i][:, :, :], in1=cmE_sb[:, :, :], op=OP.add)],
                 reads=[b_rp[i], b_cm], writes=[b_bm[i]])
            def rng_(c):
                return max(0, 4 - c), min(8, NCH - 1 - c + 4)

            def na_S(c, i=i):
                j = c % 2
                o_lo, o_hi = rng_(c)
                sbanks = [pb[3 * j], pb[3 * j + 1], pb[3 * j + 2]]
                Sps = psall[:, j * 1536:j * 1536 + 1152].rearrange("p (o q) -> p o q", o=9)
                fns = [lambda o=o: nc.tensor.matmul(
                    out=Sps[:, o, :], lhsT=KT[i][:, (c + o - 4) * 128:(c + o - 3) * 128], rhs=QT[i][:, c * 128:(c + 1) * 128],
                    start=True, stop=True) for o in range(o_lo, o_hi + 1)]
                k.op(k.pe, fns, reads=[b_in[i]], writes=sbanks)

            def na_soft(c, i=i):
                j = c % 2
                o_lo, o_hi = rng_(c)
                no = o_hi - o_lo + 1
                sbanks = [pb[3 * j], pb[3 * j + 1], pb[3 * j + 2]]
                Sps = psall[:, j * 1536:j * 1536 + 1152].rearrange("p (o q) -> p o q", o=9)
                edge = (c % 16) in epos
                BM = BME[i] if edge else BMI[i]
                k.op(k.dve, lambda: nc.vector.tensor_tensor(
                    out=S1[j][:, o_lo:o_hi + 1, :], in0=Sps[:, o_lo:o_hi + 1, :], in1=BM[:, o_lo:o_hi + 1, :], op=OP.add),
                     reads=sbanks + [b_bm[i]], writes=[b_s1[j]])
                if edge:
                    e = (c // 16) * 4 + epos[c % 16]
                    k.op(k.pool, lambda: nc.gpsimd.tensor_tensor(
                        out=S1[j][:, o_lo:o_hi + 1, :].rearrange("p o (b q) -> p o b q", b=2),
                        in0=S1[j][:, o_lo:o_hi + 1, :].rearrange("p o (b q) -> p o b q", b=2),
                        in1=rb_sb[:, e, o_lo:o_hi + 1, :].unsqueeze(3).to_broadcast([128, no, 2, 64]), op=OP.add),
                         reads=[b_cm], writes=[b_s1[j]])
                k.op(k.act, lambda: nc.scalar.activation(
                    out=PT[j][:, o_lo:o_hi + 1, :], in_=S1[j][:, o_lo:o_hi + 1, :], func=AF.Exp),
                     reads=[b_s1[j]], writes=[b_pt[j]])

            def na_PV(c, i=i):
                j = c % 2
                o_lo, o_hi = rng_(c)
                ob = 6 + j
                fns = []
                for o in range(o_lo, o_hi + 1):
                    fns.append(lambda o=o: nc.tensor.matmul(
                        out=ps[ob][:, 0:128], lhsT=VV[i][:, c + o - 4, :], rhs=PT[j][:, o, :],
                        start=(o == o_lo), stop=(o == o_hi)))
                for o in range(o_lo, o_hi + 1):
                    fns.append(lambda o=o: nc.tensor.matmul(
                        out=ps[ob][:, 128:256], lhsT=ones_bf[:, :], rhs=PT[j][:, o, :],
                        start=(o == o_lo), stop=(o == o_hi)))
                k.op(k.pe, fns, reads=[b_in[i], b_pt[j], b_ones], writes=[pb[ob]])

            def na_fin(c, i=i):
                j = c % 2
                ob = 6 + j
                k.op(k.dve, lambda: nc.vector.reciprocal(out=REC[j][:, :], in_=ps[ob][:, 128:256]),
                     reads=[pb[ob]], writes=[b_rec[j]])
                k.op(k.dve, lambda: nc.vector.tensor_tensor(
                    out=YA[i][:, c * 128:(c + 1) * 128], in0=ps[ob][:, 0:128], in1=REC[j][:, :], op=OP.mult),
                     reads=[pb[ob], b_rec[j]], writes=[b_ya[i]])

            na_S(0)
            for c in range(NCH):
                if c + 1 < NCH:
                    na_S(c + 1)
                na_soft(c)
                if c >= 1:
                    na_fin(c - 1)
                na_PV(c)
            na_fin(NCH - 1)
            k.dma(k.sp, [(YT[h], YA[i][:, :])], reads=[b_ya[i]], owner=b_ya[i])
        k.end_phase(keep=(b_cst,))


    if "M" in phases:
      with ExitStack() as esM:
        sbt = lambda name, shape, dt: esM.enter_context(nc.sbuf_tensor(name, shape, dt))
        NG = NCH * 8
        G_sb = sbt("G_sb", [128, NCH, 16], F32)
        gbias_sb = sbt("gbias_sb", [128, 16], F32)
        keep_sb = sbt("keep_sb", [128, 2, NCH], F32)
        LF = sbt("LF", [128, NCH, 8], F32)
        BS = sbt("BS", [128, NCH, 8], F32)
        IB = sbt("IB", [128, NCH, 8], F32)
        KS = sbt("KS", [128, NCH, 8], F32)
        DEC = sbt("DEC", [128, NCH, 8], F32)
        b_g = Buf("mG"); b_lf = Buf("mLF"); b_bs = Buf("mBS"); b_ib = Buf("mIB"); b_ks = Buf("mKS"); b_dec = Buf("mDEC")
        b_gb = Buf("mGB")
        k.dma(k.sp, [(G_sb[:, n0:n0 + 16, :], GT[n0 * 128:(n0 + 16) * 128, :].rearrange("(n p) g -> p n g", p=128))
                     for n0 in range(0, NCH, 16)], writes=[b_g], owner=b_g)
        k.dma(k.sp, [(gbias_sb[:, :], gbias[:, :]), (keep_sb[:, :, :], keep[:, :, :])], writes=[b_gb], owner=b_gb)
        k.op(k.dve, lambda: nc.vector.tensor_tensor(out=G_sb[:, :, :], in0=G_sb[:, :, :],
                                                    in1=gbias_sb[:, :].unsqueeze(1).to_broadcast([128, NCH, 16]), op=OP.add),
             reads=[b_gb], writes=[b_g])
        k.op(k.act, lambda: nc.scalar.activation(out=LF[:, :, :], in_=G_sb[:, :, 8:16], func=AF.Exp, scale=-1.0),
             reads=[b_g], writes=[b_lf])
        k.op(k.act, lambda: nc.scalar.activation(out=LF[:, :, :], in_=LF[:, :, :], func=AF.Ln, bias=eps_sb[:, 2:3]),
             reads=[b_lf, b_eps], writes=[b_lf])
        k.op(k.dve, lambda: nc.vector.tensor_scalar(out=LF[:, :, :], in0=LF[:, :, :], scalar1=-1.0, scalar2=None, op0=OP.mult),
             reads=[b_lf], writes=[b_lf])
        LF2 = LF[:, :, :].rearrange("p n g -> p (n g)")
        k.op(k.pe, lambda: nc.tensor.matmul(out=ps[0][:, 0:NG], lhsT=triF, rhs=LF2, start=True, stop=True),
             reads=[b_lf, b_cst], writes=[pb[0]])
        k.op(k.pe, lambda: nc.tensor.matmul(out=ps[1][:, 0:NG], lhsT=triB, rhs=LF2, start=True, stop=True),
             reads=[b_lf, b_cst], writes=[pb[1]])
        k.op(k.pe, lambda: nc.tensor.matmul(out=ps[2][:, 0:NG], lhsT=ones_f, rhs=LF2, start=True, stop=True),
             reads=[b_lf, b_cst], writes=[pb[2]])
        k.op(k.dve, [lambda: nc.vector.tensor_copy(out=BS[:, :, 0:4], in_=ps[0][:, 0:NG].rearrange("p (n g) -> p n g", g=8)[:, :, 0:4]),
                     lambda: nc.vector.tensor_copy(out=BS[:, :, 4:8], in_=ps[1][:, 0:NG].rearrange("p (n g) -> p n g", g=8)[:, :, 4:8])],
             reads=[pb[0], pb[1]], writes=[b_bs])
        k.op(k.dve, lambda: nc.vector.tensor_tensor(out=IB[:, :, :], in0=G_sb[:, :, 0:8], in1=BS[:, :, :], op=OP.subtract),
             reads=[b_g, b_bs], writes=[b_ib])
        k.op(k.act, lambda: nc.scalar.activation(out=KS[:, :, :], in_=IB[:, :, :], func=AF.Exp), reads=[b_ib], writes=[b_ks])
        k.op(k.act, lambda: nc.scalar.activation(out=DEC[:, :, :].rearrange("p n g -> p (n g)"), in_=ps[2][:, 0:NG], func=AF.Exp),
             reads=[pb[2]], writes=[b_dec])
        k.op(k.dve, [lambda d=d: nc.vector.tensor_tensor(
            out=DEC[:, :, d * 4:(d + 1) * 4], in0=DEC[:, :, d * 4:(d + 1) * 4],
            in1=keep_sb[:, d, :].unsqueeze(2).to_broadcast([128, NCH, 4]), op=OP.mult) for d in range(2)],
             reads=[b_gb], writes=[b_dec])

        QT2 = sbt("mQT", [128, 2, NTOK], BF16)
        KT2 = sbt("mKT", [128, 2, NTOK], BF16)
        Ktok = sbt("mK", [128, NCH, 256], BF16)
        Vaug = sbt("mV", [128, NCH, 264], BF16)
        b_hd = Buf("mHead")
        b_v1 = Buf("mVones")
        k.op(k.pool, lambda: nc.gpsimd.memset(Vaug[:, :, 256:264], 1.0), writes=[b_v1])

        class DS:
            pass
        dd = []
        for d in range(2):
            s_ = DS()
            nm = lambda x, d=d: f"m{x}{d}"
            def two(name, shape, dt, s_=s_, nm=nm):
                setattr(s_, name, [sbt(nm(name) + f"_{q}", shape, dt) for q in range(2)])
                setattr(s_, "b_" + name, [Buf(nm(name) + f"_{q}") for q in range(2)])
            def one(name, shape, dt, s_=s_, nm=nm):
                setattr(s_, name, sbt(nm(name), shape, dt))
                setattr(s_, "b_" + name, Buf(nm(name)))
            two("diag", [128, 128], F32); two("DT", [128, 128], F32); two("EB", [128, 128], F32)
            two("DTm", [128, 128], F32); two("SD", [128, 128], BF16); two("Qp", [128, 2, 128], BF16)
            two("Kp", [128, 256], BF16); two("dcl", [128, 128], F32); two("hbuf", [128, 2, 128], F32)
            two("hl", [128, 2, 128], F32); two("og", [128, 2, 128], BF16); two("sqh", [128, 2, 128], BF16)
            one("rs", [128, 128], F32); one("tmp", [128, 2, 128], F32); one("ym", [128, 2, 128], BF16)
            one("U32", [128, 2, 256], F32); two("Ubf", [128, 2, 256], BF16)
            one("n32", [128, 2], F32); two("nB", [128, 2, 128], BF16)
            s_.p_a = pb[4 * d]; s_.p_nd = pb[4 * d + 1]; s_.p_du = [pb[4 * d + 2], pb[4 * d + 3]]
            dd.append(s_)
        hbb = {}

        def geom(i, d):
            n = i if d == 0 else NCH - 1 - i
            nprev = n - 1 if d == 0 else n + 1
            first = (n < NCH // 2) if d == 0 else (n >= NCH // 2)
            return n, nprev, first

        def m_prep(h, i, d):
            s_ = dd[d]; q = i % 2
            n, nprev, first = geom(i, d)
            g = d * 4 + h
            ck = slice(n * 128, (n + 1) * 128)
            A = ps[4 * d]
            Bt = A[:, 0:128]
            Sp = A[:, 128:256]
            mask = triF if d == 0 else triB
            k.op(k.pool, lambda: nc.gpsimd.tensor_scalar(
                out=s_.diag[q][:, :], in0=ident, scalar1=BS[:, n, g:g + 1], scalar2=1.0, op0=OP.mult, op1=OP.mult),
                 reads=[b_bs, b_cst], writes=[s_.b_diag[q]])
            yield
            k.op(k.pe, lambda: nc.tensor.matmul(out=Bt, lhsT=ones_f, rhs=s_.diag[q][:, :], start=True, stop=True),
                 reads=[s_.b_diag[q], b_cst], writes=[s_.p_a])
            yield
            k.op(k.pe, [lambda dkc=dkc: nc.tensor.matmul(
                out=Sp, lhsT=KT2[:, dkc, ck], rhs=QT2[:, dkc, ck], start=(dkc == 0), stop=(dkc == 1))
                for dkc in range(2)], reads=[b_hd], writes=[s_.p_a])
            yield
            k.op(k.act, lambda: nc.scalar.activation(out=s_.DT[q][:, :], in_=Bt, func=AF.Exp, bias=IB[:, n, g:g + 1]),
                 reads=[s_.p_a, b_ib], writes=[s_.b_DT[q]])
            yield
            if i > 0:
                k.op(k.act, lambda: nc.scalar.activation(out=s_.EB[q][:, :], in_=Bt, func=AF.Exp),
                     reads=[s_.p_a], writes=[s_.b_EB[q]])
                yield
            k.op(k.pool, lambda: nc.gpsimd.tensor_tensor(out=s_.DTm[q][:, :], in0=s_.DT[q][:, :], in1=mask, op=OP.mult),
                 reads=[s_.b_DT[q], b_cst], writes=[s_.b_DTm[q]])
            yield
            k.op(k.dve, lambda: nc.vector.tensor_tensor(out=s_.SD[q][:, :], in0=Sp, in1=s_.DTm[q][:, :], op=OP.mult),
                 reads=[s_.p_a, s_.b_DTm[q]], writes=[s_.b_SD[q]])
            yield
            if i > 0:
                k.op(k.dve, lambda: nc.vector.scalar_tensor_tensor(
                    out=s_.Qp[q][:, :, :], in0=QT2[:, :, ck], scalar=DEC[:, nprev, g:g + 1],
                    in1=s_.EB[q][:, :].unsqueeze(1).to_broadcast([128, 2, 128]), op0=OP.mult, op1=OP.mult),
                     reads=[b_hd, b_dec, s_.b_EB[q]], writes=[s_.b_Qp[q]])
                yield
            if i < NCH - 1:
                k.op(k.pool, lambda: nc.gpsimd.tensor_scalar(
                    out=s_.Kp[q][:, :], in0=Ktok[:, n, :], scalar1=KS[:, n, g:g + 1], scalar2=1.0, op0=OP.mult, op1=OP.mult),
                     reads=[b_hd, b_ks], writes=[s_.b_Kp[q]])
                yield
                B2 = ps[4 * d + 2 + q]
                fns = [lambda dkc=dkc: nc.tensor.matmul(
                    out=B2[:, dkc * 256:(dkc + 1) * 256], lhsT=s_.Kp[q][:, dkc * 128:(dkc + 1) * 128], rhs=Vaug[:, n, 0:256],
                    start=True, stop=True) for dkc in range(2)]
                k.op(k.pe, fns, reads=[s_.b_Kp[q], b_hd], writes=[s_.p_du[q]])
                yield

        def m_chain(h, i, d):
            s_ = dd[d]; q = i % 2
            n, nprev, first = geom(i, d)
            g = d * 4 + h
            B1 = ps[4 * d + 1]
            fns = []
            for dvc in range(2):
                fns.append(lambda dvc=dvc: nc.tensor.matmul(
                    out=B1[:, dvc * 128:(dvc + 1) * 128], lhsT=Vaug[:, n, dvc * 128:(dvc + 1) * 128], rhs=s_.SD[q][:, :],
                    start=True, stop=(i == 0)))
                if i > 0:
                    for dkc in range(2):
                        fns.append(lambda dvc=dvc, dkc=dkc: nc.tensor.matmul(
                            out=B1[:, dvc * 128:(dvc + 1) * 128], lhsT=s_.Ubf[1 - q][:, dkc, dvc * 128:(dvc + 1) * 128],
                            rhs=s_.Qp[q][:, dkc, :], start=False, stop=(dkc == 1)))
            fns.append(lambda: nc.tensor.matmul(out=B1[:, 256:384], lhsT=ones_bf[:, :], rhs=s_.SD[q][:, :],
                                                start=True, stop=(i == 0)))
            if i > 0:
                for dkc in range(2):
                    fns.append(lambda dkc=dkc: nc.tensor.matmul(
                        out=B1[:, 256:384], lhsT=s_.nB[1 - q][:, dkc, :], rhs=s_.Qp[q][:, dkc, :], start=False, stop=(dkc == 1)))
            rd = [b_hd, b_v1, s_.b_SD[q], b_ones] + ([s_.b_Ubf[1 - q], s_.b_Qp[q], s_.b_nB[1 - q]] if i > 0 else [])
            k.op(k.pe, fns, reads=rd, writes=[s_.p_nd])
            yield
            if i < NCH - 1:
                B2 = ps[4 * d + 2 + q]
                A = ps[4 * d]
                k.op(k.pe, [lambda dkc=dkc: nc.tensor.matmul(
                    out=A[:, 384 + dkc:385 + dkc], lhsT=s_.Kp[q][:, dkc * 128:(dkc + 1) * 128], rhs=Vaug[:, n, 256:257],
                    start=True, stop=True) for dkc in range(2)], reads=[s_.b_Kp[q], b_v1], writes=[s_.p_a])
                yield
                U2 = s_.U32[:, :, :].rearrange("p c v -> p (c v)")
                if i == 0:
                    k.op(k.dve, lambda: nc.vector.tensor_copy(out=U2, in_=B2[:, :]), reads=[s_.p_du[q]], writes=[s_.b_U32])
                    yield
                    k.op(k.dve, lambda: nc.vector.tensor_copy(out=s_.n32[:, :], in_=A[:, 384:386]), reads=[s_.p_a], writes=[s_.b_n32])
                    yield
                else:
                    k.op(k.dve, lambda: nc.vector.scalar_tensor_tensor(
                        out=U2, in0=U2, scalar=DEC[:, nprev, g:g + 1], in1=B2[:, :], op0=OP.mult, op1=OP.add),
                         reads=[s_.p_du[q], b_dec], writes=[s_.b_U32])
                    yield
                    k.op(k.dve, lambda: nc.vector.scalar_tensor_tensor(
                        out=s_.n32[:, :], in0=s_.n32[:, :], scalar=DEC[:, nprev, g:g + 1], in1=A[:, 384:386], op0=OP.mult, op1=OP.add),
                         reads=[s_.p_a, b_dec], writes=[s_.b_n32])
                    yield
                k.op(k.act, lambda: nc.scalar.copy(out=s_.Ubf[q][:, :, :], in_=s_.U32[:, :, :]), reads=[s_.b_U32], writes=[s_.b_Ubf[q]])
                yield
                k.op(k.act, lambda: nc.scalar.copy(out=s_.nB[q][:, :, :], in_=s_.n32[:, :].unsqueeze(2).to_broadcast([128, 2, 128])),
                     reads=[s_.b_n32], writes=[s_.b_nB[q]])
                yield

        def m_h(h, i, d):
            s_ = dd[d]; q = i % 2
            n, nprev, first = geom(i, d)
            ck = slice(n * 128, (n + 1) * 128)
            B1 = ps[4 * d + 1]
            if not first:
                k.dma(k.sp, [(s_.hl[q][:, :, :], HB[2 * h:2 * h + 2, :, ck].rearrange("c p t -> p c t"))],
                      reads=[hbb[(h, n)]], writes=[s_.b_hl[q]], owner=s_.b_hl[q])
                yield
                k.dma(k.sp, [(s_.og[q][:, :, :], OGT[2 * h:2 * h + 2, :, ck].rearrange("c p t -> p c t"))],
                      writes=[s_.b_og[q]], owner=s_.b_og[q])
                yield
            k.op(k.act, lambda: nc.scalar.activation(out=s_.dcl[q][:, :], in_=B1[:, 256:384], func=AF.Abs),
                 reads=[s_.p_nd], writes=[s_.b_dcl[q]])
            yield
            k.op(k.dve, lambda: nc.vector.tensor_scalar(out=s_.dcl[q][:, :], in0=s_.dcl[q][:, :], scalar1=1.0, scalar2=None, op0=OP.max),
                 reads=[s_.b_dcl[q]], writes=[s_.b_dcl[q]])
            yield
            k.op(k.act, lambda: nc.scalar.activation(out=s_.dcl[q][:, :], in_=s_.dcl[q][:, :], func=AF.Ln),
                 reads=[s_.b_dcl[q]], writes=[s_.b_dcl[q]])
            yield
            k.op(k.act, lambda: nc.scalar.activation(out=s_.dcl[q][:, :], in_=s_.dcl[q][:, :], func=AF.Exp, scale=-1.0),
                 reads=[s_.b_dcl[q]], writes=[s_.b_dcl[q]])
            yield
            k.op(k.dve, lambda: nc.vector.tensor_tensor(
                out=s_.hbuf[q][:, :, :], in0=B1[:, 0:256].rearrange("p (c t) -> p c t", c=2),
                in1=s_.dcl[q][:, :].unsqueeze(1).to_broadcast([128, 2, 128]), op=OP.mult),
                 reads=[s_.p_nd, s_.b_dcl[q]], writes=[s_.b_hbuf[q]])
            yield
            if first:
                hbb[(h, n)] = Buf(f"hb{h}_{n}")
                k.dma(k.sp, [(HB[2 * h:2 * h + 2, :, ck].rearrange("c p t -> p c t"), s_.hbuf[q][:, :, :])],
                      reads=[s_.b_hbuf[q]], writes=[hbb[(h, n)]], owner=s_.b_hbuf[q])
                yield
            else:
                k.op(k.pool, lambda: nc.gpsimd.tensor_tensor(out=s_.hl[q][:, :, :], in0=s_.hl[q][:, :, :], in1=s_.hbuf[q][:, :, :], op=OP.add),
                     reads=[s_.b_hbuf[q]], writes=[s_.b_hl[q]])
                yield
                k.op(k.act, lambda: nc.scalar.activation(out=s_.sqh[q][:, :, :], in_=s_.hl[q][:, :, :], func=AF.Square),
                     reads=[s_.b_hl[q]], writes=[s_.b_sqh[q]])
                yield

        def m_fin2(h, i, d):
            s_ = dd[d]; q = i % 2
            n, nprev, first = geom(i, d)
            if first:
                return
            ck = slice(n * 128, (n + 1) * 128)
            A = ps[4 * d]
            k.op(k.pe, [lambda dvc=dvc: nc.tensor.matmul(
                out=A[:, 256:384], lhsT=ones_bf[:, :], rhs=s_.sqh[q][:, dvc, :], start=(dvc == 0), stop=(dvc == 1))
                for dvc in range(2)], reads=[s_.b_sqh[q], b_ones], writes=[s_.p_a])
            yield
            k.op(k.act, lambda: nc.scalar.activation(out=s_.rs[:, :], in_=A[:, 256:384], func=AF.Ln,
                                                     scale=1.0 / 256, bias=eps_sb[:, 0:1]),
                 reads=[s_.p_a, b_eps], writes=[s_.b_rs])
            yield
            k.op(k.act, lambda: nc.scalar.activation(out=s_.rs[:, :], in_=s_.rs[:, :], func=AF.Exp, scale=-0.5),
                 reads=[s_.b_rs], writes=[s_.b_rs])
            yield
            k.op(k.dve, [lambda dvc=dvc: nc.vector.scalar_tensor_tensor(
                out=s_.tmp[:, dvc, :], in0=s_.hl[q][:, dvc, :], scalar=gsm_sb[:, 2 + 2 * h + dvc:3 + 2 * h + dvc],
                in1=s_.rs[:, :], op0=OP.mult, op1=OP.mult) for dvc in range(2)],
                 reads=[s_.b_hl[q], s_.b_rs, b_cst], writes=[s_.b_tmp])
            yield
            k.op(k.pool, lambda: nc.gpsimd.tensor_tensor(out=s_.ym[:, :, :], in0=s_.tmp[:, :, :], in1=s_.og[q][:, :, :], op=OP.mult),
                 reads=[s_.b_tmp, s_.b_og[q]], writes=[s_.b_ym])
            yield
            k.dma(k.sp, [(YT[8 + 2 * h:10 + 2 * h, :, ck].rearrange("c p t -> p c t"), s_.ym[:, :, :])],
                  reads=[s_.b_ym], owner=s_.b_ym)
            yield

        for h in range(4):
            prs = [(QT2[:, :, :], QMT[2 * h:2 * h + 2].rearrange("c p t -> p c t")),
                   (KT2[:, :, :], KMT[2 * h:2 * h + 2].rearrange("c p t -> p c t"))]
            for n0 in range(0, NCH, 16):
                rows = slice(n0 * 128, (n0 + 16) * 128)
                prs.append((Ktok[:, n0:n0 + 16, :], KM[rows, h * 256:(h + 1) * 256].rearrange("(n p) d -> p n d", p=128)))
                prs.append((Vaug[:, n0:n0 + 16, 0:256], VM[rows, h * 256:(h + 1) * 256].rearrange("(n p) d -> p n d", p=128)))
            k.dma(k.sp, prs, writes=[b_hd], owner=b_hd)
            def rr(*gens):
                gens = list(gens)
                while gens:
                    for g_ in list(gens):
                        try:
                            next(g_)
                        except StopIteration:
                            gens.remove(g_)

            def seq(*gens):
                for g_ in gens:
                    for _ in g_:
                        pass

            seq(m_prep(h, 0, 0), m_prep(h, 0, 1))
            for i in range(NCH):
                if i + 1 < NCH:
                    seq(m_prep(h, i + 1, 0), m_prep(h, i + 1, 1))
                seq(m_chain(h, i, 0), m_chain(h, i, 1))
                if i >= 1:
                    seq(m_fin2(h, i - 1, 0), m_fin2(h, i - 1, 1))
                seq(m_h(h, i, 0), m_h(h, i, 1))
            seq(m_fin2(h, NCH - 1, 0), m_fin2(h, NCH - 1, 1))
        k.end_phase(keep=(b_cst,))


    if "C" in phases:
      with ExitStack() as esC:
        tb = alloc_tiles(esC, "C_")
        sbt = lambda name, shape, dt: esC.enter_context(nc.sbuf_tensor(name, shape, dt))
        hT, hb, xT, xb, sg, sgb = tb.hT, tb.hb, tb.xT, tb.xb, tb.sg, tb.sgb
        wpp_sb = sbt("wpp_sb", [128, 2, D], BF16)
        b_wpp = Buf("wpp")
        k.dma(k.sp, [(wpp_sb[:, :, :], WPP[:, :, :])], writes=[b_wpp], owner=b_wpp)
        pin = [sbt(f"pin{i}", [128, 256], F32) for i in range(2)]
        pinb = [Buf(f"pin{i}") for i in range(2)]
        peT = sbt("peT", [128, 2, T], BF16)
        b_peT = Buf("peT")
        sg2 = [sbt(f"sg2_{j}", [128, T], F32) for j in range(2)]
        sg2b = [Buf(f"sg2_{j}") for j in range(2)]
        for t in range(NT):
            tk = slice(t * T, (t + 1) * T)
            if t == 0:
                k.dma(k.sp, [(hT[:, :, :], YT[:, :, tk].rearrange("c p t -> p c t"))], writes=[hb], owner=hb)
            for g4 in range(4):
                k.dma(k.sp, [(xT[:, 4 * g4:4 * g4 + 4, :], X1T[:, 4 * g4:4 * g4 + 4, tk])],
                      writes=xb[4 * g4:4 * g4 + 4], owner=xb[4 * g4])
            for n in range(4):
                wt, wb = wload(tb, lambda s, n=n: [(s[:, :], WOUT[n].rearrange("p c f -> p (c f)"))])
                wv = wt[:, :].rearrange("p (c f) -> p c f", c=16)
                for j in range(4):
                    dc = n * 4 + j
                    bank = 6 + dc % 2
                    fns = [lambda kc=kc, j=j, bank=bank, wv=wv: nc.tensor.matmul(
                        out=ps[bank], lhsT=wv[:, kc, j * 128:(j + 1) * 128], rhs=hT[:, kc, :],
                        start=(kc == 0), stop=(kc == 15)) for kc in range(16)]
                    k.op(k.pe, fns, reads=[wb, hb], writes=[pb[bank]])
                    k.op(k.dve, lambda dc=dc, bank=bank: nc.vector.tensor_tensor(
                        out=xT[:, dc, :], in0=ps[bank], in1=xT[:, dc, :], op=OP.add),
                         reads=[pb[bank]], writes=[xb[dc]])
            rmsnorm_T(tb, 2 * DC)
            ffn(tb, WGU2, WD2)
            rmsnorm_T(tb, 3 * DC)
            for s in range(4):
                i = s % 2
                r0 = t * T + s * 128
                k.dma(k.sp, [(pin[i][:, :], pes[r0:r0 + 128, :])], writes=[pinb[i]], owner=pinb[i])
                k.op(k.pe, [lambda k2=k2, i=i: nc.tensor.transpose(
                    out=ps[i][:, k2 * 128:(k2 + 1) * 128], in_=pin[i][:, k2 * 128:(k2 + 1) * 128], identity=ident)
                    for k2 in range(2)], reads=[pinb[i], b_cst], writes=[pb[i]])
                k.op(k.act, lambda i=i, s=s: nc.scalar.copy(
                    out=peT[:, :, s * 128:(s + 1) * 128], in_=ps[i][:, 0:256].rearrange("p (c t) -> p c t", c=2)),
                     reads=[pb[i]], writes=[b_peT])
            for n in range(4):
                wt, wb = wload(tb, lambda s, n=n: [(s[:, :], WPG[n].rearrange("p c f -> p (c f)"))])
                wv = wt[:, :].rearrange("p (c f) -> p c f", c=16)
                for j in range(4):
                    dc = n * 4 + j
                    bank = 6 + dc % 2
                    pbank = 4 + dc % 2
                    jj = dc % 2
                    fns = [lambda kc=kc, j=j, bank=bank, wv=wv: nc.tensor.matmul(
                        out=ps[bank], lhsT=wv[:, kc, j * 128:(j + 1) * 128], rhs=hT[:, kc, :],
                        start=(kc == 0), stop=(kc == 15)) for kc in range(16)]
                    k.op(k.pe, fns, reads=[wb, hb], writes=[pb[bank]])
                    k.op(k.pe, [lambda k2=k2, dc=dc, pbank=pbank: nc.tensor.matmul(
                        out=ps[pbank], lhsT=wpp_sb[:, k2, dc * 128:(dc + 1) * 128], rhs=peT[:, k2, :],
                        start=(k2 == 0), stop=(k2 == 1)) for k2 in range(2)], reads=[b_wpp, b_peT], writes=[pb[pbank]])
                    k.op(k.act, lambda jj=jj, bank=bank: nc.scalar.activation(out=sg[jj][:, :], in_=ps[bank], func=AF.Sigmoid),
                         reads=[pb[bank]], writes=[sgb[jj]])
                    k.op(k.dve, lambda jj=jj, pbank=pbank: nc.vector.tensor_tensor(
                        out=sg2[jj][:, :], in0=ps[pbank], in1=sg[jj][:, :], op=OP.mult),
                         reads=[pb[pbank], sgb[jj]], writes=[sg2b[jj]])
                    k.op(k.pool, lambda jj=jj, dc=dc: nc.gpsimd.tensor_tensor(
                        out=xT[:, dc, :], in0=xT[:, dc, :], in1=sg2[jj][:, :], op=OP.add),
                         reads=[sg2b[jj]], writes=[xb[dc]])
            if t + 1 < NT:
                tk1 = slice((t + 1) * T, (t + 2) * T)
                k.dma(k.sp, [(hT[:, :, :], YT[:, :, tk1].rearrange("c p t -> p c t"))], writes=[hb], owner=hb)
            xin, xinb = tb.xin, tb.xinb
            for s in range(4):
                i = s % 2
                r0 = t * T + s * 128
                for q in range(4):
                    bank = q % 2
                    fns = [lambda dc=dc, j=j, bank=bank, s=s: nc.tensor.transpose(
                        out=ps[bank][:, j * 128:(j + 1) * 128], in_=xT[:, dc, s * 128:(s + 1) * 128], identity=ident)
                        for j, dc in enumerate(range(q * 4, q * 4 + 4))]
                    k.op(k.pe, fns, reads=[xb[dc] for dc in range(q * 4, q * 4 + 4)] + [b_cst], writes=[pb[bank]])
                    k.op(k.act, lambda q=q, i=i, bank=bank: nc.scalar.copy(out=xin[i][:, q * 512:(q + 1) * 512], in_=ps[bank]),
                         reads=[pb[bank]], writes=[xinb[i]])
                k.dma(k.pool, [(y[r0:r0 + 128, :], xin[i][:, :])], reads=[xinb[i]], owner=xinb[i])
        k.end_phase(keep=(b_cst,))


    k.barrier()
    es.close()
    return nc, k


def _consts():
    ident = np.eye(128, dtype=np.float32)
    s = np.arange(128)
    triF = (s[:, None] <= s[None, :]).astype(np.float32)
    triB = (s[:, None] >= s[None, :]).astype(np.float32)
    ones = np.ones((128, 128), np.float32)
    zeros = np.zeros((128, 128), np.float32)
    return np.ascontiguousarray(np.concatenate([ident, triF, triB, ones, zeros], axis=1))


def _na_tables(rpb, seq_rows):
    a = np.arange(2)[:, None, None, None, None]
    kc = np.arange(64)[None, :, None, None, None]
    o = np.arange(9)[None, None, :, None, None]
    b = np.arange(2)[None, None, None, :, None]
    qc = np.arange(64)[None, None, None, None, :]
    dr = 2 * (o - 4) + a - b + 0 * kc + 0 * qc
    dcol = np.clip(kc - qc, -15, 15) + 15 + 0 * dr
    cs = np.clip(qc - 8, 0, 48)
    colok = (kc >= cs) & (kc < cs + 16)
    inwin = np.abs(dr) <= 7
    dri = np.clip(dr + 7, 0, 14)
    g = rpb[:, dri, dcol]
    g = np.where(inwin[None], g, np.float32(0.0))
    rpbg = np.ascontiguousarray(g.reshape(8, 128, 9, 128).astype(np.float32))
    cmE = np.where(inwin & colok, 0.0, NEG).astype(np.float32).reshape(128, 9, 128)
    cmI = np.where(inwin & colok & (dr >= -4) & (dr <= 3), 0.0, NEG).astype(np.float32).reshape(128, 9, 128)
    R = sum(seq_rows)
    nch = R // 2
    seq_of_row = np.concatenate([np.full(r, i) for i, r in enumerate(seq_rows)])
    start = np.concatenate([[0], np.cumsum(seq_rows)[:-1]])
    ne = max(4, (nch // 16) * 4)
    rb = np.full((128, ne, 9, 2), NEG, np.float32)
    pos = {0: 0, 1: 1, 14: 2, 15: 3}
    for c in range(nch):
        if c % 16 not in pos:
            continue
        e = (c // 16) * 4 + pos[c % 16]
        for bb in range(2):
            qr = 2 * c + bb
            si = seq_of_row[qr]
            r0 = start[si]; Rs = seq_rows[si]
            rs = r0 + min(max(qr - r0 - 4, 0), Rs - 8)
            for oo in range(9):
                for aa in range(2):
                    kr = 2 * (c + oo - 4) + aa
                    if 0 <= kr < R and seq_of_row[kr] == si and rs <= kr <= rs + 7:
                        rb[aa * 64:(aa + 1) * 64, e, oo, bb] = 0.0
    keep = np.ones((128, 2, nch), np.float32)
    for s0 in start[1:]:
        cst = s0 // 2
        keep[:, 0, cst - 1] = 0.0
        keep[:, 1, cst] = 0.0
    return rpbg, cmE, cmI, rb, keep


def host_inputs(x_stream, p_stream, W, seq_rows):
    f = lambda a: np.ascontiguousarray(a, dtype=np.float32)
    gvec = lambda g: g.reshape(DC, 128).T
    rpbg, cmE, cmI, rb, keep = _na_tables(W["rpb"], seq_rows)
    gsm = np.zeros((128, 16), np.float32)
    gsm[:, 0] = W["g_qn"]; gsm[:, 1] = W["g_kn"]
    gsm[:, 2:10] = W["g_mh"].reshape(8, 128).T
    gb = np.concatenate([W["b_igate"].reshape(8), W["b_fgate"].reshape(8)])
    return {
        "xs": f(x_stream), "pes": f(p_stream),
        "w1g": f(W["w_ffn1_gate"]), "w1u": f(W["w_ffn1_up"]), "w1d": f(W["w_ffn1_down"]),
        "w2g": f(W["w_ffn2_gate"]), "w2u": f(W["w_ffn2_up"]), "w2d": f(W["w_ffn2_down"]),
        "win": f(W["w_in"]), "wout": f(W["w_out"]), "wpg": f(W["w_ple_gate"]), "wpp": f(W["w_ple_proj"]),
        "gv": f(np.concatenate([gvec(W["g_ffn1"]), gvec(W["g_mix"]), gvec(W["g_ffn2"]), gvec(W["g_ple"])], axis=1)),
        "gsm": f(gsm), "gbias": f(np.broadcast_to(gb[None, :], (128, 16))),
        "cst": _consts(), "rpbg": rpbg, "cmI": cmI, "cmE": cmE, "rb": rb, "keep": keep,
    }


_NC_CACHE = {}


def kernel(**inputs):
    NTOK = 8192
    f32 = lambda a: np.asarray(a, dtype=np.float32)
    W = {}
    for name in ("g_ffn1", "w_ffn1_gate", "w_ffn1_up", "w_ffn1_down", "g_mix", "w_in", "b_igate", "b_fgate",
                 "g_qn", "g_kn", "rpb", "g_mh", "w_out", "g_ffn2", "w_ffn2_gate", "w_ffn2_up", "w_ffn2_down",
                 "g_ple", "w_ple_gate", "w_ple_proj"):
        W[name] = f32(inputs[name])[0]
    xp = f32(inputs["x_prompt"]); xsm = f32(inputs["x_sample"])
    pp = f32(inputs["p_prompt"])[0]; psm = f32(inputs["p_sample"])[0]
    zx = np.zeros((NTOK, D), np.float32); zp = np.zeros((NTOK, 256), np.float32)
    streams = [(zx, zp, [128])] * 8
    streams[0] = (xsm[0], psm[0], [128])
    streams[2] = (xsm[1], psm[1], [128])
    streams[4] = (xp.reshape(NTOK, D), pp.reshape(NTOK, 256), [32, 32, 32, 32])
    in_maps = [host_inputs(x, p, W, rows) for (x, p, rows) in streams]
    if "nc" not in _NC_CACHE:
        _NC_CACHE["nc"] = build(NTOK)[0]
    res = run_bass_kernel_spmd(_NC_CACHE["nc"], in_maps, core_ids=list(range(8)))
    outs = [np.asarray(res.results[c]["y"], dtype=np.float32) for c in (0, 2, 4)]
    y_sample = np.stack([outs[0], outs[1]], axis=0)
    y_prompt = outs[2].reshape(4, 2048, D)
    return (y_prompt, y_sample)
```

```python
import os
from contextlib import ExitStack

import numpy as np
import concourse.bass as bass
import concourse.mybir as mybir
from concourse.bass_utils import run_bass_kernel_spmd

F32 = mybir.dt.float32
BF16 = mybir.dt.bfloat16
AF = mybir.ActivationFunctionType
OP = mybir.AluOpType

D = 2048
DC = 16
DFF = 5632
FC = 44
DIN = 7184
T = 512
EPS = 1e-6
NEG = -1000.0


class Buf:
    __slots__ = ("name", "w", "r", "dsem", "dcnt")

    def __init__(self, name):
        self.name = name
        self.w = {}
        self.r = {}
        self.dsem = None
        self.dcnt = 0


class Eng:
    def __init__(self, name, e, sem, inorder=False):
        self.name = name
        self.e = e
        self.sem = sem
        self.cnt = 0
        self.waited = {}
        self.inorder = inorder


def _merge(d, sem, val):
    k = id(sem)
    if k not in d or d[k][1] < val:
        d[k] = (sem, val)


class K:
    def __init__(self, nc, es):
        self.nc = nc
        self.es = es
        self.n_sem = 0
        self.pe = Eng("pe", nc.tensor, self.sem("pe"), inorder=True)
        self.act = Eng("act", nc.scalar, self.sem("act"))
        self.dve = Eng("dve", nc.vector, self.sem("dve"))
        self.pool = Eng("pool", nc.gpsimd, self.sem("pool"))
        self.sp = Eng("sp", nc.sync, self.sem("sp"), inorder=True)
        self.engs = [self.pe, self.act, self.dve, self.pool, self.sp]
        self.dbufs = []
        self.sem_pool = {"hw": [], "sw": []}
        self.n_inst = 0

    def sem(self, name):
        self.n_sem += 1
        return self.es.enter_context(self.nc.semaphore(name))

    def _wait(self, eng, reads, writes):
        deps = {}
        for b in reads:
            for s, v in b.w.values():
                _merge(deps, s, v)
        for b in writes:
            for s, v in b.w.values():
                _merge(deps, s, v)
            for s, v in b.r.values():
                _merge(deps, s, v)
        for s, v in deps.values():
            if s is eng.sem and eng.inorder:
                continue
            if eng.waited.get(id(s), 0) < v:
                eng.e.wait_ge(s, v)
                eng.waited[id(s)] = v
                self.n_inst += 1
                eng.ni = getattr(eng, 'ni', 0) + 1

    def op(self, eng, fns, reads=(), writes=()):
        self._wait(eng, reads, writes)
        if not isinstance(fns, (list, tuple)):
            fns = [fns]
        inst = None
        for f in fns:
            inst = f()
            self.n_inst += 1
            eng.ni = getattr(eng, 'ni', 0) + 1
        inst.then_inc(eng.sem, 1)
        eng.cnt += 1
        assert eng.cnt < 60000, eng.name
        for b in reads:
            _merge(b.r, eng.sem, eng.cnt)
        for b in writes:
            b.w = {id(eng.sem): (eng.sem, eng.cnt)}
            b.r = {}

    def dma(self, q, pairs, reads=(), writes=(), owner=None):
        kind = "sw" if q is self.pool else "hw"
        ent = owner.dsem.get(kind) if isinstance(owner.dsem, dict) else None
        if ent is None:
            if not isinstance(owner.dsem, dict):
                owner.dsem = {}
            if self.sem_pool[kind]:
                ent = list(self.sem_pool[kind].pop())
            else:
                ent = [self.sem(f"d{self.n_sem}_{kind}_" + owner.name), 0]
            owner.dsem[kind] = ent
            self.dbufs.append((owner, kind))
        self._wait(q, reads, writes)
        for out, in_ in pairs:
            q.e.dma_start(out=out, in_=in_).then_inc(ent[0], 16)
            ent[1] += 16
            self.n_inst += 1
            q.ni = getattr(q, 'ni', 0) + 1
            q.nd = getattr(q, 'nd', 0) + 1
        assert ent[1] < 60000, owner.name
        for b in reads:
            _merge(b.r, ent[0], ent[1])
        for b in writes:
            b.w = {id(ent[0]): (ent[0], ent[1])}
            b.r = {}

    def barrier(self):
        for e in self.engs:
            for o in self.engs:
                if o is e or o.cnt == 0:
                    continue
                if e.waited.get(id(o.sem), 0) < o.cnt:
                    e.e.wait_ge(o.sem, o.cnt)
                    e.waited[id(o.sem)] = o.cnt
            for b, kind in self.dbufs:
                s, c = b.dsem[kind]
                if c and e.waited.get(id(s), 0) < c:
                    e.e.wait_ge(s, c)
                    e.waited[id(s)] = c

    def end_phase(self, keep=()):
        self.barrier()
        kept = []
        for b, kind in self.dbufs:
            if b in keep:
                kept.append((b, kind))
            else:
                self.sem_pool[kind].append(tuple(b.dsem[kind]))
                del b.dsem[kind]
        self.dbufs = kept

    def sb(self, name, shape, dt):
        return self.es.enter_context(self.nc.sbuf_tensor(name, shape, dt))


def build(NTOK, debug=(), phases="0ABMC"):
    NT = NTOK // T
    NCH = NTOK // 128
    nc = bass.Bass("TRN2", target_bir_lowering=False)
    es = ExitStack()
    k = K(nc, es)

    def din(name, shape, dt=F32):
        return nc.dram_tensor(name, shape, dt, kind="ExternalInput").ap()

    def dscr(name, shape, dt):
        kind = "ExternalOutput" if name in debug else "Internal"
        return nc.dram_tensor(name, shape, dt, kind=kind).ap()

    xs = din("xs", [NTOK, D])
    pes = din("pes", [NTOK, 256])
    w1g = din("w1g", [D, DFF]); w1u = din("w1u", [D, DFF]); w1d = din("w1d", [DFF, D])
    w2g = din("w2g", [D, DFF]); w2u = din("w2u", [D, DFF]); w2d = din("w2d", [DFF, D])
    win = din("win", [D, DIN]); wout = din("wout", [D, D])
    wpg = din("wpg", [D, D]); wpp = din("wpp", [256, D])
    gv = din("gv", [128, 4 * DC])
    gsm = din("gsm", [128, 16])
    gbias = din("gbias", [128, 16])
    cst = din("cst", [128, 5 * 128])
    rpbg = din("rpbg", [8, 128, 9, 128])
    cmI = din("cmI", [128, 9, 128]); cmE = din("cmE", [128, 9, 128])
    NE = max(4, (NCH // 16) * 4)
    rb = din("rb", [128, NE, 9, 2])
    keep = din("keep", [128, 2, NCH])
    y = nc.dram_tensor("y", [NTOK, D], F32, kind="ExternalOutput").ap()

    WGU1 = dscr("WGU1", [22, 128, 2, 16, 256], BF16)
    WD1 = dscr("WD1", [16, 128, FC, 128], BF16)
    WGU2 = dscr("WGU2", [22, 128, 2, 16, 256], BF16)
    WD2 = dscr("WD2", [16, 128, FC, 128], BF16)
    WIN = dscr("WIN", [14, 128, 16, 512], BF16)
    WING = dscr("WING", [128, 16, 16], BF16)
    WOUT = dscr("WOUT", [4, 128, 16, 512], BF16)
    WPG = dscr("WPG", [4, 128, 16, 512], BF16)
    WPP = dscr("WPP", [128, 2, D], BF16)
    X1T = dscr("X1T", [128, DC, NTOK], F32)
    QAT = dscr("QAT", [8, 128, NTOK], BF16); KAT = dscr("KAT", [8, 128, NTOK], BF16)
    VA = dscr("VA", [NTOK, 1024], BF16)
    QMT = dscr("QMT", [8, 128, NTOK], BF16); KMT = dscr("KMT", [8, 128, NTOK], BF16)
    KM = dscr("KM", [NTOK, 1024], BF16); VM = dscr("VM", [NTOK, 1024], BF16)
    OGT = dscr("OGT", [8, 128, NTOK], BF16)
    GT = dscr("GT", [NTOK, 16], F32)
    HB = dscr("HB", [8, 128, NTOK], F32)
    YT = dscr("YT", [16, 128, NTOK], BF16)

    cst_sb = k.sb("cst_sb", [128, 5 * 128], F32)
    ident = cst_sb[:, 0:128]
    triF = cst_sb[:, 128:256]
    triB = cst_sb[:, 256:384]
    ones_f = cst_sb[:, 384:512]
    gv_sb = k.sb("gv_sb", [128, 4 * DC], F32)
    gsm_sb = k.sb("gsm_sb", [128, 16], F32)
    ones_bf = k.sb("ones_bf", [128, 128], BF16)
    b_cst = Buf("cst")
    k.dma(k.sp, [(cst_sb[:, :], cst[:, :]), (gv_sb[:, :], gv[:, :]), (gsm_sb[:, :], gsm[:, :])],
          writes=[b_cst], owner=b_cst)
    b_ones = Buf("ones")
    k.op(k.dve, lambda: nc.vector.tensor_copy(out=ones_bf[:, :], in_=cst_sb[:, 384:512]),
         reads=[b_cst], writes=[b_ones])

    psall = es.enter_context(nc.psum_tensor("psall", [128, 4096], F32))
    ps = [psall[:, i * 512:(i + 1) * 512] for i in range(8)]
    pb = [Buf(f"ps{i}") for i in range(8)]

    if "0" in phases:
      with ExitStack() as es0:
        def sb0(name, shape, dt):
            return es0.enter_context(nc.sbuf_tensor(name, shape, dt))
        NST = 3
        st_in = [sb0(f"c_in{i}", [128, DIN], F32) for i in range(NST)]
        st_out = [sb0(f"c_out{i}", [128, DIN], BF16) for i in range(NST)]
        bi = [Buf(f"c_in{i}") for i in range(NST)]
        bo = [Buf(f"c_out{i}") for i in range(NST)]
        state = {"i": 0}
        cast_engs = [k.act, k.dve, k.pool]

        def cast_rows(src_pairs_fn, ncols, dst_pairs_fn):
            i = state["i"] % NST
            e = cast_engs[state["i"] % 3]
            state["i"] += 1
            k.dma(k.sp, src_pairs_fn(st_in[i]), writes=[bi[i]], owner=bi[i])
            if e is k.act:
                fn = lambda: nc.scalar.copy(out=st_out[i][:, 0:ncols], in_=st_in[i][:, 0:ncols])
            elif e is k.dve:
                fn = lambda: nc.vector.tensor_copy(out=st_out[i][:, 0:ncols], in_=st_in[i][:, 0:ncols])
            else:
                fn = lambda: nc.gpsimd.tensor_copy(out=st_out[i][:, 0:ncols], in_=st_in[i][:, 0:ncols])
            k.op(e, fn, reads=[bi[i]], writes=[bo[i]])
            k.dma(k.pool, dst_pairs_fn(st_out[i]), reads=[bo[i]], owner=bo[i])

        def cast_gu(wg, wu, WGU):
            for gu, w in enumerate((wg, wu)):
                for kc in range(16):
                    cast_rows(lambda si, w=w, kc=kc: [(si[:, 0:DFF], w[kc * 128:(kc + 1) * 128, :])], DFF,
                              lambda so, gu=gu, kc=kc: [(
                                  WGU[a * 11:(a + 1) * 11, :, gu, kc, :].rearrange("n p f -> p n f"),
                                  so[:, a * 2816:(a + 1) * 2816].rearrange("p (n f) -> p n f", f=256)) for a in range(2)])

        def cast_d(wd, WD):
            for fc2 in range(FC // 2):
                def dst(so, fc2=fc2):
                    pairs = []
                    for a in range(2):
                        fc = fc2 * 2 + a
                        pairs.append((WD[:, :, fc, :].rearrange("c p d -> p c d"),
                                      so[:, a * 2048:(a + 1) * 2048].rearrange("p (c d) -> p c d", d=128)))
                    return pairs
                cast_rows(lambda si, fc2=fc2: [(si[:, 0:4096].rearrange("p (a d) -> p a d", a=2),
                                                wd[fc2 * 256:(fc2 + 1) * 256, :].rearrange("(a p) d -> p a d", p=128))],
                          4096, dst)

        def cast_panels512(w, WS, ncols_total):
            for kc in range(16):
                cast_rows(lambda si, kc=kc: [(si[:, 0:ncols_total], w[kc * 128:(kc + 1) * 128, 0:ncols_total])],
                          ncols_total,
                          lambda so, kc=kc: [(
                              WS[:, :, kc, :].rearrange("n p f -> p n f"),
                              so[:, 0:ncols_total].rearrange("p (n f) -> p n f", f=512))])

        cast_gu(w1g, w1u, WGU1)
        cast_d(w1d, WD1)
        cast_panels512(win, WIN, 7168)
        for kc in range(16):
            cast_rows(lambda si, kc=kc: [(si[:, 0:16], win[kc * 128:(kc + 1) * 128, 7168:7184])], 16,
                      lambda so, kc=kc: [(WING[:, kc, :], so[:, 0:16])])
        k.end_phase(keep=(b_cst,))

    class TB:
        pass

    def alloc_tiles(stk, pfx):
        tb = TB()
        sbt = lambda name, shape, dt: stk.enter_context(nc.sbuf_tensor(pfx + name, shape, dt))
        tb.xT = sbt("xT", [128, DC, T], F32)
        tb.xb = [Buf(f"xT{dc}") for dc in range(DC)]
        tb.hT = sbt("hT", [128, DC, T], BF16)
        tb.hb = Buf("hT")
        tb.act = sbt("act", [128, FC, T], BF16)
        tb.actb = [Buf(f"act{fc}") for fc in range(FC)]
        tb.sq = [sbt(f"sq{j}", [128, T], BF16) for j in range(2)]
        tb.sqb = [Buf(f"sq{j}") for j in range(2)]
        tb.sg = [sbt(f"sg{j}", [128, T], F32) for j in range(2)]
        tb.sgb = [Buf(f"sg{j}") for j in range(2)]
        tb.rstd = sbt("rstd", [128, T], F32)
        tb.rstdb = Buf("rstd")
        tb.xin = [sbt(f"xin{i}", [128, D], F32) for i in range(2)]
        tb.xinb = [Buf(f"xin{i}") for i in range(2)]
        tb.wslot = [sbt(f"wslot{i}", [128, 8192], BF16) for i in range(NSLOT)]
        tb.wsb = [Buf(f"wslot{i}") for i in range(NSLOT)]
        tb.wi = 0
        tb.stg = [sbt(f"stg{i}", [128, T], BF16) for i in range(4)]
        tb.stgb = [Buf(f"stg{i}") for i in range(4)]
        tb.stgf = [sbt(f"stgf{i}", [128, T], F32) for i in range(2)]
        tb.stgfb = [Buf(f"stgf{i}") for i in range(2)]
        tb.si = 0
        tb.fi = 0
        return tb

    NSLOT = 4

    def wload(tb, pairs_fn):
        i = tb.wi % NSLOT
        tb.wi += 1
        k.dma(k.sp, pairs_fn(tb.wslot[i]), writes=[tb.wsb[i]], owner=tb.wsb[i])
        return tb.wslot[i], tb.wsb[i]

    def rmsnorm_T(tb, gcol):
        xT, xb, hT, sq, sqb, rstd, rstdb = tb.xT, tb.xb, tb.hT, tb.sq, tb.sqb, tb.rstd, tb.rstdb
        act, actb = tb.act, tb.actb
        H = DC // 2
        k.op(k.act, lambda: nc.scalar.activation(out=act[:, 0:H, :], in_=xT[:, 0:H, :], func=AF.Square),
             reads=list(xb[0:H]), writes=list(actb[0:H]))
        k.op(k.dve, lambda: nc.vector.tensor_tensor(out=act[:, H:DC, :], in0=xT[:, H:DC, :], in1=xT[:, H:DC, :], op=OP.mult),
             reads=list(xb[H:DC]), writes=list(actb[H:DC]))
        k.op(k.pe, [lambda dc=dc: nc.tensor.matmul(out=ps[0], lhsT=ones_bf[:, :], rhs=act[:, dc, :],
                                                   start=(dc == 0), stop=(dc == DC - 1)) for dc in range(DC)],
             reads=list(actb[0:DC]) + [b_ones], writes=[pb[0]])
        k.op(k.act, lambda: nc.scalar.activation(out=rstd[:, :], in_=ps[0], func=AF.Sqrt,
                                                 scale=1.0 / D, bias=eps_sb[:, 0:1]),
             reads=[pb[0], b_eps], writes=[rstdb])
        k.op(k.dve, lambda: nc.vector.reciprocal(out=rstd[:, :], in_=rstd[:, :]), reads=[rstdb], writes=[rstdb])
        fns = []
        for dc in range(DC):
            fns.append(lambda dc=dc: nc.vector.scalar_tensor_tensor(
                out=hT[:, dc, :], in0=xT[:, dc, :], scalar=gv_sb[:, gcol + dc:gcol + dc + 1], in1=rstd[:, :],
                op0=OP.mult, op1=OP.mult))
        k.op(k.dve, fns, reads=list(xb) + [rstdb, b_cst], writes=[tb.hb])

    eps_sb = k.sb("eps_sb", [128, 4], F32)
    b_eps = Buf("eps")
    k.op(k.dve, [lambda: nc.vector.memset(eps_sb[:, 0:1], EPS), lambda: nc.vector.memset(eps_sb[:, 1:2], 128 * EPS),
                 lambda: nc.vector.memset(eps_sb[:, 2:3], 1.0)], writes=[b_eps])


    def bg_pieces():
        for kc in range(16):
            yield (lambda si, kc=kc: [(si[:, 0:2048], wout[kc * 128:(kc + 1) * 128, :])], 2048,
                   lambda so, kc=kc: [(WOUT[:, :, kc, :].rearrange("n p f -> p n f"), so[:, 0:2048].rearrange("p (n f) -> p n f", f=512))])
        for gu, w in enumerate((w2g, w2u)):
            for kc in range(16):
                for (n0, n1) in ((0, 8), (8, 16), (16, 22)):
                    nc_ = (n1 - n0) * 256
                    yield (lambda si, w=w, kc=kc, n0=n0, nc_=nc_: [(si[:, 0:nc_], w[kc * 128:(kc + 1) * 128, n0 * 256:n0 * 256 + nc_])], nc_,
                           lambda so, gu=gu, kc=kc, n0=n0, n1=n1, nc_=nc_: [(
                               WGU2[n0:n1, :, gu, kc, :].rearrange("n p f -> p n f"), so[:, 0:nc_].rearrange("p (n f) -> p n f", f=256))])
        for fc in range(FC):
            yield (lambda si, fc=fc: [(si[:, 0:2048], w2d[fc * 128:(fc + 1) * 128, :])], 2048,
                   lambda so, fc=fc: [(WD2[:, :, fc, :].rearrange("c p d -> p c d"), so[:, 0:2048].rearrange("p (c d) -> p c d", d=128))])
        for kc in range(16):
            yield (lambda si, kc=kc: [(si[:, 0:2048], wpg[kc * 128:(kc + 1) * 128, :])], 2048,
                   lambda so, kc=kc: [(WPG[:, :, kc, :].rearrange("n p f -> p n f"), so[:, 0:2048].rearrange("p (n f) -> p n f", f=512))])
        for k2 in range(2):
            yield (lambda si, k2=k2: [(si[:, 0:2048], wpp[k2 * 128:(k2 + 1) * 128, :])], 2048,
                   lambda so, k2=k2: [(WPP[:, k2, :], so[:, 0:2048])])

    class BG:
        pass
    bgs = BG()
    bgs.gen = None

    def bg_init(stk):
        bgs.cin = stk.enter_context(nc.sbuf_tensor("bg_in", [128, 2048], F32))
        bgs.cout = stk.enter_context(nc.sbuf_tensor("bg_out", [128, 2048], BF16))
        bgs.bi = Buf("bg_in"); bgs.bo = Buf("bg_out")
        bgs.gen = bg_pieces()
        bgs.pending = None
        bgs.n = 0

    def bg_step():
        if bgs.gen is None:
            return
        if bgs.pending is not None:
            ncols, dstf = bgs.pending
            e = (k.act, k.dve, k.pool)[bgs.n % 3]
            bgs.n += 1
            if e is k.act:
                fn = lambda: nc.scalar.copy(out=bgs.cout[:, 0:ncols], in_=bgs.cin[:, 0:ncols])
            elif e is k.dve:
                fn = lambda: nc.vector.tensor_copy(out=bgs.cout[:, 0:ncols], in_=bgs.cin[:, 0:ncols])
            else:
                fn = lambda: nc.gpsimd.tensor_copy(out=bgs.cout[:, 0:ncols], in_=bgs.cin[:, 0:ncols])
            k.op(e, fn, reads=[bgs.bi], writes=[bgs.bo])
            k.dma(k.pool, dstf(bgs.cout), reads=[bgs.bo], owner=bgs.bo)
            bgs.pending = None
        nxt = next(bgs.gen, None)
        if nxt is None:
            bgs.gen = None
            return
        srcf, ncols, dstf = nxt
        k.dma(k.sp, srcf(bgs.cin), writes=[bgs.bi], owner=bgs.bi)
        bgs.pending = (ncols, dstf)

    def ffn(tb, WGU, WD):
        hT, hb, act, actb, xT, xb, sg, sgb = tb.hT, tb.hb, tb.act, tb.actb, tb.xT, tb.xb, tb.sg, tb.sgb
        def load_gu(n):
            return wload(tb, lambda s, n=n: [(s[:, :], WGU[n].rearrange("p a c f -> p (a c f)"))])
        nxt = load_gu(0)
        for n in range(22):
            cur = nxt
            if n + 1 < 22:
                nxt = load_gu(n + 1)
            wt, wb = cur
            wv = wt[:, :].rearrange("p (a c f) -> p a c f", a=2, c=16)
            bg_step()
            for f2 in range(2):
                fc = n * 2 + f2
                gbank, ubank = 2 + (fc % 2) * 2, 3 + (fc % 2) * 2
                for gu, bank in ((0, gbank), (1, ubank)):
                    fns = [lambda kc=kc, gu=gu, bank=bank, f2=f2: nc.tensor.matmul(
                        out=ps[bank], lhsT=wv[:, gu, kc, f2 * 128:(f2 + 1) * 128], rhs=hT[:, kc, :],
                        start=(kc == 0), stop=(kc == 15)) for kc in range(16)]
                    k.op(k.pe, fns, reads=[wb, hb], writes=[pb[bank]])
                j = fc % 2
                k.op(k.act, lambda j=j, gbank=gbank: nc.scalar.activation(out=sg[j][:, :], in_=ps[gbank], func=AF.Silu),
                     reads=[pb[gbank]], writes=[sgb[j]])
                k.op(k.dve, lambda j=j, ubank=ubank, fc=fc: nc.vector.tensor_tensor(
                    out=act[:, fc, :], in0=ps[ubank], in1=sg[j][:, :], op=OP.mult),
                     reads=[pb[ubank], sgb[j]], writes=[actb[fc]])
        def load_d(dc):
            return wload(tb, lambda s, dc=dc: [(s[:, 0:FC * 128], WD[dc].rearrange("p c d -> p (c d)"))])
        nxt = load_d(0)
        for dc in range(DC):
            cur = nxt
            if dc + 1 < DC:
                nxt = load_d(dc + 1)
            wt, wb = cur
            bank = 6 + dc % 2
            fns = [lambda fc=fc, bank=bank, wt=wt: nc.tensor.matmul(
                out=ps[bank], lhsT=wt[:, fc * 128:(fc + 1) * 128], rhs=act[:, fc, :],
                start=(fc == 0), stop=(fc == FC - 1)) for fc in range(FC)]
            k.op(k.pe, fns, reads=[wb] + actb, writes=[pb[bank]])
            k.op(k.dve, lambda dc=dc, bank=bank: nc.vector.scalar_tensor_tensor(
                out=xT[:, dc, :], in0=ps[bank], scalar=0.5, in1=xT[:, dc, :], op0=OP.mult, op1=OP.add),
                 reads=[pb[bank]], writes=[xb[dc]])

    def load_xT_from_tokmajor(tb, src, t):
        xin, xinb, xT, xb = tb.xin, tb.xinb, tb.xT, tb.xb
        for s in range(4):
            i = s % 2
            r0 = t * T + s * 128
            k.dma(k.sp, [(xin[i][:, :], src[r0:r0 + 128, :])], writes=[xinb[i]], owner=xinb[i])
            for q in range(4):
                bank = (s * 4 + q) % 8
                fns = [lambda dc=dc, j=j, i=i, bank=bank: nc.tensor.transpose(
                    out=ps[bank][:, j * 128:(j + 1) * 128], in_=xin[i][:, dc * 128:(dc + 1) * 128], identity=ident)
                    for j, dc in enumerate(range(q * 4, q * 4 + 4))]
                k.op(k.pe, fns, reads=[xinb[i], b_cst], writes=[pb[bank]])
                if q % 2 == 0:
                    k.op(k.act, lambda q=q, s=s, bank=bank: nc.scalar.copy(
                        out=xT[:, q * 4:q * 4 + 4, s * 128:(s + 1) * 128],
                        in_=ps[bank].rearrange("p (j t) -> p j t", j=4)),
                         reads=[pb[bank]], writes=[xb[dc] for dc in range(q * 4, q * 4 + 4)])
                else:
                    k.op(k.dve, lambda q=q, s=s, bank=bank: nc.vector.tensor_copy(
                        out=xT[:, q * 4:q * 4 + 4, s * 128:(s + 1) * 128],
                        in_=ps[bank].rearrange("p (j t) -> p j t", j=4)),
                         reads=[pb[bank]], writes=[xb[dc] for dc in range(q * 4, q * 4 + 4)])

    if "A" in phases:
      with ExitStack() as esA:
        tb = alloc_tiles(esA, "A_")
        bg_init(esA)
        hT, hb, sq, sqb, rstd, rstdb = tb.hT, tb.hb, tb.sq, tb.sqb, tb.rstd, tb.rstdb
        stg, stgb, stgf, stgfb = tb.stg, tb.stgb, tb.stgf, tb.stgfb
        for t in range(NT):
            tk = slice(t * T, (t + 1) * T)
            load_xT_from_tokmajor(tb, xs, t)
            rmsnorm_T(tb, 0)
            ffn(tb, WGU1, WD1)
            k.dma(k.pool, [(X1T[:, :, tk], tb.xT[:, :, :])], reads=tb.xb, owner=tb.xb[0])
            rmsnorm_T(tb, DC)
            pend = []
            FM = {0: "qa", 1: "qa", 2: "ka", 3: "ka", 6: "qm", 7: "qm", 8: "km", 9: "km", 12: "og", 13: "og"}
            TM = {4: "va", 5: "va", 8: "km", 9: "km", 10: "vm", 11: "vm"}
            for n in range(14):
                wt, wb = wload(tb, lambda s, n=n: [(s[:, :], WIN[n].rearrange("p c f -> p (c f)"))])
                wv = wt[:, :].rearrange("p (c f) -> p c f", c=16)
                if n in FM:
                    kind = FM[n]
                    for j in range(4):
                        oc = n * 4 + j
                        bank = 6 + oc % 2
                        fns = [lambda kc=kc, j=j, bank=bank: nc.tensor.matmul(
                            out=ps[bank], lhsT=wv[:, kc, j * 128:(j + 1) * 128], rhs=hT[:, kc, :],
                            start=(kc == 0), stop=(kc == 15)) for kc in range(16)]
                        k.op(k.pe, fns, reads=[wb, hb], writes=[pb[bank]])
                        while pend:
                            pend.pop(0)()
                        si = tb.si % 4
                        tb.si += 1
                        if kind in ("qa", "ka"):
                            fi = tb.fi % 2
                            tb.fi += 1
                            sj = fi
                            k.op(k.act, lambda fi=fi, bank=bank: nc.scalar.copy(out=stgf[fi][:, :], in_=ps[bank]),
                                 reads=[pb[bank]], writes=[stgfb[fi]])
                            k.op(k.act, lambda bank=bank, sj=sj: nc.scalar.activation(out=sq[sj][:, :], in_=ps[bank], func=AF.Square),
                                 reads=[pb[bank]], writes=[sqb[sj]])

                            def fin(kind=kind, fi=fi, si=si, sj=sj, oc=oc):
                                k.op(k.pe, lambda: nc.tensor.matmul(out=ps[1], lhsT=ones_bf[:, :], rhs=sq[sj][:, :],
                                                                    start=True, stop=True),
                                     reads=[sqb[sj], b_ones], writes=[pb[1]])
                                if kind == "qa":
                                    k.op(k.act, lambda: nc.scalar.activation(out=rstd[:, :], in_=ps[1], func=AF.Sqrt,
                                                                             scale=1.0, bias=eps_sb[:, 1:2]),
                                         reads=[pb[1], b_eps], writes=[rstdb])
                                else:
                                    k.op(k.act, lambda: nc.scalar.activation(out=rstd[:, :], in_=ps[1], func=AF.Sqrt,
                                                                             scale=1.0 / 128, bias=eps_sb[:, 0:1]),
                                         reads=[pb[1], b_eps], writes=[rstdb])
                                k.op(k.dve, lambda: nc.vector.reciprocal(out=rstd[:, :], in_=rstd[:, :]),
                                     reads=[rstdb], writes=[rstdb])
                                gc = 0 if kind == "qa" else 1
                                k.op(k.dve, lambda: nc.vector.scalar_tensor_tensor(
                                    out=stg[si][:, :], in0=stgf[fi][:, :], scalar=gsm_sb[:, gc:gc + 1], in1=rstd[:, :],
                                    op0=OP.mult, op1=OP.mult),
                                     reads=[stgfb[fi], rstdb, b_cst], writes=[stgb[si]])
                                dst_ = (QAT if kind == "qa" else KAT)[oc % 8, :, tk]
                                k.dma(k.pool, [(dst_, stg[si][:, :])], reads=[stgb[si]], owner=stgb[si])
                            pend.append(fin)
                            continue
                        elif kind == "og":
                            k.op(k.act, lambda si=si, bank=bank: nc.scalar.activation(out=stg[si][:, :], in_=ps[bank], func=AF.Sigmoid),
                                 reads=[pb[bank]], writes=[stgb[si]])
                            dst = OGT[oc % 8, :, tk]
                        elif kind == "km":
                            k.op(k.act, lambda si=si, bank=bank: nc.scalar.mul(out=stg[si][:, :], in_=ps[bank], mul=0.0625),
                                 reads=[pb[bank]], writes=[stgb[si]])
                            dst = KMT[oc % 8, :, tk]
                        else:
                            k.op(k.act, lambda si=si, bank=bank: nc.scalar.copy(out=stg[si][:, :], in_=ps[bank]),
                                 reads=[pb[bank]], writes=[stgb[si]])
                            dst = QMT[oc % 8, :, tk]
                        k.dma(k.pool, [(dst, stg[si][:, :])], reads=[stgb[si]], owner=stgb[si])
                if n in TM:
                    while pend:
                        pend.pop(0)()
                    kind = TM[n]
                    half = n % 2
                    dstT = {"va": VA, "km": KM, "vm": VM}[kind]
                    for s in range(4):
                        bank = 4 + s % 2
                        fns = [lambda kc=kc, s=s, bank=bank: nc.tensor.matmul(
                            out=ps[bank], lhsT=hT[:, kc, s * 128:(s + 1) * 128], rhs=wv[:, kc, :],
                            start=(kc == 0), stop=(kc == 15)) for kc in range(16)]
                        k.op(k.pe, fns, reads=[wb, hb], writes=[pb[bank]])
                        si = tb.si % 4
                        tb.si += 1
                        if kind == "km":
                            k.op(k.dve, lambda si=si, bank=bank: nc.vector.tensor_scalar(
                                out=stg[si][:, :], in0=ps[bank], scalar1=0.0625, scalar2=None, op0=OP.mult),
                                 reads=[pb[bank]], writes=[stgb[si]])
                        else:
                            k.op(k.dve, lambda si=si, bank=bank: nc.vector.tensor_copy(out=stg[si][:, :], in_=ps[bank]),
                                 reads=[pb[bank]], writes=[stgb[si]])
                        r0 = t * T + s * 128
                        k.dma(k.pool, [(dstT[r0:r0 + 128, half * 512:(half + 1) * 512], stg[si][:, :])],
                              reads=[stgb[si]], owner=stgb[si])
            while pend:
                pend.pop(0)()
            wt, wb = wload(tb, lambda s: [(s[:, 0:256], WING.rearrange("p c g -> p (c g)"))])
            wv = wt[:, 0:256].rearrange("p (c g) -> p c g", c=16)
            for s in range(4):
                bank = 4 + s % 2
                fns = [lambda kc=kc, s=s, bank=bank: nc.tensor.matmul(
                    out=ps[bank][:, 0:16], lhsT=hT[:, kc, s * 128:(s + 1) * 128], rhs=wv[:, kc, :],
                    start=(kc == 0), stop=(kc == 15)) for kc in range(16)]
                k.op(k.pe, fns, reads=[wb, hb], writes=[pb[bank]])
                fi = tb.fi % 2
                tb.fi += 1
                k.op(k.dve, lambda fi=fi, bank=bank: nc.vector.tensor_copy(out=stgf[fi][:, 0:16], in_=ps[bank][:, 0:16]),
                     reads=[pb[bank]], writes=[stgfb[fi]])
                r0 = t * T + s * 128
                k.dma(k.pool, [(GT[r0:r0 + 128, :], stgf[fi][:, 0:16])], reads=[stgfb[fi]], owner=stgfb[fi])
        while bgs.gen is not None or bgs.pending is not None:
            bg_step()
        bgs.gen = None
        k.end_phase(keep=(b_cst,))


    if "B" in phases:
      with ExitStack() as esB:
        sbt = lambda name, shape, dt: esB.enter_context(nc.sbuf_tensor(name, shape, dt))
        QT = [sbt(f"naQ{i}", [128, NTOK], BF16) for i in range(2)]
        KT = [sbt(f"naK{i}", [128, NTOK], BF16) for i in range(2)]
        VV = [sbt(f"naV{i}", [128, NCH, 128], BF16) for i in range(2)]
        YA = [sbt(f"naY{i}", [128, NTOK], BF16) for i in range(2)]
        RPB = [sbt(f"naR{i}", [128, 9, 128], F32) for i in range(2)]
        BMI = [sbt(f"naBI{i}", [128, 9, 128], F32) for i in range(2)]
        BME = [sbt(f"naBE{i}", [128, 9, 128], F32) for i in range(2)]
        cmI_sb = sbt("cmI_sb", [128, 9, 128], F32)
        cmE_sb = sbt("cmE_sb", [128, 9, 128], F32)
        rb_sb = sbt("rb_sb", [128, NE, 9, 2], F32)
        S1 = [sbt(f"naS{i}", [128, 9, 128], F32) for i in range(2)]
        PT = [sbt(f"naP{i}", [128, 9, 128], BF16) for i in range(2)]
        REC = [sbt(f"naRec{i}", [128, 128], F32) for i in range(2)]
        b_in = [Buf(f"naIn{i}") for i in range(2)]
        b_ya = [Buf(f"naY{i}") for i in range(2)]
        b_rp = [Buf(f"naR{i}") for i in range(2)]
        b_bm = [Buf(f"naBM{i}") for i in range(2)]
        b_cm = Buf("naCM")
        b_s1 = [Buf(f"naS{i}") for i in range(2)]
        b_pt = [Buf(f"naP{i}") for i in range(2)]
        b_rec = [Buf(f"naRec{i}") for i in range(2)]
        k.dma(k.sp, [(cmI_sb[:, :, :], cmI[:, :, :]), (cmE_sb[:, :, :], cmE[:, :, :]), (rb_sb[:, :, :, :], rb[:, :, :, :])],
              writes=[b_cm], owner=b_cm)
        epos = {0: 0, 1: 1, 14: 2, 15: 3}

        def na_load(h):
            i = h % 2
            prs = [(QT[i][:, :], QAT[h]), (KT[i][:, :], KAT[h])]
            for n0 in range(0, NCH, 16):
                prs.append((VV[i][:, n0:n0 + 16, :],
                            VA[n0 * 128:(n0 + 16) * 128, h * 128:(h + 1) * 128].rearrange("(n p) d -> p n d", p=128)))
            k.dma(k.sp, prs, writes=[b_in[i]], owner=b_in[i])
            k.dma(k.sp, [(RPB[i][:, :, :], rpbg[h])], writes=[b_rp[i]], owner=b_rp[i])

        na_load(0)
        for h in range(8):
            i = h % 2
            if h + 1 < 8:
                na_load(h + 1)
            k.op(k.pool, [lambda i=i: nc.gpsimd.tensor_tensor(out=BMI[i][:, :, :], in0=RPB[i][:, :, :], in1=cmI_sb[:, :, :], op=OP.add),
                          lambda i=i: nc.gpsimd.tensor_tensor(out=BME[i][:, :, :], in0=RPB[i][:, :, :], in1=cmE_sb[:, :, :], op=OP.add)],
                 reads=[b_rp[i], b_cm], writes=[b_bm[i]])
            def rng_(c):
                return max(0, 4 - c), min(8, NCH - 1 - c + 4)

            def na_S(c, i=i):
                j = c % 2
                o_lo, o_hi = rng_(c)
                sbanks = [pb[3 * j], pb[3 * j + 1], pb[3 * j + 2]]
                Sps = psall[:, j * 1536:j * 1536 + 1152].rearrange("p (o q) -> p o q", o=9)
                fns = [lambda o=o: nc.tensor.matmul(
                    out=Sps[:, o, :], lhsT=KT[i][:, (c + o - 4) * 128:(c + o - 3) * 128], rhs=QT[i][:, c * 128:(c + 1) * 128],
                    start=True, stop=True) for o in range(o_lo, o_hi + 1)]
                k.op(k.pe, fns, reads=[b_in[i]], writes=sbanks)

            def na_soft(c, i=i):
                j = c % 2
                o_lo, o_hi = rng_(c)
                no = o_hi - o_lo + 1
                sbanks = [pb[3 * j], pb[3 * j + 1], pb[3 * j + 2]]
                Sps = psall[:, j * 1536:j * 1536 + 1152].rearrange("p (o q) -> p o q", o=9)
                edge = (c % 16) in epos
                BM = BME[i] if edge else BMI[i]
                k.op(k.dve, lambda: nc.vector.tensor_tensor(
                    out=S1[j][:, o_lo:o_hi + 1, :], in0=Sps[:, o_lo:o_hi + 1, :], in1=BM[:, o_lo:o_hi + 1, :], op=OP.add),
                     reads=sbanks + [b_bm[i]], writes=[b_s1[j]])
                if edge:
                    e = (c // 16) * 4 + epos[c % 16]
                    k.op(k.pool, lambda: nc.gpsimd.tensor_tensor(
                        out=S1[j][:, o_lo:o_hi + 1, :].rearrange("p o (b q) -> p o b q", b=2),
                        in0=S1[j][:, o_lo:o_hi + 1, :].rearrange("p o (b q) -> p o b q", b=2),
                        in1=rb_sb[:, e, o_lo:o_hi + 1, :].unsqueeze(3).to_broadcast([128, no, 2, 64]), op=OP.add),
                         reads=[b_cm], writes=[b_s1[j]])
                k.op(k.act, lambda: nc.scalar.activation(
                    out=PT[j][:, o_lo:o_hi + 1, :], in_=S1[j][:, o_lo:o_hi + 1, :], func=AF.Exp),
                     reads=[b_s1[j]], writes=[b_pt[j]])

            def na_PV(c, i=i):
                j = c % 2
                o_lo, o_hi = rng_(c)
                ob = 6 + j
                fns = []
                for o in range(o_lo, o_hi + 1):
                    fns.append(lambda o=o: nc.tensor.matmul(
                        out=ps[ob][:, 0:128], lhsT=VV[i][:, c + o - 4, :], rhs=PT[j][:, o, :],
                        start=(o == o_lo), stop=(o == o_hi)))
                for o in range(o_lo, o_hi + 1):
                    fns.append(lambda o=o: nc.tensor.matmul(
                        out=ps[ob][:, 128:256], lhsT=ones_bf[:, :], rhs=PT[j][:, o, :],
                        start=(o == o_lo), stop=(o == o_hi)))
                k.op(k.pe, fns, reads=[b_in[i], b_pt[j], b_ones], writes=[pb[ob]])

            def na_fin(c, i=i):
                j = c % 2
                ob = 6 + j
                k.op(k.dve, lambda: nc.vector.reciprocal(out=REC[j][:, :], in_=ps[ob][:, 128:256]),
                     reads=[pb[ob]], writes=[b_rec[j]])
                k.op(k.dve, lambda: nc.vector.tensor_tensor(
                    out=YA[i][:, c * 128:(c + 1) * 128], in0=ps[ob][:, 0:128], in1=REC[j][:, :], op=OP.mult),
                     reads=[pb[ob], b_rec[j]], writes=[b_ya[i]])

            na_S(0)
            for c in range(NCH):
                if c + 1 < NCH:
                    na_S(c + 1)
                na_soft(c)
                if c >= 1:
                    na_fin(c - 1)
                na_PV(c)
            na_fin(NCH - 1)
            k.dma(k.sp, [(YT[h], YA[i][:, :])], reads=[b_ya[i]], owner=b_ya[i])
        k.end_phase(keep=(b_cst,))


    if "M" in phases:
      with ExitStack() as esM:
        sbt = lambda name, shape, dt: esM.enter_context(nc.sbuf_tensor(name, shape, dt))
        NG = NCH * 8
        G_sb = sbt("G_sb", [128, NCH, 16], F32)
        gbias_sb = sbt("gbias_sb", [128, 16], F32)
        keep_sb = sbt("keep_sb", [128, 2, NCH], F32)
        LF = sbt("LF", [128, NCH, 8], F32)
        BS = sbt("BS", [128, NCH, 8], F32)
        IB = sbt("IB", [128, NCH, 8], F32)
        KS = sbt("KS", [128, NCH, 8], F32)
        DEC = sbt("DEC", [128, NCH, 8], F32)
        b_g = Buf("mG"); b_lf = Buf("mLF"); b_bs = Buf("mBS"); b_ib = Buf("mIB"); b_ks = Buf("mKS"); b_dec = Buf("mDEC")
        b_gb = Buf("mGB")
        k.dma(k.sp, [(G_sb[:, n0:n0 + 16, :], GT[n0 * 128:(n0 + 16) * 128, :].rearrange("(n p) g -> p n g", p=128))
                     for n0 in range(0, NCH, 16)], writes=[b_g], owner=b_g)
        k.dma(k.sp, [(gbias_sb[:, :], gbias[:, :]), (keep_sb[:, :, :], keep[:, :, :])], writes=[b_gb], owner=b_gb)
        k.op(k.dve, lambda: nc.vector.tensor_tensor(out=G_sb[:, :, :], in0=G_sb[:, :, :],
                                                    in1=gbias_sb[:, :].unsqueeze(1).to_broadcast([128, NCH, 16]), op=OP.add),
             reads=[b_gb], writes=[b_g])
        k.op(k.act, lambda: nc.scalar.activation(out=LF[:, :, :], in_=G_sb[:, :, 8:16], func=AF.Exp, scale=-1.0),
             reads=[b_g], writes=[b_lf])
        k.op(k.act, lambda: nc.scalar.activation(out=LF[:, :, :], in_=LF[:, :, :], func=AF.Ln, bias=eps_sb[:, 2:3]),
             reads=[b_lf, b_eps], writes=[b_lf])
        k.op(k.dve, lambda: nc.vector.tensor_scalar(out=LF[:, :, :], in0=LF[:, :, :], scalar1=-1.0, scalar2=None, op0=OP.mult),
             reads=[b_lf], writes=[b_lf])
        LF2 = LF[:, :, :].rearrange("p n g -> p (n g)")
        k.op(k.pe, lambda: nc.tensor.matmul(out=ps[0][:, 0:NG], lhsT=triF, rhs=LF2, start=True, stop=True),
             reads=[b_lf, b_cst], writes=[pb[0]])
        k.op(k.pe, lambda: nc.tensor.matmul(out=ps[1][:, 0:NG], lhsT=triB, rhs=LF2, start=True, stop=True),
             reads=[b_lf, b_cst], writes=[pb[1]])
        k.op(k.pe, lambda: nc.tensor.matmul(out=ps[2][:, 0:NG], lhsT=ones_f, rhs=LF2, start=True, stop=True),
             reads=[b_lf, b_cst], writes=[pb[2]])
        k.op(k.dve, [lambda: nc.vector.tensor_copy(out=BS[:, :, 0:4], in_=ps[0][:, 0:NG].rearrange("p (n g) -> p n g", g=8)[:, :, 0:4]),
                     lambda: nc.vector.tensor_copy(out=BS[:, :, 4:8], in_=ps[1][:, 0:NG].rearrange("p (n g) -> p n g", g=8)[:, :, 4:8])],
             reads=[pb[0], pb[1]], writes=[b_bs])
        k.op(k.dve, lambda: nc.vector.tensor_tensor(out=IB[:, :, :], in0=G_sb[:, :, 0:8], in1=BS[:, :, :], op=OP.subtract),
             reads=[b_g, b_bs], writes=[b_ib])
        k.op(k.act, lambda: nc.scalar.activation(out=KS[:, :, :], in_=IB[:, :, :], func=AF.Exp), reads=[b_ib], writes=[b_ks])
        k.op(k.act, lambda: nc.scalar.activation(out=DEC[:, :, :].rearrange("p n g -> p (n g)"), in_=ps[2][:, 0:NG], func=AF.Exp),
             reads=[pb[2]], writes=[b_dec])
        k.op(k.dve, [lambda d=d: nc.vector.tensor_tensor(
            out=DEC[:, :, d * 4:(d + 1) * 4], in0=DEC[:, :, d * 4:(d + 1) * 4],
            in1=keep_sb[:, d, :].unsqueeze(2).to_broadcast([128, NCH, 4]), op=OP.mult) for d in range(2)],
             reads=[b_gb], writes=[b_dec])

        QT2 = sbt("mQT", [128, 2, NTOK], BF16)
        KT2 = sbt("mKT", [128, 2, NTOK], BF16)
        Ktok = sbt("mK", [128, NCH, 256], BF16)
        Vaug = sbt("mV", [128, NCH, 264], BF16)
        b_hd = Buf("mHead")
        b_v1 = Buf("mVones")
        k.op(k.pool, lambda: nc.gpsimd.memset(Vaug[:, :, 256:264], 1.0), writes=[b_v1])

        class DS:
            pass
        dd = []
        for d in range(2):
            s_ = DS()
            nm = lambda x, d=d: f"m{x}{d}"
            def two(name, shape, dt, s_=s_, nm=nm):
                setattr(s_, name, [sbt(nm(name) + f"_{q}", shape, dt) for q in range(2)])
                setattr(s_, "b_" + name, [Buf(nm(name) + f"_{q}") for q in range(2)])
            def one(name, shape, dt, s_=s_, nm=nm):
                setattr(s_, name, sbt(nm(name), shape, dt))
                setattr(s_, "b_" + name, Buf(nm(name)))
            two("diag", [128, 128], F32); two("DT", [128, 128], F32); two("EB", [128, 128], F32)
            two("DTm", [128, 128], F32); two("SD", [128, 128], BF16); two("Qp", [128, 2, 128], BF16)
            two("Kp", [128, 256], BF16); two("dcl", [128, 128], F32); two("hbuf", [128, 2, 128], F32)
            two("hl", [128, 2, 128], F32); two("og", [128, 2, 128], BF16); two("sqh", [128, 2, 128], BF16)
            one("rs", [128, 128], F32); one("tmp", [128, 2, 128], F32); one("ym", [128, 2, 128], BF16)
            one("U32", [128, 2, 256], F32); two("Ubf", [128, 2, 256], BF16)
            one("n32", [128, 2], F32); two("nB", [128, 2, 128], BF16)
            s_.p_a = pb[4 * d]; s_.p_nd = pb[4 * d + 1]; s_.p_du = [pb[4 * d + 2], pb[4 * d + 3]]
            dd.append(s_)
        hbb = {}

        def geom(i, d):
            n = i if d == 0 else NCH - 1 - i
            nprev = n - 1 if d == 0 else n + 1
            first = (n < NCH // 2) if d == 0 else (n >= NCH // 2)
            return n, nprev, first

        def m_prep(h, i, d):
            s_ = dd[d]; q = i % 2
            n, nprev, first = geom(i, d)
            g = d * 4 + h
            ck = slice(n * 128, (n + 1) * 128)
            A = ps[4 * d]
            Bt = A[:, 0:128]
            Sp = A[:, 128:256]
            mask = triF if d == 0 else triB
            k.op(k.pool, lambda: nc.gpsimd.tensor_scalar(
                out=s_.diag[q][:, :], in0=ident, scalar1=BS[:, n, g:g + 1], scalar2=1.0, op0=OP.mult, op1=OP.mult),
                 reads=[b_bs, b_cst], writes=[s_.b_diag[q]])
            yield
            k.op(k.pe, lambda: nc.tensor.matmul(out=Bt, lhsT=ones_f, rhs=s_.diag[q][:, :], start=True, stop=True),
                 reads=[s_.b_diag[q], b_cst], writes=[s_.p_a])
            yield
            k.op(k.pe, [lambda dkc=dkc: nc.tensor.matmul(
                out=Sp, lhsT=KT2[:, dkc, ck], rhs=QT2[:, dkc, ck], start=(dkc == 0), stop=(dkc == 1))
                for dkc in range(2)], reads=[b_hd], writes=[s_.p_a])
            yield
            k.op(k.act, lambda: nc.scalar.activation(out=s_.DT[q][:, :], in_=Bt, func=AF.Exp, bias=IB[:, n, g:g + 1]),
                 reads=[s_.p_a, b_ib], writes=[s_.b_DT[q]])
            yield
            if i > 0:
                k.op(k.act, lambda: nc.scalar.activation(out=s_.EB[q][:, :], in_=Bt, func=AF.Exp),
                     reads=[s_.p_a], writes=[s_.b_EB[q]])
                yield
            k.op(k.pool, lambda: nc.gpsimd.tensor_tensor(out=s_.DTm[q][:, :], in0=s_.DT[q][:, :], in1=mask, op=OP.mult),
                 reads=[s_.b_DT[q], b_cst], writes=[s_.b_DTm[q]])
            yield
            k.op(k.dve, lambda: nc.vector.tensor_tensor(out=s_.SD[q][:, :], in0=Sp, in1=s_.DTm[q][:, :], op=OP.mult),
                 reads=[s_.p_a, s_.b_DTm[q]], writes=[s_.b_SD[q]])
            yield
            if i > 0:
                k.op(k.dve, lambda: nc.vector.scalar_tensor_tensor(
                    out=s_.Qp[q][:, :, :], in0=QT2[:, :, ck], scalar=DEC[:, nprev, g:g + 1],
                    in1=s_.EB[q][:, :].unsqueeze(1).to_broadcast([128, 2, 128]), op0=OP.mult, op1=OP.mult),
                     reads=[b_hd, b_dec, s_.b_EB[q]], writes=[s_.b_Qp[q]])
                yield
            if i < NCH - 1:
                k.op(k.pool, lambda: nc.gpsimd.tensor_scalar(
                    out=s_.Kp[q][:, :], in0=Ktok[:, n, :], scalar1=KS[:, n, g:g + 1], scalar2=1.0, op0=OP.mult, op1=OP.mult),
                     reads=[b_hd, b_ks], writes=[s_.b_Kp[q]])
                yield
                B2 = ps[4 * d + 2 + q]
                fns = [lambda dkc=dkc: nc.tensor.matmul(
                    out=B2[:, dkc * 256:(dkc + 1) * 256], lhsT=s_.Kp[q][:, dkc * 128:(dkc + 1) * 128], rhs=Vaug[:, n, 0:256],
                    start=True, stop=True) for dkc in range(2)]
                k.op(k.pe, fns, reads=[s_.b_Kp[q], b_hd], writes=[s_.p_du[q]])
                yield

        def m_chain(h, i, d):
            s_ = dd[d]; q = i % 2
            n, nprev, first = geom(i, d)
            g = d * 4 + h
            B1 = ps[4 * d + 1]
            fns = []
            for dvc in range(2):
                fns.append(lambda dvc=dvc: nc.tensor.matmul(
                    out=B1[:, dvc * 128:(dvc + 1) * 128], lhsT=Vaug[:, n, dvc * 128:(dvc + 1) * 128], rhs=s_.SD[q][:, :],
                    start=True, stop=(i == 0)))
                if i > 0:
                    for dkc in range(2):
                        fns.append(lambda dvc=dvc, dkc=dkc: nc.tensor.matmul(
                            out=B1[:, dvc * 128:(dvc + 1) * 128], lhsT=s_.Ubf[1 - q][:, dkc, dvc * 128:(dvc + 1) * 128],
                            rhs=s_.Qp[q][:, dkc, :], start=False, stop=(dkc == 1)))
            fns.append(lambda: nc.tensor.matmul(out=B1[:, 256:384], lhsT=ones_bf[:, :], rhs=s_.SD[q][:, :],
                                                start=True, stop=(i == 0)))
            if i > 0:
                for dkc in range(2):
                    fns.append(lambda dkc=dkc: nc.tensor.matmul(
                        out=B1[:, 256:384], lhsT=s_.nB[1 - q][:, dkc, :], rhs=s_.Qp[q][:, dkc, :], start=False, stop=(dkc == 1)))
            rd = [b_hd, b_v1, s_.b_SD[q], b_ones] + ([s_.b_Ubf[1 - q], s_.b_Qp[q], s_.b_nB[1 - q]] if i > 0 else [])
            k.op(k.pe, fns, reads=rd, writes=[s_.p_nd])
            yield
            if i < NCH - 1:
                B2 = ps[4 * d + 2 + q]
                A = ps[4 * d]
                k.op(k.pe, [lambda dkc=dkc: nc.tensor.matmul(
                    out=A[:, 384 + dkc:385 + dkc], lhsT=s_.Kp[q][:, dkc * 128:(dkc + 1) * 128], rhs=Vaug[:, n, 256:257],
                    start=True, stop=True) for dkc in range(2)], reads=[s_.b_Kp[q], b_v1], writes=[s_.p_a])
                yield
                U2 = s_.U32[:, :, :].rearrange("p c v -> p (c v)")
                if i == 0:
                    k.op(k.dve, lambda: nc.vector.tensor_copy(out=U2, in_=B2[:, :]), reads=[s_.p_du[q]], writes=[s_.b_U32])
                    yield
                    k.op(k.dve, lambda: nc.vector.tensor_copy(out=s_.n32[:, :], in_=A[:, 384:386]), reads=[s_.p_a], writes=[s_.b_n32])
                    yield
                else:
                    k.op(k.dve, lambda: nc.vector.scalar_tensor_tensor(
                        out=U2, in0=U2, scalar=DEC[:, nprev, g:g + 1], in1=B2[:, :], op0=OP.mult, op1=OP.add),
                         reads=[s_.p_du[q], b_dec], writes=[s_.b_U32])
                    yield
                    k.op(k.dve, lambda: nc.vector.scalar_tensor_tensor(
                        out=s_.n32[:, :], in0=s_.n32[:, :], scalar=DEC[:, nprev, g:g + 1], in1=A[:, 384:386], op0=OP.mult, op1=OP.add),
                         reads=[s_.p_a, b_dec], writes=[s_.b_n32])
                    yield
                k.op(k.act, lambda: nc.scalar.copy(out=s_.Ubf[q][:, :, :], in_=s_.U32[:, :, :]), reads=[s_.b_U32], writes=[s_.b_Ubf[q]])
                yield
                k.op(k.act, lambda: nc.scalar.copy(out=s_.nB[q][:, :, :], in_=s_.n32[:, :].unsqueeze(2).to_broadcast([128, 2, 128])),
                     reads=[s_.b_n32], writes=[s_.b_nB[q]])
                yield

        def m_h(h, i, d):
            s_ = dd[d]; q = i % 2
            n, nprev, first = geom(i, d)
            ck = slice(n * 128, (n + 1) * 128)
            B1 = ps[4 * d + 1]
            if not first:
                k.dma(k.sp, [(s_.hl[q][:, :, :], HB[2 * h:2 * h + 2, :, ck].rearrange("c p t -> p c t"))],
                      reads=[hbb[(h, n)]], writes=[s_.b_hl[q]], owner=s_.b_hl[q])
                yield
                k.dma(k.sp, [(s_.og[q][:, :, :], OGT[2 * h:2 * h + 2, :, ck].rearrange("c p t -> p c t"))],
                      writes=[s_.b_og[q]], owner=s_.b_og[q])
                yield
            k.op(k.act, lambda: nc.scalar.activation(out=s_.dcl[q][:, :], in_=B1[:, 256:384], func=AF.Abs),
                 reads=[s_.p_nd], writes=[s_.b_dcl[q]])
            yield
            k.op(k.dve, lambda: nc.vector.tensor_scalar(out=s_.dcl[q][:, :], in0=s_.dcl[q][:, :], scalar1=1.0, scalar2=None, op0=OP.max),
                 reads=[s_.b_dcl[q]], writes=[s_.b_dcl[q]])
            yield
            k.op(k.act, lambda: nc.scalar.activation(out=s_.dcl[q][:, :], in_=s_.dcl[q][:, :], func=AF.Ln),
                 reads=[s_.b_dcl[q]], writes=[s_.b_dcl[q]])
            yield
            k.op(k.act, lambda: nc.scalar.activation(out=s_.dcl[q][:, :], in_=s_.dcl[q][:, :], func=AF.Exp, scale=-1.0),
                 reads=[s_.b_dcl[q]], writes=[s_.b_dcl[q]])
            yield
            k.op(k.dve, lambda: nc.vector.tensor_tensor(
                out=s_.hbuf[q][:, :, :], in0=B1[:, 0:256].rearrange("p (c t) -> p c t", c=2),
                in1=s_.dcl[q][:, :].unsqueeze(1).to_broadcast([128, 2, 128]), op=OP.mult),
                 reads=[s_.p_nd, s_.b_dcl[q]], writes=[s_.b_hbuf[q]])
            yield
            if first:
                hbb[(h, n)] = Buf(f"hb{h}_{n}")
                k.dma(k.sp, [(HB[2 * h:2 * h + 2, :, ck].rearrange("c p t -> p c t"), s_.hbuf[q][:, :, :])],
                      reads=[s_.b_hbuf[q]], writes=[hbb[(h, n)]], owner=s_.b_hbuf[q])
                yield
            else:
                k.op(k.pool, lambda: nc.gpsimd.tensor_tensor(out=s_.hl[q][:, :, :], in0=s_.hl[q][:, :, :], in1=s_.hbuf[q][:, :, :], op=OP.add),
                     reads=[s_.b_hbuf[q]], writes=[s_.b_hl[q]])
                yield
                k.op(k.act, lambda: nc.scalar.activation(out=s_.sqh[q][:, :, :], in_=s_.hl[q][:, :, :], func=AF.Square),
                     reads=[s_.b_hl[q]], writes=[s_.b_sqh[q]])
                yield

        def m_fin2(h, i, d):
            s_ = dd[d]; q = i % 2
            n, nprev, first = geom(i, d)
            if first:
                return
            ck = slice(n * 128, (n + 1) * 128)
            A = ps[4 * d]
            k.op(k.pe, [lambda dvc=dvc: nc.tensor.matmul(
                out=A[:, 256:384], lhsT=ones_bf[:, :], rhs=s_.sqh[q][:, dvc, :], start=(dvc == 0), stop=(dvc == 1))
                for dvc in range(2)], reads=[s_.b_sqh[q], b_ones], writes=[s_.p_a])
            yield
            k.op(k.act, lambda: nc.scalar.activation(out=s_.rs[:, :], in_=A[:, 256:384], func=AF.Ln,
                                                     scale=1.0 / 256, bias=eps_sb[:, 0:1]),
                 reads=[s_.p_a, b_eps], writes=[s_.b_rs])
            yield
            k.op(k.act, lambda: nc.scalar.activation(out=s_.rs[:, :], in_=s_.rs[:, :], func=AF.Exp, scale=-0.5),
                 reads=[s_.b_rs], writes=[s_.b_rs])
            yield
            k.op(k.dve, [lambda dvc=dvc: nc.vector.scalar_tensor_tensor(
                out=s_.tmp[:, dvc, :], in0=s_.hl[q][:, dvc, :], scalar=gsm_sb[:, 2 + 2 * h + dvc:3 + 2 * h + dvc],
                in1=s_.rs[:, :], op0=OP.mult, op1=OP.mult) for dvc in range(2)],
                 reads=[s_.b_hl[q], s_.b_rs, b_cst], writes=[s_.b_tmp])
            yield
            k.op(k.pool, lambda: nc.gpsimd.tensor_tensor(out=s_.ym[:, :, :], in0=s_.tmp[:, :, :], in1=s_.og[q][:, :, :], op=OP.mult),
                 reads=[s_.b_tmp, s_.b_og[q]], writes=[s_.b_ym])
            yield
            k.dma(k.sp, [(YT[8 + 2 * h:10 + 2 * h, :, ck].rearrange("c p t -> p c t"), s_.ym[:, :, :])],
                  reads=[s_.b_ym], owner=s_.b_ym)
            yield

        for h in range(4):
            prs = [(QT2[:, :, :], QMT[2 * h:2 * h + 2].rearrange("c p t -> p c t")),
                   (KT2[:, :, :], KMT[2 * h:2 * h + 2].rearrange("c p t -> p c t"))]
            for n0 in range(0, NCH, 16):
                rows = slice(n0 * 128, (n0 + 16) * 128)
                prs.append((Ktok[:, n0:n0 + 16, :], KM[rows, h * 256:(h + 1) * 256].rearrange("(n p) d -> p n d", p=128)))
                prs.append((Vaug[:, n0:n0 + 16, 0:256], VM[rows, h * 256:(h + 1) * 256].rearrange("(n p) d -> p n d", p=128)))
            k.dma(k.sp, prs, writes=[b_hd], owner=b_hd)
            def rr(*gens):
                gens = list(gens)
                while gens:
                    for g_ in list(gens):
                        try:
                            next(g_)
                        except StopIteration:
                            gens.remove(g_)

            def seq(*gens):
                for g_ in gens:
                    for _ in g_:
                        pass

            seq(m_prep(h, 0, 0), m_prep(h, 0, 1))
            for i in range(NCH):
                if i + 1 < NCH:
                    seq(m_prep(h, i + 1, 0), m_prep(h, i + 1, 1))
                seq(m_chain(h, i, 0), m_chain(h, i, 1))
                if i >= 1:
                    seq(m_fin2(h, i - 1, 0), m_fin2(h, i - 1, 1))
                seq(m_h(h, i, 0), m_h(h, i, 1))
            seq(m_fin2(h, NCH - 1, 0), m_fin2(h, NCH - 1, 1))
        k.end_phase(keep=(b_cst,))


    if "C" in phases:
      with ExitStack() as esC:
        tb = alloc_tiles(esC, "C_")
        sbt = lambda name, shape, dt: esC.enter_context(nc.sbuf_tensor(name, shape, dt))
        hT, hb, xT, xb, sg, sgb = tb.hT, tb.hb, tb.xT, tb.xb, tb.sg, tb.sgb
        wpp_sb = sbt("wpp_sb", [128, 2, D], BF16)
        b_wpp = Buf("wpp")
        k.dma(k.sp, [(wpp_sb[:, :, :], WPP[:, :, :])], writes=[b_wpp], owner=b_wpp)
        pin = [sbt(f"pin{i}", [128, 256], F32) for i in range(2)]
        pinb = [Buf(f"pin{i}") for i in range(2)]
        peT = sbt("peT", [128, 2, T], BF16)
        b_peT = Buf("peT")
        sg2 = [sbt(f"sg2_{j}", [128, T], F32) for j in range(2)]
        sg2b = [Buf(f"sg2_{j}") for j in range(2)]
        for t in range(NT):
            tk = slice(t * T, (t + 1) * T)
            if t == 0:
                k.dma(k.sp, [(hT[:, :, :], YT[:, :, tk].rearrange("c p t -> p c t"))], writes=[hb], owner=hb)
            for g4 in range(4):
                k.dma(k.sp, [(xT[:, 4 * g4:4 * g4 + 4, :], X1T[:, 4 * g4:4 * g4 + 4, tk])],
                      writes=xb[4 * g4:4 * g4 + 4], owner=xb[4 * g4])
            for n in range(4):
                wt, wb = wload(tb, lambda s, n=n: [(s[:, :], WOUT[n].rearrange("p c f -> p (c f)"))])
                wv = wt[:, :].rearrange("p (c f) -> p c f", c=16)
                for j in range(4):
                    dc = n * 4 + j
                    bank = 6 + dc % 2
                    fns = [lambda kc=kc, j=j, bank=bank, wv=wv: nc.tensor.matmul(
                        out=ps[bank], lhsT=wv[:, kc, j * 128:(j + 1) * 128], rhs=hT[:, kc, :],
                        start=(kc == 0), stop=(kc == 15)) for kc in range(16)]
                    k.op(k.pe, fns, reads=[wb, hb], writes=[pb[bank]])
                    k.op(k.dve, lambda dc=dc, bank=bank: nc.vector.tensor_tensor(
                        out=xT[:, dc, :], in0=ps[bank], in1=xT[:, dc, :], op=OP.add),
                         reads=[pb[bank]], writes=[xb[dc]])
            rmsnorm_T(tb, 2 * DC)
            ffn(tb, WGU2, WD2)
            rmsnorm_T(tb, 3 * DC)
            for s in range(4):
                i = s % 2
                r0 = t * T + s * 128
                k.dma(k.sp, [(pin[i][:, :], pes[r0:r0 + 128, :])], writes=[pinb[i]], owner=pinb[i])
                k.op(k.pe, [lambda k2=k2, i=i: nc.tensor.transpose(
                    out=ps[i][:, k2 * 128:(k2 + 1) * 128], in_=pin[i][:, k2 * 128:(k2 + 1) * 128], identity=ident)
                    for k2 in range(2)], reads=[pinb[i], b_cst], writes=[pb[i]])
                k.op(k.act, lambda i=i, s=s: nc.scalar.copy(
                    out=peT[:, :, s * 128:(s + 1) * 128], in_=ps[i][:, 0:256].rearrange("p (c t) -> p c t", c=2)),
                     reads=[pb[i]], writes=[b_peT])
            for n in range(4):
                wt, wb = wload(tb, lambda s, n=n: [(s[:, :], WPG[n].rearrange("p c f -> p (c f)"))])
                wv = wt[:, :].rearrange("p (c f) -> p c f", c=16)
                for j in range(4):
                    dc = n * 4 + j
                    bank = 6 + dc % 2
                    pbank = 4 + dc % 2
                    jj = dc % 2
                    fns = [lambda kc=kc, j=j, bank=bank, wv=wv: nc.tensor.matmul(
                        out=ps[bank], lhsT=wv[:, kc, j * 128:(j + 1) * 128], rhs=hT[:, kc, :],
                        start=(kc == 0), stop=(kc == 15)) for kc in range(16)]
                    k.op(k.pe, fns, reads=[wb, hb], writes=[pb[bank]])
                    k.op(k.pe, [lambda k2=k2, dc=dc, pbank=pbank: nc.tensor.matmul(
                        out=ps[pbank], lhsT=wpp_sb[:, k2, dc * 128:(dc + 1) * 128], rhs=peT[:, k2, :],
                        start=(k2 == 0), stop=(k2 == 1)) for k2 in range(2)], reads=[b_wpp, b_peT], writes=[pb[pbank]])
                    k.op(k.act, lambda jj=jj, bank=bank: nc.scalar.activation(out=sg[jj][:, :], in_=ps[bank], func=AF.Sigmoid),
                         reads=[pb[bank]], writes=[sgb[jj]])
                    k.op(k.dve, lambda jj=jj, pbank=pbank: nc.vector.tensor_tensor(
                        out=sg2[jj][:, :], in0=ps[pbank], in1=sg[jj][:, :], op=OP.mult),
                         reads=[pb[pbank], sgb[jj]], writes=[sg2b[jj]])
                    k.op(k.pool, lambda jj=jj, dc=dc: nc.gpsimd.tensor_tensor(
                        out=xT[:, dc, :], in0=xT[:, dc, :], in1=sg2[jj][:, :], op=OP.add),
                         reads=[sg2b[jj]], writes=[xb[dc]])
            if t + 1 < NT:
                tk1 = slice((t + 1) * T, (t + 2) * T)
                k.dma(k.sp, [(hT[:, :, :], YT[:, :, tk1].rearrange("c p t -> p c t"))], writes=[hb], owner=hb)
            xin, xinb = tb.xin, tb.xinb
            for s in range(4):
                i = s % 2
                r0 = t * T + s * 128
                for q in range(4):
                    bank = (s * 4 + q) % 8
                    fns = [lambda dc=dc, j=j, bank=bank, s=s: nc.tensor.transpose(
                        out=ps[bank][:, j * 128:(j + 1) * 128], in_=xT[:, dc, s * 128:(s + 1) * 128], identity=ident)
                        for j, dc in enumerate(range(q * 4, q * 4 + 4))]
                    k.op(k.pe, fns, reads=[xb[dc] for dc in range(q * 4, q * 4 + 4)] + [b_cst], writes=[pb[bank]])
                    if q % 2 == 0:
                        k.op(k.act, lambda q=q, i=i, bank=bank: nc.scalar.copy(out=xin[i][:, q * 512:(q + 1) * 512], in_=ps[bank]),
                             reads=[pb[bank]], writes=[xinb[i]])
                    else:
                        k.op(k.dve, lambda q=q, i=i, bank=bank: nc.vector.tensor_copy(out=xin[i][:, q * 512:(q + 1) * 512], in_=ps[bank]),
                             reads=[pb[bank]], writes=[xinb[i]])
                k.dma(k.pool, [(y[r0:r0 + 128, :], xin[i][:, :])], reads=[xinb[i]], owner=xinb[i])
        k.end_phase(keep=(b_cst,))


    k.barrier()
    es.close()
    return nc, k


def _consts():
    ident = np.eye(128, dtype=np.float32)
    s = np.arange(128)
    triF = (s[:, None] <= s[None, :]).astype(np.float32)
    triB = (s[:, None] >= s[None, :]).astype(np.float32)
    ones = np.ones((128, 128), np.float32)
    zeros = np.zeros((128, 128), np.float32)
    return np.ascontiguousarray(np.concatenate([ident, triF, triB, ones, zeros], axis=1))


def _na_tables(rpb, seq_rows):
    a = np.arange(2)[:, None, None, None, None]
    kc = np.arange(64)[None, :, None, None, None]
    o = np.arange(9)[None, None, :, None, None]
    b = np.arange(2)[None, None, None, :, None]
    qc = np.arange(64)[None, None, None, None, :]
    dr = 2 * (o - 4) + a - b + 0 * kc + 0 * qc
    dcol = np.clip(kc - qc, -15, 15) + 15 + 0 * dr
    cs = np.clip(qc - 8, 0, 48)
    colok = (kc >= cs) & (kc < cs + 16)
    inwin = np.abs(dr) <= 7
    dri = np.clip(dr + 7, 0, 14)
    g = rpb[:, dri, dcol]
    g = np.where(inwin[None], g, np.float32(0.0))
    rpbg = np.ascontiguousarray(g.reshape(8, 128, 9, 128).astype(np.float32))
    cmE = np.where(inwin & colok, 0.0, NEG).astype(np.float32).reshape(128, 9, 128)
    cmI = np.where(inwin & colok & (dr >= -4) & (dr <= 3), 0.0, NEG).astype(np.float32).reshape(128, 9, 128)
    R = sum(seq_rows)
    nch = R // 2
    seq_of_row = np.concatenate([np.full(r, i) for i, r in enumerate(seq_rows)])
    start = np.concatenate([[0], np.cumsum(seq_rows)[:-1]])
    ne = max(4, (nch // 16) * 4)
    rb = np.full((128, ne, 9, 2), NEG, np.float32)
    pos = {0: 0, 1: 1, 14: 2, 15: 3}
    for c in range(nch):
        if c % 16 not in pos:
            continue
        e = (c // 16) * 4 + pos[c % 16]
        for bb in range(2):
            qr = 2 * c + bb
            si = seq_of_row[qr]
            r0 = start[si]; Rs = seq_rows[si]
            rs = r0 + min(max(qr - r0 - 4, 0), Rs - 8)
            for oo in range(9):
                for aa in range(2):
                    kr = 2 * (c + oo - 4) + aa
                    if 0 <= kr < R and seq_of_row[kr] == si and rs <= kr <= rs + 7:
                        rb[aa * 64:(aa + 1) * 64, e, oo, bb] = 0.0
    keep = np.ones((128, 2, nch), np.float32)
    for s0 in start[1:]:
        cst = s0 // 2
        keep[:, 0, cst - 1] = 0.0
        keep[:, 1, cst] = 0.0
    return rpbg, cmE, cmI, rb, keep


def host_inputs(x_stream, p_stream, W, seq_rows):
    f = lambda a: np.ascontiguousarray(a, dtype=np.float32)
    gvec = lambda g: g.reshape(DC, 128).T
    rpbg, cmE, cmI, rb, keep = _na_tables(W["rpb"], seq_rows)
    gsm = np.zeros((128, 16), np.float32)
    gsm[:, 0] = W["g_qn"]; gsm[:, 1] = W["g_kn"]
    gsm[:, 2:10] = W["g_mh"].reshape(8, 128).T
    gb = np.concatenate([W["b_igate"].reshape(8), W["b_fgate"].reshape(8)])
    return {
        "xs": f(x_stream), "pes": f(p_stream),
        "w1g": f(W["w_ffn1_gate"]), "w1u": f(W["w_ffn1_up"]), "w1d": f(W["w_ffn1_down"]),
        "w2g": f(W["w_ffn2_gate"]), "w2u": f(W["w_ffn2_up"]), "w2d": f(W["w_ffn2_down"]),
        "win": f(W["w_in"]), "wout": f(W["w_out"]), "wpg": f(W["w_ple_gate"]), "wpp": f(W["w_ple_proj"]),
        "gv": f(np.concatenate([gvec(W["g_ffn1"]), gvec(W["g_mix"]), gvec(W["g_ffn2"]), gvec(W["g_ple"])], axis=1)),
        "gsm": f(gsm), "gbias": f(np.broadcast_to(gb[None, :], (128, 16))),
        "cst": _consts(), "rpbg": rpbg, "cmI": cmI, "cmE": cmE, "rb": rb, "keep": keep,
    }


_NC_CACHE = {}


def kernel(**inputs):
    NTOK = 8192
    f32 = lambda a: np.asarray(a, dtype=np.float32)
    W = {}
    for name in ("g_ffn1", "w_ffn1_gate", "w_ffn1_up", "w_ffn1_down", "g_mix", "w_in", "b_igate", "b_fgate",
                 "g_qn", "g_kn", "rpb", "g_mh", "w_out", "g_ffn2", "w_ffn2_gate", "w_ffn2_up", "w_ffn2_down",
                 "g_ple", "w_ple_gate", "w_ple_proj"):
        W[name] = f32(inputs[name])[0]
    xp = f32(inputs["x_prompt"]); xsm = f32(inputs["x_sample"])
    pp = f32(inputs["p_prompt"])[0]; psm = f32(inputs["p_sample"])[0]
    zx = np.zeros((NTOK, D), np.float32); zp = np.zeros((NTOK, 256), np.float32)
    streams = [(zx, zp, [128])] * 8
    streams[0] = (xsm[0], psm[0], [128])
    streams[2] = (xsm[1], psm[1], [128])
    streams[4] = (xp.reshape(NTOK, D), pp.reshape(NTOK, 256), [32, 32, 32, 32])
    in_maps = [host_inputs(x, p, W, rows) for (x, p, rows) in streams]
    if "nc" not in _NC_CACHE:
        _NC_CACHE["nc"] = build(NTOK)[0]
    res = run_bass_kernel_spmd(_NC_CACHE["nc"], in_maps, core_ids=list(range(8)))
    outs = [np.asarray(res.results[c]["y"], dtype=np.float32) for c in (0, 2, 4)]
    y_sample = np.stack([outs[0], outs[1]], axis=0)
    y_prompt = outs[2].reshape(4, 2048, D)
    return (y_prompt, y_sample)
```

```python
import os
from contextlib import ExitStack

import numpy as np
import concourse.bass as bass
import concourse.mybir as mybir
from concourse.bass_utils import run_bass_kernel_spmd

F32 = mybir.dt.float32
BF16 = mybir.dt.bfloat16
AF = mybir.ActivationFunctionType
OP = mybir.AluOpType

D = 2048
DC = 16
DFF = 5632
FC = 44
DIN = 7184
T = 512
EPS = 1e-6
NEG = -1000.0


class Buf:
    __slots__ = ("name", "w", "r", "dsem", "dcnt")

    def __init__(self, name):
        self.name = name
        self.w = {}
        self.r = {}
        self.dsem = None
        self.dcnt = 0


class Eng:
    def __init__(self, name, e, sem, inorder=False):
        self.name = name
        self.e = e
        self.sem = sem
        self.cnt = 0
        self.waited = {}
        self.inorder = inorder


def _merge(d, sem, val):
    k = id(sem)
    if k not in d or d[k][1] < val:
        d[k] = (sem, val)


class K:
    def __init__(self, nc, es):
        self.nc = nc
        self.es = es
        self.n_sem = 0
        self.pe = Eng("pe", nc.tensor, self.sem("pe"), inorder=True)
        self.act = Eng("act", nc.scalar, self.sem("act"))
        self.dve = Eng("dve", nc.vector, self.sem("dve"))
        self.pool = Eng("pool", nc.gpsimd, self.sem("pool"))
        self.sp = Eng("sp", nc.sync, self.sem("sp"), inorder=True)
        self.engs = [self.pe, self.act, self.dve, self.pool, self.sp]
        self.dbufs = []
        self.sem_pool = {"hw": [], "sw": []}
        self.n_inst = 0

    def sem(self, name):
        self.n_sem += 1
        return self.es.enter_context(self.nc.semaphore(name))

    def _wait(self, eng, reads, writes):
        deps = {}
        for b in reads:
            for s, v in b.w.values():
                _merge(deps, s, v)
        for b in writes:
            for s, v in b.w.values():
                _merge(deps, s, v)
            for s, v in b.r.values():
                _merge(deps, s, v)
        for s, v in deps.values():
            if s is eng.sem and eng.inorder:
                continue
            if eng.waited.get(id(s), 0) < v:
                eng.e.wait_ge(s, v)
                eng.waited[id(s)] = v
                self.n_inst += 1
                eng.ni = getattr(eng, 'ni', 0) + 1

    def op(self, eng, fns, reads=(), writes=()):
        self._wait(eng, reads, writes)
        if not isinstance(fns, (list, tuple)):
            fns = [fns]
        inst = None
        for f in fns:
            inst = f()
            self.n_inst += 1
            eng.ni = getattr(eng, 'ni', 0) + 1
        inst.then_inc(eng.sem, 1)
        eng.cnt += 1
        assert eng.cnt < 60000, eng.name
        for b in reads:
            _merge(b.r, eng.sem, eng.cnt)
        for b in writes:
            b.w = {id(eng.sem): (eng.sem, eng.cnt)}
            b.r = {}

    def dma(self, q, pairs, reads=(), writes=(), owner=None):
        kind = "sw" if q is self.pool else "hw"
        ent = owner.dsem.get(kind) if isinstance(owner.dsem, dict) else None
        if ent is None:
            if not isinstance(owner.dsem, dict):
                owner.dsem = {}
            if self.sem_pool[kind]:
                ent = list(self.sem_pool[kind].pop())
            else:
                ent = [self.sem(f"d{self.n_sem}_{kind}_" + owner.name), 0]
            owner.dsem[kind] = ent
            self.dbufs.append((owner, kind))
        self._wait(q, reads, writes)
        for out, in_ in pairs:
            q.e.dma_start(out=out, in_=in_).then_inc(ent[0], 16)
            ent[1] += 16
            self.n_inst += 1
            q.ni = getattr(q, 'ni', 0) + 1
            q.nd = getattr(q, 'nd', 0) + 1
        assert ent[1] < 60000, owner.name
        for b in reads:
            _merge(b.r, ent[0], ent[1])
        for b in writes:
            b.w = {id(ent[0]): (ent[0], ent[1])}
            b.r = {}

    def barrier(self):
        for e in self.engs:
            for o in self.engs:
                if o is e or o.cnt == 0:
                    continue
                if e.waited.get(id(o.sem), 0) < o.cnt:
                    e.e.wait_ge(o.sem, o.cnt)
                    e.waited[id(o.sem)] = o.cnt
            for b, kind in self.dbufs:
                s, c = b.dsem[kind]
                if c and e.waited.get(id(s), 0) < c:
                    e.e.wait_ge(s, c)
                    e.waited[id(s)] = c

    def end_phase(self, keep=()):
        self.barrier()
        kept = []
        for b, kind in self.dbufs:
            if b in keep:
                kept.append((b, kind))
            else:
                self.sem_pool[kind].append(tuple(b.dsem[kind]))
                del b.dsem[kind]
        self.dbufs = kept

    def sb(self, name, shape, dt):
        return self.es.enter_context(self.nc.sbuf_tensor(name, shape, dt))


def build(NTOK, debug=(), phases="0ABMC"):
    NT = NTOK // T
    NCH = NTOK // 128
    nc = bass.Bass("TRN2", target_bir_lowering=False)
    es = ExitStack()
    k = K(nc, es)

    def din(name, shape, dt=F32):
        return nc.dram_tensor(name, shape, dt, kind="ExternalInput").ap()

    def dscr(name, shape, dt):
        kind = "ExternalOutput" if name in debug else "Internal"
        return nc.dram_tensor(name, shape, dt, kind=kind).ap()

    xs = din("xs", [NTOK, D])
    pes = din("pes", [NTOK, 256])
    w1g = din("w1g", [D, DFF]); w1u = din("w1u", [D, DFF]); w1d = din("w1d", [DFF, D])
    w2g = din("w2g", [D, DFF]); w2u = din("w2u", [D, DFF]); w2d = din("w2d", [DFF, D])
    win = din("win", [D, DIN]); wout = din("wout", [D, D])
    wpg = din("wpg", [D, D]); wpp = din("wpp", [256, D])
    gv = din("gv", [128, 4 * DC])
    gsm = din("gsm", [128, 16])
    gbias = din("gbias", [128, 16])
    cst = din("cst", [128, 5 * 128])
    rpbg = din("rpbg", [8, 128, 9, 128])
    cmI = din("cmI", [128, 9, 128]); cmE = din("cmE", [128, 9, 128])
    NE = max(4, (NCH // 16) * 4)
    rb = din("rb", [128, NE, 9, 2])
    keep = din("keep", [128, 2, NCH])
    y = nc.dram_tensor("y", [NTOK, D], F32, kind="ExternalOutput").ap()

    WGU1 = dscr("WGU1", [22, 128, 2, 16, 256], BF16)
    WD1 = dscr("WD1", [16, 128, FC, 128], BF16)
    WGU2 = dscr("WGU2", [22, 128, 2, 16, 256], BF16)
    WD2 = dscr("WD2", [16, 128, FC, 128], BF16)
    WIN = dscr("WIN", [14, 128, 16, 512], BF16)
    WING = dscr("WING", [128, 16, 16], BF16)
    WOUT = dscr("WOUT", [4, 128, 16, 512], BF16)
    WPG = dscr("WPG", [4, 128, 16, 512], BF16)
    WPP = dscr("WPP", [128, 2, D], BF16)
    X1T = dscr("X1T", [128, DC, NTOK], F32)
    QAT = dscr("QAT", [8, 128, NTOK], BF16); KAT = dscr("KAT", [8, 128, NTOK], BF16)
    VA = dscr("VA", [NTOK, 1024], BF16)
    QMT = dscr("QMT", [8, 128, NTOK], BF16); KMT = dscr("KMT", [8, 128, NTOK], BF16)
    KM = dscr("KM", [NTOK, 1024], BF16); VM = dscr("VM", [NTOK, 1024], BF16)
    OGT = dscr("OGT", [8, 128, NTOK], BF16)
    GT = dscr("GT", [NTOK, 16], F32)
    HB = dscr("HB", [8, 128, NTOK], F32)
    YT = dscr("YT", [16, 128, NTOK], BF16)

    cst_sb = k.sb("cst_sb", [128, 5 * 128], F32)
    ident = cst_sb[:, 0:128]
    triF = cst_sb[:, 128:256]
    triB = cst_sb[:, 256:384]
    ones_f = cst_sb[:, 384:512]
    gv_sb = k.sb("gv_sb", [128, 4 * DC], F32)
    gsm_sb = k.sb("gsm_sb", [128, 16], F32)
    ones_bf = k.sb("ones_bf", [128, 128], BF16)
    b_cst = Buf("cst")
    k.dma(k.sp, [(cst_sb[:, :], cst[:, :]), (gv_sb[:, :], gv[:, :]), (gsm_sb[:, :], gsm[:, :])],
          writes=[b_cst], owner=b_cst)
    b_ones = Buf("ones")
    k.op(k.dve, lambda: nc.vector.tensor_copy(out=ones_bf[:, :], in_=cst_sb[:, 384:512]),
         reads=[b_cst], writes=[b_ones])

    psall = es.enter_context(nc.psum_tensor("psall", [128, 4096], F32))
    ps = [psall[:, i * 512:(i + 1) * 512] for i in range(8)]
    pb = [Buf(f"ps{i}") for i in range(8)]

    if "0" in phases:
      with ExitStack() as es0:
        def sb0(name, shape, dt):
            return es0.enter_context(nc.sbuf_tensor(name, shape, dt))
        NST = 3
        st_in = [sb0(f"c_in{i}", [128, DIN], F32) for i in range(NST)]
        st_out = [sb0(f"c_out{i}", [128, DIN], BF16) for i in range(NST)]
        bi = [Buf(f"c_in{i}") for i in range(NST)]
        bo = [Buf(f"c_out{i}") for i in range(NST)]
        state = {"i": 0}
        cast_engs = [k.act, k.dve, k.pool]

        def cast_rows(src_pairs_fn, ncols, dst_pairs_fn):
            i = state["i"] % NST
            e = cast_engs[state["i"] % 3]
            state["i"] += 1
            k.dma(k.sp, src_pairs_fn(st_in[i]), writes=[bi[i]], owner=bi[i])
            if e is k.act:
                fn = lambda: nc.scalar.copy(out=st_out[i][:, 0:ncols], in_=st_in[i][:, 0:ncols])
            elif e is k.dve:
                fn = lambda: nc.vector.tensor_copy(out=st_out[i][:, 0:ncols], in_=st_in[i][:, 0:ncols])
            else:
                fn = lambda: nc.gpsimd.tensor_copy(out=st_out[i][:, 0:ncols], in_=st_in[i][:, 0:ncols])
            k.op(e, fn, reads=[bi[i]], writes=[bo[i]])
            k.dma(k.pool, dst_pairs_fn(st_out[i]), reads=[bo[i]], owner=bo[i])

        def cast_gu(wg, wu, WGU):
            for gu, w in enumerate((wg, wu)):
                for kc in range(16):
                    cast_rows(lambda si, w=w, kc=kc: [(si[:, 0:DFF], w[kc * 128:(kc + 1) * 128, :])], DFF,
                              lambda so, gu=gu, kc=kc: [(
                                  WGU[a * 11:(a + 1) * 11, :, gu, kc, :].rearrange("n p f -> p n f"),
                                  so[:, a * 2816:(a + 1) * 2816].rearrange("p (n f) -> p n f", f=256)) for a in range(2)])

        def cast_d(wd, WD):
            for fc2 in range(FC // 2):
                def dst(so, fc2=fc2):
                    pairs = []
                    for a in range(2):
                        fc = fc2 * 2 + a
                        pairs.append((WD[:, :, fc, :].rearrange("c p d -> p c d"),
                                      so[:, a * 2048:(a + 1) * 2048].rearrange("p (c d) -> p c d", d=128)))
                    return pairs
                cast_rows(lambda si, fc2=fc2: [(si[:, 0:4096].rearrange("p (a d) -> p a d", a=2),
                                                wd[fc2 * 256:(fc2 + 1) * 256, :].rearrange("(a p) d -> p a d", p=128))],
                          4096, dst)

        def cast_panels512(w, WS, ncols_total):
            for kc in range(16):
                cast_rows(lambda si, kc=kc: [(si[:, 0:ncols_total], w[kc * 128:(kc + 1) * 128, 0:ncols_total])],
                          ncols_total,
                          lambda so, kc=kc: [(
                              WS[:, :, kc, :].rearrange("n p f -> p n f"),
                              so[:, 0:ncols_total].rearrange("p (n f) -> p n f", f=512))])

        cast_gu(w1g, w1u, WGU1)
        cast_d(w1d, WD1)
        cast_panels512(win, WIN, 7168)
        for kc in range(16):
            cast_rows(lambda si, kc=kc: [(si[:, 0:16], win[kc * 128:(kc + 1) * 128, 7168:7184])], 16,
                      lambda so, kc=kc: [(WING[:, kc, :], so[:, 0:16])])
        k.end_phase(keep=(b_cst,))

    class TB:
        pass

    def alloc_tiles(stk, pfx):
        tb = TB()
        sbt = lambda name, shape, dt: stk.enter_context(nc.sbuf_tensor(pfx + name, shape, dt))
        tb.xT = sbt("xT", [128, DC, T], F32)
        tb.xb = [Buf(f"xT{dc}") for dc in range(DC)]
        tb.hT = sbt("hT", [128, DC, T], BF16)
        tb.hb = Buf("hT")
        tb.act = sbt("act", [128, FC, T], BF16)
        tb.actb = [Buf(f"act{fc}") for fc in range(FC)]
        tb.sq = [sbt(f"sq{j}", [128, T], BF16) for j in range(2)]
        tb.sqb = [Buf(f"sq{j}") for j in range(2)]
        tb.sg = [sbt(f"sg{j}", [128, T], F32) for j in range(2)]
        tb.sgb = [Buf(f"sg{j}") for j in range(2)]
        tb.rstd = sbt("rstd", [128, T], F32)
        tb.rstdb = Buf("rstd")
        tb.xin = [sbt(f"xin{i}", [128, D], F32) for i in range(2)]
        tb.xinb = [Buf(f"xin{i}") for i in range(2)]
        tb.wslot = [sbt(f"wslot{i}", [128, 8192], BF16) for i in range(NSLOT)]
        tb.wsb = [Buf(f"wslot{i}") for i in range(NSLOT)]
        tb.wi = 0
        tb.stg = [sbt(f"stg{i}", [128, T], BF16) for i in range(4)]
        tb.stgb = [Buf(f"stg{i}") for i in range(4)]
        tb.stgf = [sbt(f"stgf{i}", [128, T], F32) for i in range(2)]
        tb.stgfb = [Buf(f"stgf{i}") for i in range(2)]
        tb.si = 0
        tb.fi = 0
        return tb

    NSLOT = 4

    def wload(tb, pairs_fn):
        i = tb.wi % NSLOT
        tb.wi += 1
        k.dma(k.sp, pairs_fn(tb.wslot[i]), writes=[tb.wsb[i]], owner=tb.wsb[i])
        return tb.wslot[i], tb.wsb[i]

    def rmsnorm_T(tb, gcol):
        xT, xb, hT, sq, sqb, rstd, rstdb = tb.xT, tb.xb, tb.hT, tb.sq, tb.sqb, tb.rstd, tb.rstdb
        act, actb = tb.act, tb.actb
        H = DC // 2
        k.op(k.act, lambda: nc.scalar.activation(out=act[:, 0:H, :], in_=xT[:, 0:H, :], func=AF.Square),
             reads=list(xb[0:H]), writes=list(actb[0:H]))
        k.op(k.dve, lambda: nc.vector.tensor_tensor(out=act[:, H:DC, :], in0=xT[:, H:DC, :], in1=xT[:, H:DC, :], op=OP.mult),
             reads=list(xb[H:DC]), writes=list(actb[H:DC]))
        k.op(k.pe, [lambda dc=dc: nc.tensor.matmul(out=ps[0], lhsT=ones_bf[:, :], rhs=act[:, dc, :],
                                                   start=(dc == 0), stop=(dc == DC - 1)) for dc in range(DC)],
             reads=list(actb[0:DC]) + [b_ones], writes=[pb[0]])
        k.op(k.act, lambda: nc.scalar.activation(out=rstd[:, :], in_=ps[0], func=AF.Ln,
                                                 scale=1.0 / D, bias=eps_sb[:, 0:1]),
             reads=[pb[0], b_eps], writes=[rstdb])
        k.op(k.act, lambda: nc.scalar.activation(out=rstd[:, :], in_=rstd[:, :], func=AF.Exp, scale=-0.5),
             reads=[rstdb], writes=[rstdb])
        fns = []
        for dc in range(DC):
            fns.append(lambda dc=dc: nc.vector.scalar_tensor_tensor(
                out=hT[:, dc, :], in0=xT[:, dc, :], scalar=gv_sb[:, gcol + dc:gcol + dc + 1], in1=rstd[:, :],
                op0=OP.mult, op1=OP.mult))
        k.op(k.dve, fns, reads=list(xb) + [rstdb, b_cst], writes=[tb.hb])

    eps_sb = k.sb("eps_sb", [128, 4], F32)
    b_eps = Buf("eps")
    k.op(k.dve, [lambda: nc.vector.memset(eps_sb[:, 0:1], EPS), lambda: nc.vector.memset(eps_sb[:, 1:2], 128 * EPS),
                 lambda: nc.vector.memset(eps_sb[:, 2:3], 1.0)], writes=[b_eps])


    def bg_pieces():
        for kc in range(16):
            yield (lambda si, kc=kc: [(si[:, 0:2048], wout[kc * 128:(kc + 1) * 128, :])], 2048,
                   lambda so, kc=kc: [(WOUT[:, :, kc, :].rearrange("n p f -> p n f"), so[:, 0:2048].rearrange("p (n f) -> p n f", f=512))])
        for gu, w in enumerate((w2g, w2u)):
            for kc in range(16):
                for (n0, n1) in ((0, 8), (8, 16), (16, 22)):
                    nc_ = (n1 - n0) * 256
                    yield (lambda si, w=w, kc=kc, n0=n0, nc_=nc_: [(si[:, 0:nc_], w[kc * 128:(kc + 1) * 128, n0 * 256:n0 * 256 + nc_])], nc_,
                           lambda so, gu=gu, kc=kc, n0=n0, n1=n1, nc_=nc_: [(
                               WGU2[n0:n1, :, gu, kc, :].rearrange("n p f -> p n f"), so[:, 0:nc_].rearrange("p (n f) -> p n f", f=256))])
        for fc in range(FC):
            yield (lambda si, fc=fc: [(si[:, 0:2048], w2d[fc * 128:(fc + 1) * 128, :])], 2048,
                   lambda so, fc=fc: [(WD2[:, :, fc, :].rearrange("c p d -> p c d"), so[:, 0:2048].rearrange("p (c d) -> p c d", d=128))])
        for kc in range(16):
            yield (lambda si, kc=kc: [(si[:, 0:2048], wpg[kc * 128:(kc + 1) * 128, :])], 2048,
                   lambda so, kc=kc: [(WPG[:, :, kc, :].rearrange("n p f -> p n f"), so[:, 0:2048].rearrange("p (n f) -> p n f", f=512))])
        for k2 in range(2):
            yield (lambda si, k2=k2: [(si[:, 0:2048], wpp[k2 * 128:(k2 + 1) * 128, :])], 2048,
                   lambda so, k2=k2: [(WPP[:, k2, :], so[:, 0:2048])])

    class BG:
        pass
    bgs = BG()
    bgs.gen = None

    def bg_init(stk):
        bgs.cin = stk.enter_context(nc.sbuf_tensor("bg_in", [128, 2048], F32))
        bgs.cout = stk.enter_context(nc.sbuf_tensor("bg_out", [128, 2048], BF16))
        bgs.bi = Buf("bg_in"); bgs.bo = Buf("bg_out")
        bgs.gen = bg_pieces()
        bgs.pending = None
        bgs.n = 0

    def bg_step():
        if bgs.gen is None:
            return
        if bgs.pending is not None:
            ncols, dstf = bgs.pending
            e = (k.act, k.dve, k.pool)[bgs.n % 3]
            bgs.n += 1
            if e is k.act:
                fn = lambda: nc.scalar.copy(out=bgs.cout[:, 0:ncols], in_=bgs.cin[:, 0:ncols])
            elif e is k.dve:
                fn = lambda: nc.vector.tensor_copy(out=bgs.cout[:, 0:ncols], in_=bgs.cin[:, 0:ncols])
            else:
                fn = lambda: nc.gpsimd.tensor_copy(out=bgs.cout[:, 0:ncols], in_=bgs.cin[:, 0:ncols])
            k.op(e, fn, reads=[bgs.bi], writes=[bgs.bo])
            k.dma(k.pool, dstf(bgs.cout), reads=[bgs.bo], owner=bgs.bo)
            bgs.pending = None
        nxt = next(bgs.gen, None)
        if nxt is None:
            bgs.gen = None
            return
        srcf, ncols, dstf = nxt
        k.dma(k.sp, srcf(bgs.cin), writes=[bgs.bi], owner=bgs.bi)
        bgs.pending = (ncols, dstf)

    def ffn(tb, WGU, WD):
        hT, hb, act, actb, xT, xb, sg, sgb = tb.hT, tb.hb, tb.act, tb.actb, tb.xT, tb.xb, tb.sg, tb.sgb
        def load_gu(n):
            return wload(tb, lambda s, n=n: [(s[:, :], WGU[n].rearrange("p a c f -> p (a c f)"))])
        nxt = load_gu(0)
        for n in range(22):
            cur = nxt
            if n + 1 < 22:
                nxt = load_gu(n + 1)
            wt, wb = cur
            wv = wt[:, :].rearrange("p (a c f) -> p a c f", a=2, c=16)
            bg_step()
            for f2 in range(2):
                fc = n * 2 + f2
                gbank, ubank = 2 + (fc % 2) * 2, 3 + (fc % 2) * 2
                for gu, bank in ((0, gbank), (1, ubank)):
                    fns = [lambda kc=kc, gu=gu, bank=bank, f2=f2: nc.tensor.matmul(
                        out=ps[bank], lhsT=wv[:, gu, kc, f2 * 128:(f2 + 1) * 128], rhs=hT[:, kc, :],
                        start=(kc == 0), stop=(kc == 15)) for kc in range(16)]
                    k.op(k.pe, fns, reads=[wb, hb], writes=[pb[bank]])
                j = fc % 2
                k.op(k.act, lambda j=j, gbank=gbank: nc.scalar.activation(out=sg[j][:, :], in_=ps[gbank], func=AF.Silu),
                     reads=[pb[gbank]], writes=[sgb[j]])
                k.op(k.dve, lambda j=j, ubank=ubank, fc=fc: nc.vector.tensor_tensor(
                    out=act[:, fc, :], in0=ps[ubank], in1=sg[j][:, :], op=OP.mult),
                     reads=[pb[ubank], sgb[j]], writes=[actb[fc]])
        def load_d(dc):
            return wload(tb, lambda s, dc=dc: [(s[:, 0:FC * 128], WD[dc].rearrange("p c d -> p (c d)"))])
        nxt = load_d(0)
        for dc in range(DC):
            cur = nxt
            if dc + 1 < DC:
                nxt = load_d(dc + 1)
            wt, wb = cur
            bank = 6 + dc % 2
            fns = [lambda fc=fc, bank=bank, wt=wt: nc.tensor.matmul(
                out=ps[bank], lhsT=wt[:, fc * 128:(fc + 1) * 128], rhs=act[:, fc, :],
                start=(fc == 0), stop=(fc == FC - 1)) for fc in range(FC)]
            k.op(k.pe, fns, reads=[wb] + actb, writes=[pb[bank]])
            k.op(k.dve, lambda dc=dc, bank=bank: nc.vector.scalar_tensor_tensor(
                out=xT[:, dc, :], in0=ps[bank], scalar=0.5, in1=xT[:, dc, :], op0=OP.mult, op1=OP.add),
                 reads=[pb[bank]], writes=[xb[dc]])

    def load_xT_from_tokmajor(tb, src, t):
        xin, xinb, xT, xb = tb.xin, tb.xinb, tb.xT, tb.xb
        for s in range(4):
            i = s % 2
            r0 = t * T + s * 128
            k.dma(k.sp, [(xin[i][:, :], src[r0:r0 + 128, :])], writes=[xinb[i]], owner=xinb[i])
            for q in range(4):
                bank = (s * 4 + q) % 8
                fns = [lambda dc=dc, j=j, i=i, bank=bank: nc.tensor.transpose(
                    out=ps[bank][:, j * 128:(j + 1) * 128], in_=xin[i][:, dc * 128:(dc + 1) * 128], identity=ident)
                    for j, dc in enumerate(range(q * 4, q * 4 + 4))]
                k.op(k.pe, fns, reads=[xinb[i], b_cst], writes=[pb[bank]])
                if q % 2 == 0:
                    k.op(k.act, lambda q=q, s=s, bank=bank: nc.scalar.copy(
                        out=xT[:, q * 4:q * 4 + 4, s * 128:(s + 1) * 128],
                        in_=ps[bank].rearrange("p (j t) -> p j t", j=4)),
                         reads=[pb[bank]], writes=[xb[dc] for dc in range(q * 4, q * 4 + 4)])
                else:
                    k.op(k.dve, lambda q=q, s=s, bank=bank: nc.vector.tensor_copy(
                        out=xT[:, q * 4:q * 4 + 4, s * 128:(s + 1) * 128],
                        in_=ps[bank].rearrange("p (j t) -> p j t", j=4)),
                         reads=[pb[bank]], writes=[xb[dc] for dc in range(q * 4, q * 4 + 4)])

    if "A" in phases:
      with ExitStack() as esA:
        tb = alloc_tiles(esA, "A_")
        bg_init(esA)
        hT, hb, sq, sqb, rstd, rstdb = tb.hT, tb.hb, tb.sq, tb.sqb, tb.rstd, tb.rstdb
        stg, stgb, stgf, stgfb = tb.stg, tb.stgb, tb.stgf, tb.stgfb
        for t in range(NT):
            tk = slice(t * T, (t + 1) * T)
            load_xT_from_tokmajor(tb, xs, t)
            rmsnorm_T(tb, 0)
            ffn(tb, WGU1, WD1)
            k.dma(k.pool, [(X1T[:, :, tk], tb.xT[:, :, :])], reads=tb.xb, owner=tb.xb[0])
            rmsnorm_T(tb, DC)
            pend = []
            FM = {0: "qa", 1: "qa", 2: "ka", 3: "ka", 6: "qm", 7: "qm", 8: "km", 9: "km", 12: "og", 13: "og"}
            TM = {4: "va", 5: "va", 8: "km", 9: "km", 10: "vm", 11: "vm"}
            for n in range(14):
                wt, wb = wload(tb, lambda s, n=n: [(s[:, :], WIN[n].rearrange("p c f -> p (c f)"))])
                wv = wt[:, :].rearrange("p (c f) -> p c f", c=16)
                if n in FM:
                    kind = FM[n]
                    for j in range(4):
                        oc = n * 4 + j
                        bank = 6 + oc % 2
                        fns = [lambda kc=kc, j=j, bank=bank: nc.tensor.matmul(
                            out=ps[bank], lhsT=wv[:, kc, j * 128:(j + 1) * 128], rhs=hT[:, kc, :],
                            start=(kc == 0), stop=(kc == 15)) for kc in range(16)]
                        k.op(k.pe, fns, reads=[wb, hb], writes=[pb[bank]])
                        while pend:
                            pend.pop(0)()
                        si = tb.si % 4
                        tb.si += 1
                        if kind in ("qa", "ka"):
                            fi = tb.fi % 2
                            tb.fi += 1
                            sj = fi
                            k.op(k.act, lambda fi=fi, bank=bank: nc.scalar.copy(out=stgf[fi][:, :], in_=ps[bank]),
                                 reads=[pb[bank]], writes=[stgfb[fi]])
                            k.op(k.act, lambda bank=bank, sj=sj: nc.scalar.activation(out=sq[sj][:, :], in_=ps[bank], func=AF.Square),
                                 reads=[pb[bank]], writes=[sqb[sj]])

                            def fin(kind=kind, fi=fi, si=si, sj=sj, oc=oc):
                                k.op(k.pe, lambda: nc.tensor.matmul(out=ps[1], lhsT=ones_bf[:, :], rhs=sq[sj][:, :],
                                                                    start=True, stop=True),
                                     reads=[sqb[sj], b_ones], writes=[pb[1]])
                                if kind == "qa":
                                    k.op(k.act, lambda: nc.scalar.activation(out=rstd[:, :], in_=ps[1], func=AF.Sqrt,
                                                                             scale=1.0, bias=eps_sb[:, 1:2]),
                                         reads=[pb[1], b_eps], writes=[rstdb])
                                else:
                                    k.op(k.act, lambda: nc.scalar.activation(out=rstd[:, :], in_=ps[1], func=AF.Sqrt,
                                                                             scale=1.0 / 128, bias=eps_sb[:, 0:1]),
                                         reads=[pb[1], b_eps], writes=[rstdb])
                                k.op(k.dve, lambda: nc.vector.reciprocal(out=rstd[:, :], in_=rstd[:, :]),
                                     reads=[rstdb], writes=[rstdb])
                                gc = 0 if kind == "qa" else 1
                                k.op(k.dve, lambda: nc.vector.scalar_tensor_tensor(
                                    out=stg[si][:, :], in0=stgf[fi][:, :], scalar=gsm_sb[:, gc:gc + 1], in1=rstd[:, :],
                                    op0=OP.mult, op1=OP.mult),
                                     reads=[stgfb[fi], rstdb, b_cst], writes=[stgb[si]])
                                dst_ = (QAT if kind == "qa" else KAT)[oc % 8, :, tk]
                                k.dma(k.pool, [(dst_, stg[si][:, :])], reads=[stgb[si]], owner=stgb[si])
                            pend.append(fin)
                            continue
                        elif kind == "og":
                            k.op(k.act, lambda si=si, bank=bank: nc.scalar.activation(out=stg[si][:, :], in_=ps[bank], func=AF.Sigmoid),
                                 reads=[pb[bank]], writes=[stgb[si]])
                            dst = OGT[oc % 8, :, tk]
                        elif kind == "km":
                            k.op(k.act, lambda si=si, bank=bank: nc.scalar.mul(out=stg[si][:, :], in_=ps[bank], mul=0.0625),
                                 reads=[pb[bank]], writes=[stgb[si]])
                            dst = KMT[oc % 8, :, tk]
                        else:
                            k.op(k.act, lambda si=si, bank=bank: nc.scalar.copy(out=stg[si][:, :], in_=ps[bank]),
                                 reads=[pb[bank]], writes=[stgb[si]])
                            dst = QMT[oc % 8, :, tk]
                        k.dma(k.pool, [(dst, stg[si][:, :])], reads=[stgb[si]], owner=stgb[si])
                if n in TM:
                    while pend:
                        pend.pop(0)()
                    kind = TM[n]
                    half = n % 2
                    dstT = {"va": VA, "km": KM, "vm": VM}[kind]
                    for s in range(4):
                        bank = 4 + s % 2
                        fns = [lambda kc=kc, s=s, bank=bank: nc.tensor.matmul(
                            out=ps[bank], lhsT=hT[:, kc, s * 128:(s + 1) * 128], rhs=wv[:, kc, :],
                            start=(kc == 0), stop=(kc == 15)) for kc in range(16)]
                        k.op(k.pe, fns, reads=[wb, hb], writes=[pb[bank]])
                        si = tb.si % 4
                        tb.si += 1
                        if kind == "km":
                            k.op(k.dve, lambda si=si, bank=bank: nc.vector.tensor_scalar(
                                out=stg[si][:, :], in0=ps[bank], scalar1=0.0625, scalar2=None, op0=OP.mult),
                                 reads=[pb[bank]], writes=[stgb[si]])
                        else:
                            k.op(k.dve, lambda si=si, bank=bank: nc.vector.tensor_copy(out=stg[si][:, :], in_=ps[bank]),
                                 reads=[pb[bank]], writes=[stgb[si]])
                        r0 = t * T + s * 128
                        k.dma(k.pool, [(dstT[r0:r0 + 128, half * 512:(half + 1) * 512], stg[si][:, :])],
                              reads=[stgb[si]], owner=stgb[si])
            while pend:
                pend.pop(0)()
            wt, wb = wload(tb, lambda s: [(s[:, 0:256], WING.rearrange("p c g -> p (c g)"))])
            wv = wt[:, 0:256].rearrange("p (c g) -> p c g", c=16)
            for s in range(4):
                bank = 4 + s % 2
                fns = [lambda kc=kc, s=s, bank=bank: nc.tensor.matmul(
                    out=ps[bank][:, 0:16], lhsT=hT[:, kc, s * 128:(s + 1) * 128], rhs=wv[:, kc, :],
                    start=(kc == 0), stop=(kc == 15)) for kc in range(16)]
                k.op(k.pe, fns, reads=[wb, hb], writes=[pb[bank]])
                fi = tb.fi % 2
                tb.fi += 1
                k.op(k.dve, lambda fi=fi, bank=bank: nc.vector.tensor_copy(out=stgf[fi][:, 0:16], in_=ps[bank][:, 0:16]),
                     reads=[pb[bank]], writes=[stgfb[fi]])
                r0 = t * T + s * 128
                k.dma(k.pool, [(GT[r0:r0 + 128, :], stgf[fi][:, 0:16])], reads=[stgfb[fi]], owner=stgfb[fi])
        while bgs.gen is not None or bgs.pending is not None:
            bg_step()
        bgs.gen = None
        k.end_phase(keep=(b_cst,))


    if "B" in phases:
      with ExitStack() as esB:
        sbt = lambda name, shape, dt: esB.enter_context(nc.sbuf_tensor(name, shape, dt))
        QT = [sbt(f"naQ{i}", [128, NTOK], BF16) for i in range(2)]
        KT = [sbt(f"naK{i}", [128, NTOK], BF16) for i in range(2)]
        VV = [sbt(f"naV{i}", [128, NCH, 128], BF16) for i in range(2)]
        YA = [sbt(f"naY{i}", [128, NTOK], BF16) for i in range(2)]
        RPB = [sbt(f"naR{i}", [128, 9, 128], F32) for i in range(2)]
        BMI = [sbt(f"naBI{i}", [128, 9, 128], F32) for i in range(2)]
        BME = [sbt(f"naBE{i}", [128, 9, 128], F32) for i in range(2)]
        cmI_sb = sbt("cmI_sb", [128, 9, 128], F32)
        cmE_sb = sbt("cmE_sb", [128, 9, 128], F32)
        rb_sb = sbt("rb_sb", [128, NE, 9, 2], F32)
        S1 = [sbt(f"naS{i}", [128, 9, 128], F32) for i in range(2)]
        PT = [sbt(f"naP{i}", [128, 9, 128], BF16) for i in range(2)]
        REC = [sbt(f"naRec{i}", [128, 128], F32) for i in range(2)]
        b_in = [Buf(f"naIn{i}") for i in range(2)]
        b_ya = [Buf(f"naY{i}") for i in range(2)]
        b_rp = [Buf(f"naR{i}") for i in range(2)]
        b_bm = [Buf(f"naBM{i}") for i in range(2)]
        b_cm = Buf("naCM")
        b_s1 = [Buf(f"naS{i}") for i in range(2)]
        b_pt = [Buf(f"naP{i}") for i in range(2)]
        b_rec = [Buf(f"naRec{i}") for i in range(2)]
        k.dma(k.sp, [(cmI_sb[:, :, :], cmI[:, :, :]), (cmE_sb[:, :, :], cmE[:, :, :]), (rb_sb[:, :, :, :], rb[:, :, :, :])],
              writes=[b_cm], owner=b_cm)
        epos = {0: 0, 1: 1, 14: 2, 15: 3}

        def na_load(h):
            i = h % 2
            prs = [(QT[i][:, :], QAT[h]), (KT[i][:, :], KAT[h])]
            for n0 in range(0, NCH, 16):
                prs.append((VV[i][:, n0:n0 + 16, :],
                            VA[n0 * 128:(n0 + 16) * 128, h * 128:(h + 1) * 128].rearrange("(n p) d -> p n d", p=128)))
            k.dma(k.sp, prs, writes=[b_in[i]], owner=b_in[i])
            k.dma(k.sp, [(RPB[i][:, :, :], rpbg[h])], writes=[b_rp[i]], owner=b_rp[i])

        na_load(0)
        for h in range(8):
            i = h % 2
            if h + 1 < 8:
                na_load(h + 1)
            k.op(k.pool, [lambda i=i: nc.gpsimd.tensor_tensor(out=BMI[i][:, :, :], in0=RPB[i][:, :, :], in1=cmI_sb[:, :, :], op=OP.add),
                          lambda i=i: nc.gpsimd.tensor_tensor(out=BME[i][:, :, :], in0=RPB[i][:, :, :], in1=cmE_sb[:, :, :], op=OP.add)],
                 reads=[b_rp[i], b_cm], writes=[b_bm[i]])
            def rng_(c):
                return max(0, 4 - c), min(8, NCH - 1 - c + 4)

            def na_S(c, i=i):
                j = c % 2
                o_lo, o_hi = rng_(c)
                sbanks = [pb[3 * j], pb[3 * j + 1], pb[3 * j + 2]]
                Sps = psall[:, j * 1536:j * 1536 + 1152].rearrange("p (o q) -> p o q", o=9)
                fns = [lambda o=o: nc.tensor.matmul(
                    out=Sps[:, o, :], lhsT=KT[i][:, (c + o - 4) * 128:(c + o - 3) * 128], rhs=QT[i][:, c * 128:(c + 1) * 128],
                    start=True, stop=True) for o in range(o_lo, o_hi + 1)]
                k.op(k.pe, fns, reads=[b_in[i]], writes=sbanks)

            def na_soft(c, i=i):
                j = c % 2
                o_lo, o_hi = rng_(c)
                no = o_hi - o_lo + 1
                sbanks = [pb[3 * j], pb[3 * j + 1], pb[3 * j + 2]]
                Sps = psall[:, j * 1536:j * 1536 + 1152].rearrange("p (o q) -> p o q", o=9)
                edge = (c % 16) in epos
                BM = BME[i] if edge else BMI[i]
                k.op(k.dve, lambda: nc.vector.tensor_tensor(
                    out=S1[j][:, o_lo:o_hi + 1, :], in0=Sps[:, o_lo:o_hi + 1, :], in1=BM[:, o_lo:o_hi + 1, :], op=OP.add),
                     reads=sbanks + [b_bm[i]], writes=[b_s1[j]])
                if edge:
                    e = (c // 16) * 4 + epos[c % 16]
                    k.op(k.pool, lambda: nc.gpsimd.tensor_tensor(
                        out=S1[j][:, o_lo:o_hi + 1, :].rearrange("p o (b q) -> p o b q", b=2),
                        in0=S1[j][:, o_lo:o_hi + 1, :].rearrange("p o (b q) -> p o b q", b=2),
                        in1=rb_sb[:, e, o_lo:o_hi + 1, :].unsqueeze(3).to_broadcast([128, no, 2, 64]), op=OP.add),
                         reads=[b_cm], writes=[b_s1[j]])
                k.op(k.act, lambda: nc.scalar.activation(
                    out=PT[j][:, o_lo:o_hi + 1, :], in_=S1[j][:, o_lo:o_hi + 1, :], func=AF.Exp),
                     reads=[b_s1[j]], writes=[b_pt[j]])

            def na_PV(c, i=i):
                j = c % 2
                o_lo, o_hi = rng_(c)
                ob = 6 + j
                fns = []
                for o in range(o_lo, o_hi + 1):
                    fns.append(lambda o=o: nc.tensor.matmul(
                        out=ps[ob][:, 0:128], lhsT=VV[i][:, c + o - 4, :], rhs=PT[j][:, o, :],
                        start=(o == o_lo), stop=(o == o_hi)))
                for o in range(o_lo, o_hi + 1):
                    fns.append(lambda o=o: nc.tensor.matmul(
                        out=ps[ob][:, 128:256], lhsT=ones_bf[:, :], rhs=PT[j][:, o, :],
                        start=(o == o_lo), stop=(o == o_hi)))
                k.op(k.pe, fns, reads=[b_in[i], b_pt[j], b_ones], writes=[pb[ob]])

            def na_fin(c, i=i):
                j = c % 2
                ob = 6 + j
                k.op(k.dve, lambda: nc.vector.reciprocal(out=REC[j][:, :], in_=ps[ob][:, 128:256]),
                     reads=[pb[ob]], writes=[b_rec[j]])
                k.op(k.dve, lambda: nc.vector.tensor_tensor(
                    out=YA[i][:, c * 128:(c + 1) * 128], in0=ps[ob][:, 0:128], in1=REC[j][:, :], op=OP.mult),
                     reads=[pb[ob], b_rec[j]], writes=[b_ya[i]])

            na_S(0)
            for c in range(NCH):
                if c + 1 < NCH:
                    na_S(c + 1)
                na_soft(c)
                if c >= 1:
                    na_fin(c - 1)
                na_PV(c)
            na_fin(NCH - 1)
            k.dma(k.sp, [(YT[h], YA[i][:, :])], reads=[b_ya[i]], owner=b_ya[i])
        k.end_phase(keep=(b_cst,))


    if "M" in phases:
      with ExitStack() as esM:
        sbt = lambda name, shape, dt: esM.enter_context(nc.sbuf_tensor(name, shape, dt))
        NG = NCH * 8
        G_sb = sbt("G_sb", [128, NCH, 16], F32)
        gbias_sb = sbt("gbias_sb", [128, 16], F32)
        keep_sb = sbt("keep_sb", [128, 2, NCH], F32)
        LF = sbt("LF", [128, NCH, 8], F32)
        BS = sbt("BS", [128, NCH, 8], F32)
        IB = sbt("IB", [128, NCH, 8], F32)
        KS = sbt("KS", [128, NCH, 8], F32)
        DEC = sbt("DEC", [128, NCH, 8], F32)
        b_g = Buf("mG"); b_lf = Buf("mLF"); b_bs = Buf("mBS"); b_ib = Buf("mIB"); b_ks = Buf("mKS"); b_dec = Buf("mDEC")
        b_gb = Buf("mGB")
        k.dma(k.sp, [(G_sb[:, n0:n0 + 16, :], GT[n0 * 128:(n0 + 16) * 128, :].rearrange("(n p) g -> p n g", p=128))
                     for n0 in range(0, NCH, 16)], writes=[b_g], owner=b_g)
        k.dma(k.sp, [(gbias_sb[:, :], gbias[:, :]), (keep_sb[:, :, :], keep[:, :, :])], writes=[b_gb], owner=b_gb)
        k.op(k.dve, lambda: nc.vector.tensor_tensor(out=G_sb[:, :, :], in0=G_sb[:, :, :],
                                                    in1=gbias_sb[:, :].unsqueeze(1).to_broadcast([128, NCH, 16]), op=OP.add),
             reads=[b_gb], writes=[b_g])
        k.op(k.act, lambda: nc.scalar.activation(out=LF[:, :, :], in_=G_sb[:, :, 8:16], func=AF.Exp, scale=-1.0),
             reads=[b_g], writes=[b_lf])
        k.op(k.act, lambda: nc.scalar.activation(out=LF[:, :, :], in_=LF[:, :, :], func=AF.Ln, bias=eps_sb[:, 2:3]),
             reads=[b_lf, b_eps], writes=[b_lf])
        k.op(k.dve, lambda: nc.vector.tensor_scalar(out=LF[:, :, :], in0=LF[:, :, :], scalar1=-1.0, scalar2=None, op0=OP.mult),
             reads=[b_lf], writes=[b_lf])
        LF2 = LF[:, :, :].rearrange("p n g -> p (n g)")
        k.op(k.pe, lambda: nc.tensor.matmul(out=ps[0][:, 0:NG], lhsT=triF, rhs=LF2, start=True, stop=True),
             reads=[b_lf, b_cst], writes=[pb[0]])
        k.op(k.pe, lambda: nc.tensor.matmul(out=ps[1][:, 0:NG], lhsT=triB, rhs=LF2, start=True, stop=True),
             reads=[b_lf, b_cst], writes=[pb[1]])
        k.op(k.pe, lambda: nc.tensor.matmul(out=ps[2][:, 0:NG], lhsT=ones_f, rhs=LF2, start=True, stop=True),
             reads=[b_lf, b_cst], writes=[pb[2]])
        k.op(k.dve, [lambda: nc.vector.tensor_copy(out=BS[:, :, 0:4], in_=ps[0][:, 0:NG].rearrange("p (n g) -> p n g", g=8)[:, :, 0:4]),
                     lambda: nc.vector.tensor_copy(out=BS[:, :, 4:8], in_=ps[1][:, 0:NG].rearrange("p (n g) -> p n g", g=8)[:, :, 4:8])],
             reads=[pb[0], pb[1]], writes=[b_bs])
        k.op(k.dve, lambda: nc.vector.tensor_tensor(out=IB[:, :, :], in0=G_sb[:, :, 0:8], in1=BS[:, :, :], op=OP.subtract),
             reads=[b_g, b_bs], writes=[b_ib])
        k.op(k.act, lambda: nc.scalar.activation(out=KS[:, :, :], in_=IB[:, :, :], func=AF.Exp), reads=[b_ib], writes=[b_ks])
        k.op(k.act, lambda: nc.scalar.activation(out=DEC[:, :, :].rearrange("p n g -> p (n g)"), in_=ps[2][:, 0:NG], func=AF.Exp),
             reads=[pb[2]], writes=[b_dec])
        k.op(k.dve, [lambda d=d: nc.vector.tensor_tensor(
            out=DEC[:, :, d * 4:(d + 1) * 4], in0=DEC[:, :, d * 4:(d + 1) * 4],
            in1=keep_sb[:, d, :].unsqueeze(2).to_broadcast([128, NCH, 4]), op=OP.mult) for d in range(2)],
             reads=[b_gb], writes=[b_dec])

        QT2 = sbt("mQT", [128, 2, NTOK], BF16)
        KT2 = sbt("mKT", [128, 2, NTOK], BF16)
        Ktok = sbt("mK", [128, NCH, 256], BF16)
        Vaug = sbt("mV", [128, NCH, 264], BF16)
        b_hd = Buf("mHead")
        b_v1 = Buf("mVones")
        k.op(k.pool, lambda: nc.gpsimd.memset(Vaug[:, :, 256:264], 1.0), writes=[b_v1])

        class DS:
            pass
        dd = []
        for d in range(2):
            s_ = DS()
            nm = lambda x, d=d: f"m{x}{d}"
            def two(name, shape, dt, s_=s_, nm=nm):
                setattr(s_, name, [sbt(nm(name) + f"_{q}", shape, dt) for q in range(2)])
                setattr(s_, "b_" + name, [Buf(nm(name) + f"_{q}") for q in range(2)])
            def one(name, shape, dt, s_=s_, nm=nm):
                setattr(s_, name, sbt(nm(name), shape, dt))
                setattr(s_, "b_" + name, Buf(nm(name)))
            two("diag", [128, 128], F32); two("DT", [128, 128], F32); two("EB", [128, 128], F32)
            two("DTm", [128, 128], F32); two("SD", [128, 128], BF16); two("Qp", [128, 2, 128], BF16)
            two("Kp", [128, 256], BF16); two("dcl", [128, 128], F32); two("hbuf", [128, 2, 128], F32)
            two("hl", [128, 2, 128], F32); two("og", [128, 2, 128], BF16); two("sqh", [128, 2, 128], BF16)
            one("rs", [128, 128], F32); one("tmp", [128, 2, 128], F32); one("ym", [128, 2, 128], BF16)
            one("U32", [128, 2, 256], F32); two("Ubf", [128, 2, 256], BF16)
            one("n32", [128, 2], F32); two("nB", [128, 2, 128], BF16)
            s_.p_a = pb[4 * d]; s_.p_nd = pb[4 * d + 1]; s_.p_du = [pb[4 * d + 2], pb[4 * d + 3]]
            dd.append(s_)
        hbb = {}

        def geom(i, d):
            n = i if d == 0 else NCH - 1 - i
            nprev = n - 1 if d == 0 else n + 1
            first = (n < NCH // 2) if d == 0 else (n >= NCH // 2)
            return n, nprev, first

        def m_prep(h, i, d):
            s_ = dd[d]; q = i % 2
            n, nprev, first = geom(i, d)
            g = d * 4 + h
            ck = slice(n * 128, (n + 1) * 128)
            A = ps[4 * d]
            Bt = A[:, 0:128]
            Sp = A[:, 128:256]
            mask = triF if d == 0 else triB
            k.op(k.pool, lambda: nc.gpsimd.tensor_scalar(
                out=s_.diag[q][:, :], in0=ident, scalar1=BS[:, n, g:g + 1], scalar2=1.0, op0=OP.mult, op1=OP.mult),
                 reads=[b_bs, b_cst], writes=[s_.b_diag[q]])
            yield
            k.op(k.pe, lambda: nc.tensor.matmul(out=Bt, lhsT=ones_f, rhs=s_.diag[q][:, :], start=True, stop=True),
                 reads=[s_.b_diag[q], b_cst], writes=[s_.p_a])
            yield
            k.op(k.pe, [lambda dkc=dkc: nc.tensor.matmul(
                out=Sp, lhsT=KT2[:, dkc, ck], rhs=QT2[:, dkc, ck], start=(dkc == 0), stop=(dkc == 1))
                for dkc in range(2)], reads=[b_hd], writes=[s_.p_a])
            yield
            k.op(k.act, lambda: nc.scalar.activation(out=s_.DT[q][:, :], in_=Bt, func=AF.Exp, bias=IB[:, n, g:g + 1]),
                 reads=[s_.p_a, b_ib], writes=[s_.b_DT[q]])
            yield
            if i > 0:
                k.op(k.act, lambda: nc.scalar.activation(out=s_.EB[q][:, :], in_=Bt, func=AF.Exp),
                     reads=[s_.p_a], writes=[s_.b_EB[q]])
                yield
            k.op(k.pool, lambda: nc.gpsimd.tensor_tensor(out=s_.DTm[q][:, :], in0=s_.DT[q][:, :], in1=mask, op=OP.mult),
                 reads=[s_.b_DT[q], b_cst], writes=[s_.b_DTm[q]])
            yield
            k.op(k.dve, lambda: nc.vector.tensor_tensor(out=s_.SD[q][:, :], in0=Sp, in1=s_.DTm[q][:, :], op=OP.mult),
                 reads=[s_.p_a, s_.b_DTm[q]], writes=[s_.b_SD[q]])
            yield
            if i > 0:
                k.op(k.dve, lambda: nc.vector.scalar_tensor_tensor(
                    out=s_.Qp[q][:, :, :], in0=QT2[:, :, ck], scalar=DEC[:, nprev, g:g + 1],
                    in1=s_.EB[q][:, :].unsqueeze(1).to_broadcast([128, 2, 128]), op0=OP.mult, op1=OP.mult),
                     reads=[b_hd, b_dec, s_.b_EB[q]], writes=[s_.b_Qp[q]])
                yield
            if i < NCH - 1:
                k.op(k.pool, lambda: nc.gpsimd.tensor_scalar(
                    out=s_.Kp[q][:, :], in0=Ktok[:, n, :], scalar1=KS[:, n, g:g + 1], scalar2=1.0, op0=OP.mult, op1=OP.mult),
                     reads=[b_hd, b_ks], writes=[s_.b_Kp[q]])
                yield
                B2 = ps[4 * d + 2 + q]
                fns = [lambda dkc=dkc: nc.tensor.matmul(
                    out=B2[:, dkc * 256:(dkc + 1) * 256], lhsT=s_.Kp[q][:, dkc * 128:(dkc + 1) * 128], rhs=Vaug[:, n, 0:256],
                    start=True, stop=True) for dkc in range(2)]
                k.op(k.pe, fns, reads=[s_.b_Kp[q], b_hd], writes=[s_.p_du[q]])
                yield

        def m_chain(h, i, d):
            s_ = dd[d]; q = i % 2
            n, nprev, first = geom(i, d)
            g = d * 4 + h
            B1 = ps[4 * d + 1]
            fns = []
            for dvc in range(2):
                fns.append(lambda dvc=dvc: nc.tensor.matmul(
                    out=B1[:, dvc * 128:(dvc + 1) * 128], lhsT=Vaug[:, n, dvc * 128:(dvc + 1) * 128], rhs=s_.SD[q][:, :],
                    start=True, stop=(i == 0)))
                if i > 0:
                    for dkc in range(2):
                        fns.append(lambda dvc=dvc, dkc=dkc: nc.tensor.matmul(
                            out=B1[:, dvc * 128:(dvc + 1) * 128], lhsT=s_.Ubf[1 - q][:, dkc, dvc * 128:(dvc + 1) * 128],
                            rhs=s_.Qp[q][:, dkc, :], start=False, stop=(dkc == 1)))
            fns.append(lambda: nc.tensor.matmul(out=B1[:, 256:384], lhsT=ones_bf[:, :], rhs=s_.SD[q][:, :],
                                                start=True, stop=(i == 0)))
            if i > 0:
                for dkc in range(2):
                    fns.append(lambda dkc=dkc: nc.tensor.matmul(
                        out=B1[:, 256:384], lhsT=s_.nB[1 - q][:, dkc, :], rhs=s_.Qp[q][:, dkc, :], start=False, stop=(dkc == 1)))
            rd = [b_hd, b_v1, s_.b_SD[q], b_ones] + ([s_.b_Ubf[1 - q], s_.b_Qp[q], s_.b_nB[1 - q]] if i > 0 else [])
            k.op(k.pe, fns, reads=rd, writes=[s_.p_nd])
            yield
            if i < NCH - 1:
                B2 = ps[4 * d + 2 + q]
                A = ps[4 * d]
                k.op(k.pe, [lambda dkc=dkc: nc.tensor.matmul(
                    out=A[:, 384 + dkc:385 + dkc], lhsT=s_.Kp[q][:, dkc * 128:(dkc + 1) * 128], rhs=Vaug[:, n, 256:257],
                    start=True, stop=True) for dkc in range(2)], reads=[s_.b_Kp[q], b_v1], writes=[s_.p_a])
                yield
                U2 = s_.U32[:, :, :].rearrange("p c v -> p (c v)")
                if i == 0:
                    k.op(k.dve, lambda: nc.vector.tensor_copy(out=U2, in_=B2[:, :]), reads=[s_.p_du[q]], writes=[s_.b_U32])
                    yield
                    k.op(k.dve, lambda: nc.vector.tensor_copy(out=s_.n32[:, :], in_=A[:, 384:386]), reads=[s_.p_a], writes=[s_.b_n32])
                    yield
                else:
                    k.op(k.dve, lambda: nc.vector.scalar_tensor_tensor(
                        out=U2, in0=U2, scalar=DEC[:, nprev, g:g + 1], in1=B2[:, :], op0=OP.mult, op1=OP.add),
                         reads=[s_.p_du[q], b_dec], writes=[s_.b_U32])
                    yield
                    k.op(k.dve, lambda: nc.vector.scalar_tensor_tensor(
                        out=s_.n32[:, :], in0=s_.n32[:, :], scalar=DEC[:, nprev, g:g + 1], in1=A[:, 384:386], op0=OP.mult, op1=OP.add),
                         reads=[s_.p_a, b_dec], writes=[s_.b_n32])
                    yield
                k.op(k.act, lambda: nc.scalar.copy(out=s_.Ubf[q][:, :, :], in_=s_.U32[:, :, :]), reads=[s_.b_U32], writes=[s_.b_Ubf[q]])
                yield
                k.op(k.act, lambda: nc.scalar.copy(out=s_.nB[q][:, :, :], in_=s_.n32[:, :].unsqueeze(2).to_broadcast([128, 2, 128])),
                     reads=[s_.b_n32], writes=[s_.b_nB[q]])
                yield

        def m_h(h, i, d):
            s_ = dd[d]; q = i % 2
            n, nprev, first = geom(i, d)
            ck = slice(n * 128, (n + 1) * 128)
            B1 = ps[4 * d + 1]
            if not first:
                k.dma(k.sp, [(s_.hl[q][:, :, :], HB[2 * h:2 * h + 2, :, ck].rearrange("c p t -> p c t"))],
                      reads=[hbb[(h, n)]], writes=[s_.b_hl[q]], owner=s_.b_hl[q])
                yield
                k.dma(k.sp, [(s_.og[q][:, :, :], OGT[2 * h:2 * h + 2, :, ck].rearrange("c p t -> p c t"))],
                      writes=[s_.b_og[q]], owner=s_.b_og[q])
                yield
            k.op(k.act, lambda: nc.scalar.activation(out=s_.dcl[q][:, :], in_=B1[:, 256:384], func=AF.Abs),
                 reads=[s_.p_nd], writes=[s_.b_dcl[q]])
            yield
            k.op(k.dve, lambda: nc.vector.tensor_scalar(out=s_.dcl[q][:, :], in0=s_.dcl[q][:, :], scalar1=1.0, scalar2=None, op0=OP.max),
                 reads=[s_.b_dcl[q]], writes=[s_.b_dcl[q]])
            yield
            k.op(k.act, lambda: nc.scalar.activation(out=s_.dcl[q][:, :], in_=s_.dcl[q][:, :], func=AF.Ln),
                 reads=[s_.b_dcl[q]], writes=[s_.b_dcl[q]])
            yield
            k.op(k.act, lambda: nc.scalar.activation(out=s_.dcl[q][:, :], in_=s_.dcl[q][:, :], func=AF.Exp, scale=-1.0),
                 reads=[s_.b_dcl[q]], writes=[s_.b_dcl[q]])
            yield
            k.op(k.dve, lambda: nc.vector.tensor_tensor(
                out=s_.hbuf[q][:, :, :], in0=B1[:, 0:256].rearrange("p (c t) -> p c t", c=2),
                in1=s_.dcl[q][:, :].unsqueeze(1).to_broadcast([128, 2, 128]), op=OP.mult),
                 reads=[s_.p_nd, s_.b_dcl[q]], writes=[s_.b_hbuf[q]])
            yield
            if first:
                hbb[(h, n)] = Buf(f"hb{h}_{n}")
                k.dma(k.sp, [(HB[2 * h:2 * h + 2, :, ck].rearrange("c p t -> p c t"), s_.hbuf[q][:, :, :])],
                      reads=[s_.b_hbuf[q]], writes=[hbb[(h, n)]], owner=s_.b_hbuf[q])
                yield
            else:
                k.op(k.pool, lambda: nc.gpsimd.tensor_tensor(out=s_.hl[q][:, :, :], in0=s_.hl[q][:, :, :], in1=s_.hbuf[q][:, :, :], op=OP.add),
                     reads=[s_.b_hbuf[q]], writes=[s_.b_hl[q]])
                yield
                k.op(k.act, lambda: nc.scalar.activation(out=s_.sqh[q][:, :, :], in_=s_.hl[q][:, :, :], func=AF.Square),
                     reads=[s_.b_hl[q]], writes=[s_.b_sqh[q]])
                yield

        def m_fin2(h, i, d):
            s_ = dd[d]; q = i % 2
            n, nprev, first = geom(i, d)
            if first:
                return
            ck = slice(n * 128, (n + 1) * 128)
            A = ps[4 * d]
            k.op(k.pe, [lambda dvc=dvc: nc.tensor.matmul(
                out=A[:, 256:384], lhsT=ones_bf[:, :], rhs=s_.sqh[q][:, dvc, :], start=(dvc == 0), stop=(dvc == 1))
                for dvc in range(2)], reads=[s_.b_sqh[q], b_ones], writes=[s_.p_a])
            yield
            k.op(k.act, lambda: nc.scalar.activation(out=s_.rs[:, :], in_=A[:, 256:384], func=AF.Ln,
                                                     scale=1.0 / 256, bias=eps_sb[:, 0:1]),
                 reads=[s_.p_a, b_eps], writes=[s_.b_rs])
            yield
            k.op(k.act, lambda: nc.scalar.activation(out=s_.rs[:, :], in_=s_.rs[:, :], func=AF.Exp, scale=-0.5),
                 reads=[s_.b_rs], writes=[s_.b_rs])
            yield
            k.op(k.dve, [lambda dvc=dvc: nc.vector.scalar_tensor_tensor(
                out=s_.tmp[:, dvc, :], in0=s_.hl[q][:, dvc, :], scalar=gsm_sb[:, 2 + 2 * h + dvc:3 + 2 * h + dvc],
                in1=s_.rs[:, :], op0=OP.mult, op1=OP.mult) for dvc in range(2)],
                 reads=[s_.b_hl[q], s_.b_rs, b_cst], writes=[s_.b_tmp])
            yield
            k.op(k.pool, lambda: nc.gpsimd.tensor_tensor(out=s_.ym[:, :, :], in0=s_.tmp[:, :, :], in1=s_.og[q][:, :, :], op=OP.mult),
                 reads=[s_.b_tmp, s_.b_og[q]], writes=[s_.b_ym])
            yield
            k.dma(k.sp, [(YT[8 + 2 * h:10 + 2 * h, :, ck].rearrange("c p t -> p c t"), s_.ym[:, :, :])],
                  reads=[s_.b_ym], owner=s_.b_ym)
            yield

        for h in range(4):
            prs = [(QT2[:, :, :], QMT[2 * h:2 * h + 2].rearrange("c p t -> p c t")),
                   (KT2[:, :, :], KMT[2 * h:2 * h + 2].rearrange("c p t -> p c t"))]
            for n0 in range(0, NCH, 16):
                rows = slice(n0 * 128, (n0 + 16) * 128)
                prs.append((Ktok[:, n0:n0 + 16, :], KM[rows, h * 256:(h + 1) * 256].rearrange("(n p) d -> p n d", p=128)))
                prs.append((Vaug[:, n0:n0 + 16, 0:256], VM[rows, h * 256:(h + 1) * 256].rearrange("(n p) d -> p n d", p=128)))
            k.dma(k.sp, prs, writes=[b_hd], owner=b_hd)
            def rr(*gens):
                gens = list(gens)
                while gens:
                    for g_ in list(gens):
                        try:
                            next(g_)
                        except StopIteration:
                            gens.remove(g_)

            def seq(*gens):
                for g_ in gens:
                    for _ in g_:
                        pass

            seq(m_prep(h, 0, 0), m_prep(h, 0, 1))
            for i in range(NCH):
                if i + 1 < NCH:
                    seq(m_prep(h, i + 1, 0), m_prep(h, i + 1, 1))
                seq(m_chain(h, i, 0), m_chain(h, i, 1))
                if i >= 1:
                    seq(m_fin2(h, i - 1, 0), m_fin2(h, i - 1, 1))
                seq(m_h(h, i, 0), m_h(h, i, 1))
            seq(m_fin2(h, NCH - 1, 0), m_fin2(h, NCH - 1, 1))
        k.end_phase(keep=(b_cst,))


    if "C" in phases:
      with ExitStack() as esC:
        tb = alloc_tiles(esC, "C_")
        sbt = lambda name, shape, dt: esC.enter_context(nc.sbuf_tensor(name, shape, dt))
        hT, hb, xT, xb, sg, sgb = tb.hT, tb.hb, tb.xT, tb.xb, tb.sg, tb.sgb
        wpp_sb = sbt("wpp_sb", [128, 2, D], BF16)
        b_wpp = Buf("wpp")
        k.dma(k.sp, [(wpp_sb[:, :, :], WPP[:, :, :])], writes=[b_wpp], owner=b_wpp)
        pin = [sbt(f"pin{i}", [128, 256], F32) for i in range(2)]
        pinb = [Buf(f"pin{i}") for i in range(2)]
        peT = sbt("peT", [128, 2, T], BF16)
        b_peT = Buf("peT")
        sg2 = [sbt(f"sg2_{j}", [128, T], F32) for j in range(2)]
        sg2b = [Buf(f"sg2_{j}") for j in range(2)]
        for t in range(NT):
            tk = slice(t * T, (t + 1) * T)
            if t == 0:
                k.dma(k.sp, [(hT[:, :, :], YT[:, :, tk].rearrange("c p t -> p c t"))], writes=[hb], owner=hb)
            for g4 in range(4):
                k.dma(k.sp, [(xT[:, 4 * g4:4 * g4 + 4, :], X1T[:, 4 * g4:4 * g4 + 4, tk])],
                      writes=xb[4 * g4:4 * g4 + 4], owner=xb[4 * g4])
            for n in range(4):
                wt, wb = wload(tb, lambda s, n=n: [(s[:, :], WOUT[n].rearrange("p c f -> p (c f)"))])
                wv = wt[:, :].rearrange("p (c f) -> p c f", c=16)
                for j in range(4):
                    dc = n * 4 + j
                    bank = 6 + dc % 2
                    fns = [lambda kc=kc, j=j, bank=bank, wv=wv: nc.tensor.matmul(
                        out=ps[bank], lhsT=wv[:, kc, j * 128:(j + 1) * 128], rhs=hT[:, kc, :],
                        start=(kc == 0), stop=(kc == 15)) for kc in range(16)]
                    k.op(k.pe, fns, reads=[wb, hb], writes=[pb[bank]])
                    k.op(k.dve, lambda dc=dc, bank=bank: nc.vector.tensor_tensor(
                        out=xT[:, dc, :], in0=ps[bank], in1=xT[:, dc, :], op=OP.add),
                         reads=[pb[bank]], writes=[xb[dc]])
            rmsnorm_T(tb, 2 * DC)
            ffn(tb, WGU2, WD2)
            rmsnorm_T(tb, 3 * DC)
            for s in range(4):
                i = s % 2
                r0 = t * T + s * 128
                k.dma(k.sp, [(pin[i][:, :], pes[r0:r0 + 128, :])], writes=[pinb[i]], owner=pinb[i])
                k.op(k.pe, [lambda k2=k2, i=i: nc.tensor.transpose(
                    out=ps[i][:, k2 * 128:(k2 + 1) * 128], in_=pin[i][:, k2 * 128:(k2 + 1) * 128], identity=ident)
                    for k2 in range(2)], reads=[pinb[i], b_cst], writes=[pb[i]])
                k.op(k.act, lambda i=i, s=s: nc.scalar.copy(
                    out=peT[:, :, s * 128:(s + 1) * 128], in_=ps[i][:, 0:256].rearrange("p (c t) -> p c t", c=2)),
                     reads=[pb[i]], writes=[b_peT])
            for n in range(4):
                wt, wb = wload(tb, lambda s, n=n: [(s[:, :], WPG[n].rearrange("p c f -> p (c f)"))])
                wv = wt[:, :].rearrange("p (c f) -> p c f", c=16)
                for j in range(4):
                    dc = n * 4 + j
                    bank = 6 + dc % 2
                    pbank = 4 + dc % 2
                    jj = dc % 2
                    fns = [lambda kc=kc, j=j, bank=bank, wv=wv: nc.tensor.matmul(
                        out=ps[bank], lhsT=wv[:, kc, j * 128:(j + 1) * 128], rhs=hT[:, kc, :],
                        start=(kc == 0), stop=(kc == 15)) for kc in range(16)]
                    k.op(k.pe, fns, reads=[wb, hb], writes=[pb[bank]])
                    k.op(k.pe, [lambda k2=k2, dc=dc, pbank=pbank: nc.tensor.matmul(
                        out=ps[pbank], lhsT=wpp_sb[:, k2, dc * 128:(dc + 1) * 128], rhs=peT[:, k2, :],
                        start=(k2 == 0), stop=(k2 == 1)) for k2 in range(2)], reads=[b_wpp, b_peT], writes=[pb[pbank]])
                    k.op(k.act, lambda jj=jj, bank=bank: nc.scalar.activation(out=sg[jj][:, :], in_=ps[bank], func=AF.Sigmoid),
                         reads=[pb[bank]], writes=[sgb[jj]])
                    k.op(k.dve, lambda jj=jj, pbank=pbank: nc.vector.tensor_tensor(
                        out=sg2[jj][:, :], in0=ps[pbank], in1=sg[jj][:, :], op=OP.mult),
                         reads=[pb[pbank], sgb[jj]], writes=[sg2b[jj]])
                    k.op(k.pool, lambda jj=jj, dc=dc: nc.gpsimd.tensor_tensor(
                        out=xT[:, dc, :], in0=xT[:, dc, :], in1=sg2[jj][:, :], op=OP.add),
                         reads=[sg2b[jj]], writes=[xb[dc]])
            if t + 1 < NT:
                tk1 = slice((t + 1) * T, (t + 2) * T)
                k.dma(k.sp, [(hT[:, :, :], YT[:, :, tk1].rearrange("c p t -> p c t"))], writes=[hb], owner=hb)
            xin, xinb = tb.xin, tb.xinb
            for s in range(4):
                i = s % 2
                r0 = t * T + s * 128
                for q in range(4):
                    bank = (s * 4 + q) % 8
                    fns = [lambda dc=dc, j=j, bank=bank, s=s: nc.tensor.transpose(
                        out=ps[bank][:, j * 128:(j + 1) * 128], in_=xT[:, dc, s * 128:(s + 1) * 128], identity=ident)
                        for j, dc in enumerate(range(q * 4, q * 4 + 4))]
                    k.op(k.pe, fns, reads=[xb[dc] for dc in range(q * 4, q * 4 + 4)] + [b_cst], writes=[pb[bank]])
                    if q % 2 == 0:
                        k.op(k.act, lambda q=q, i=i, bank=bank: nc.scalar.copy(out=xin[i][:, q * 512:(q + 1) * 512], in_=ps[bank]),
                             reads=[pb[bank]], writes=[xinb[i]])
                    else:
                        k.op(k.dve, lambda q=q, i=i, bank=bank: nc.vector.tensor_copy(out=xin[i][:, q * 512:(q + 1) * 512], in_=ps[bank]),
                             reads=[pb[bank]], writes=[xinb[i]])
                k.dma(k.pool, [(y[r0:r0 + 128, :], xin[i][:, :])], reads=[xinb[i]], owner=xinb[i])
        k.end_phase(keep=(b_cst,))


    k.barrier()
    es.close()
    return nc, k


def _consts():
    ident = np.eye(128, dtype=np.float32)
    s = np.arange(128)
    triF = (s[:, None] <= s[None, :]).astype(np.float32)
    triB = (s[:, None] >= s[None, :]).astype(np.float32)
    ones = np.ones((128, 128), np.float32)
    zeros = np.zeros((128, 128), np.float32)
    return np.ascontiguousarray(np.concatenate([ident, triF, triB, ones, zeros], axis=1))


def _na_tables(rpb, seq_rows):
    a = np.arange(2)[:, None, None, None, None]
    kc = np.arange(64)[None, :, None, None, None]
    o = np.arange(9)[None, None, :, None, None]
    b = np.arange(2)[None, None, None, :, None]
    qc = np.arange(64)[None, None, None, None, :]
    dr = 2 * (o - 4) + a - b + 0 * kc + 0 * qc
    dcol = np.clip(kc - qc, -15, 15) + 15 + 0 * dr
    cs = np.clip(qc - 8, 0, 48)
    colok = (kc >= cs) & (kc < cs + 16)
    inwin = np.abs(dr) <= 7
    dri = np.clip(dr + 7, 0, 14)
    g = rpb[:, dri, dcol]
    g = np.where(inwin[None], g, np.float32(0.0))
    rpbg = np.ascontiguousarray(g.reshape(8, 128, 9, 128).astype(np.float32))
    cmE = np.where(inwin & colok, 0.0, NEG).astype(np.float32).reshape(128, 9, 128)
    cmI = np.where(inwin & colok & (dr >= -4) & (dr <= 3), 0.0, NEG).astype(np.float32).reshape(128, 9, 128)
    R = sum(seq_rows)
    nch = R // 2
    seq_of_row = np.concatenate([np.full(r, i) for i, r in enumerate(seq_rows)])
    start = np.concatenate([[0], np.cumsum(seq_rows)[:-1]])
    ne = max(4, (nch // 16) * 4)
    rb = np.full((128, ne, 9, 2), NEG, np.float32)
    pos = {0: 0, 1: 1, 14: 2, 15: 3}
    for c in range(nch):
        if c % 16 not in pos:
            continue
        e = (c // 16) * 4 + pos[c % 16]
        for bb in range(2):
            qr = 2 * c + bb
            si = seq_of_row[qr]
            r0 = start[si]; Rs = seq_rows[si]
            rs = r0 + min(max(qr - r0 - 4, 0), Rs - 8)
            for oo in range(9):
                for aa in range(2):
                    kr = 2 * (c + oo - 4) + aa
                    if 0 <= kr < R and seq_of_row[kr] == si and rs <= kr <= rs + 7:
                        rb[aa * 64:(aa + 1) * 64, e, oo, bb] = 0.0
    keep = np.ones((128, 2, nch), np.float32)
    for s0 in start[1:]:
        cst = s0 // 2
        keep[:, 0, cst - 1] = 0.0
        keep[:, 1, cst] = 0.0
    return rpbg, cmE, cmI, rb, keep


def host_inputs(x_stream, p_stream, W, seq_rows):
    f = lambda a: np.ascontiguousarray(a, dtype=np.float32)
    gvec = lambda g: g.reshape(DC, 128).T
    rpbg, cmE, cmI, rb, keep = _na_tables(W["rpb"], seq_rows)
    gsm = np.zeros((128, 16), np.float32)
    gsm[:, 0] = W["g_qn"]; gsm[:, 1] = W["g_kn"]
    gsm[:, 2:10] = W["g_mh"].reshape(8, 128).T
    gb = np.concatenate([W["b_igate"].reshape(8), W["b_fgate"].reshape(8)])
    return {
        "xs": f(x_stream), "pes": f(p_stream),
        "w1g": f(W["w_ffn1_gate"]), "w1u": f(W["w_ffn1_up"]), "w1d": f(W["w_ffn1_down"]),
        "w2g": f(W["w_ffn2_gate"]), "w2u": f(W["w_ffn2_up"]), "w2d": f(W["w_ffn2_down"]),
        "win": f(W["w_in"]), "wout": f(W["w_out"]), "wpg": f(W["w_ple_gate"]), "wpp": f(W["w_ple_proj"]),
        "gv": f(np.concatenate([gvec(W["g_ffn1"]), gvec(W["g_mix"]), gvec(W["g_ffn2"]), gvec(W["g_ple"])], axis=1)),
        "gsm": f(gsm), "gbias": f(np.broadcast_to(gb[None, :], (128, 16))),
        "cst": _consts(), "rpbg": rpbg, "cmI": cmI, "cmE": cmE, "rb": rb, "keep": keep,
    }


_NC_CACHE = {}


def kernel(**inputs):
    NTOK = 8192
    f32 = lambda a: np.asarray(a, dtype=np.float32)
    W = {}
    for name in ("g_ffn1", "w_ffn1_gate", "w_ffn1_up", "w_ffn1_down", "g_mix", "w_in", "b_igate", "b_fgate",
                 "g_qn", "g_kn", "rpb", "g_mh", "w_out", "g_ffn2", "w_ffn2_gate", "w_ffn2_up", "w_ffn2_down",
                 "g_ple", "w_ple_gate", "w_ple_proj"):
        W[name] = f32(inputs[name])[0]
    xp = f32(inputs["x_prompt"]); xsm = f32(inputs["x_sample"])
    pp = f32(inputs["p_prompt"])[0]; psm = f32(inputs["p_sample"])[0]
    zx = np.zeros((NTOK, D), np.float32); zp = np.zeros((NTOK, 256), np.float32)
    streams = [(zx, zp, [128])] * 8
    streams[0] = (xsm[0], psm[0], [128])
    streams[2] = (xsm[1], psm[1], [128])
    streams[4] = (xp.reshape(NTOK, D), pp.reshape(NTOK, 256), [32, 32, 32, 32])
    in_maps = [host_inputs(x, p, W, rows) for (x, p, rows) in streams]
    if "nc" not in _NC_CACHE:
        _NC_CACHE["nc"] = build(NTOK)[0]
    res = run_bass_kernel_spmd(_NC_CACHE["nc"], in_maps, core_ids=list(range(8)))
    outs = [np.asarray(res.results[c]["y"], dtype=np.float32) for c in (0, 2, 4)]
    y_sample = np.stack([outs[0], outs[1]], axis=0)
    y_prompt = outs[2].reshape(4, 2048, D)
    return (y_prompt, y_sample)
```
